# Optimizing a Trainium2 kernel written in Bass

```python
import jax, jax.numpy as jnp
from jax import lax
import numpy as np

D_MODEL = 1024
BATCH = 4
SEQ = 4096
DEPTH = 4

MEM_LEN = 256
N_MIXERS = 2
N_GLA = (DEPTH + 1) // 2
N_SSD = DEPTH // 2
EPS = 1e-6
RESID_SCALE = (2 * DEPTH) ** -0.5

XA_HEADS = 4
XA_HEAD_DIM = 256
XA_WIDTH = XA_HEADS * XA_HEAD_DIM

GLA_HEADS = 4
GLA_DK = D_MODEL // 2
GLA_DV = D_MODEL
GLA_HEAD_K = GLA_DK // GLA_HEADS
GLA_HEAD_V = GLA_DV // GLA_HEADS
GLA_GATE_RANK = 16
GLA_GATE_TAU = 16.0
GLA_CHUNK = 64
GLA_IN = 2 * GLA_DK + 2 * GLA_DV + GLA_GATE_RANK + XA_WIDTH

SSD_EXPAND = 2
SSD_D_INNER = SSD_EXPAND * D_MODEL
SSD_HEAD_DIM = 64
SSD_HEADS = SSD_D_INNER // SSD_HEAD_DIM
SSD_GROUPS = 8
SSD_STATE = 128
SSD_CONV_K = 4
SSD_CONV_DIM = SSD_D_INNER + 2 * SSD_GROUPS * SSD_STATE
SSD_CHUNK = 128
SSD_IN = SSD_D_INNER + SSD_CONV_DIM + SSD_HEADS + XA_WIDTH

FFN_HIDDEN = ((8 * D_MODEL + 3 * 256 - 1) // (3 * 256)) * 256

kernel_name = "gla_ssd_memxattn_hybrid_trunk"


def rms_norm(x, w):
    xf = x.astype(jnp.float32)
    y = xf * lax.rsqrt(jnp.mean(xf * xf, axis=-1, keepdims=True) + EPS)
    return (y * w.astype(jnp.float32)).astype(x.dtype)


def chunk_scan(decay, inc):
    def step(s, xs):
        d, u = xs
        return d * s + u, s
    _, s_in = lax.scan(step, jnp.zeros_like(inc[0]), (decay, inc))
    return s_in


def gla_chunked(q, k, v, log_a):
    B_, S_, H, dk = q.shape
    dv = v.shape[-1]
    L = GLA_CHUNK
    N = S_ // L
    q = q.reshape(B_, N, L, H, dk)
    k = k.reshape(B_, N, L, H, dk)
    log_a = log_a.reshape(B_, N, L, H, dk)
    v = v.reshape(B_, N, L, H, dv)
    b = jnp.cumsum(log_a, axis=2)
    b_last = b[:, :, -1:]
    q_dec = q * jnp.exp(b)
    scores = jnp.einsum('bnlhd,bnmhd->bnhlm', q_dec, k * jnp.exp(-b))
    causal = jnp.tril(jnp.ones((L, L), dtype=bool))
    scores = jnp.where(causal, scores, 0.0)
    o_intra = jnp.einsum('bnhlm,bnmhv->bnlhv', scores, v)
    chunk_inc = jnp.einsum('bnlhd,bnlhv->nbhdv', k * jnp.exp(b_last - b), v)
    chunk_dec = jnp.exp(b_last[:, :, 0]).transpose(1, 0, 2, 3)[..., None]
    s_in = chunk_scan(chunk_dec, chunk_inc)
    o_inter = jnp.einsum('bnlhd,nbhdv->bnlhv', q_dec, s_in)
    return (o_intra + o_inter).reshape(B_, S_, H, dv)


def ssd_chunked(x, dt, a, b_mat, c_mat):
    B_, S_, H, P = x.shape
    G, Nst = b_mat.shape[2], b_mat.shape[3]
    R = H // G
    L = SSD_CHUNK
    C_ = S_ // L
    xdt = (x * dt[..., None]).reshape(B_, C_, L, G, R, P)
    a_dt = (dt * a).reshape(B_, C_, L, G, R).transpose(0, 1, 3, 4, 2)
    bm = b_mat.reshape(B_, C_, L, G, Nst)
    cm = c_mat.reshape(B_, C_, L, G, Nst)
    cs = jnp.cumsum(a_dt, axis=-1)
    causal = jnp.tril(jnp.ones((L, L), dtype=bool))
    seg = jnp.exp(jnp.where(causal, cs[..., :, None] - cs[..., None, :], -jnp.inf))
    cb = jnp.einsum('bclgn,bcmgn->bcglm', cm, bm)
    y_diag = jnp.einsum('bcglm,bcgrlm,bcmgrp->bclgrp', cb, seg, xdt)
    decay_to_end = jnp.exp(cs[..., -1:] - cs)
    chunk_inc = jnp.einsum('bclgn,bcgrl,bclgrp->cbgrpn', bm, decay_to_end, xdt)
    chunk_dec = jnp.exp(cs[..., -1]).transpose(1, 0, 2, 3)[..., None, None]
    s_in = chunk_scan(chunk_dec, chunk_inc)
    y_off = jnp.einsum('bclgn,cbgrpn,bcgrl->bclgrp', cm, s_in, jnp.exp(cs))
    return (y_diag + y_off).reshape(B_, S_, H, P)


def memory_attention(xq, mem_kv):
    B_, S_, _ = xq.shape
    q = xq.reshape(B_, S_, XA_HEADS, XA_HEAD_DIM)
    k, v = jnp.split(mem_kv, 2, axis=-1)
    k = k.reshape(B_, -1, XA_HEADS, XA_HEAD_DIM)
    v = v.reshape(B_, -1, XA_HEADS, XA_HEAD_DIM)
    s = jnp.einsum('bshd,bmhd->bhsm', q, k).astype(jnp.float32) * (XA_HEAD_DIM ** -0.5)
    p = jax.nn.softmax(s, axis=-1).astype(v.dtype)
    return jnp.einsum('bhsm,bmhd->bshd', p, v).reshape(B_, S_, XA_WIDTH)


def gla_mixer(hn, w_in, w_gate2, b_gate, head_norm):
    B_, S_, _ = hn.shape
    proj = hn @ w_in
    cuts = [GLA_DK, 2 * GLA_DK, 2 * GLA_DK + GLA_DV, 2 * GLA_DK + 2 * GLA_DV,
            2 * GLA_DK + 2 * GLA_DV + GLA_GATE_RANK]
    q, k, v, g, gate_lr, xq = jnp.split(proj, cuts, axis=-1)
    log_a = jax.nn.log_sigmoid((gate_lr @ w_gate2 + b_gate).astype(jnp.float32)) / GLA_GATE_TAU
    hk = lambda t: t.reshape(B_, S_, GLA_HEADS, GLA_HEAD_K).astype(jnp.float32)
    o = gla_chunked(hk(q) * (GLA_HEAD_K ** -0.5), hk(k),
                    v.reshape(B_, S_, GLA_HEADS, GLA_HEAD_V).astype(jnp.float32), hk(log_a))
    o = rms_norm(o, head_norm).reshape(B_, S_, GLA_DV).astype(hn.dtype)
    return o * jax.nn.silu(g), xq


def ssd_mixer(hn, w_in, conv_w, conv_b, dt_bias, a_log, d_skip, norm_w):
    B_, S_, _ = hn.shape
    proj = hn @ w_in
    cuts = [SSD_D_INNER, SSD_D_INNER + SSD_CONV_DIM, SSD_D_INNER + SSD_CONV_DIM + SSD_HEADS]
    z, xbc, dt_raw, xq = jnp.split(proj, cuts, axis=-1)
    xbc = lax.conv_general_dilated(
        xbc, conv_w[:, None, :], window_strides=(1,), padding=[(SSD_CONV_K - 1, 0)],
        dimension_numbers=('NWC', 'WIO', 'NWC'), feature_group_count=SSD_CONV_DIM) + conv_b
    xbc = jax.nn.silu(xbc)
    xs, bm, cm = jnp.split(xbc, [SSD_D_INNER, SSD_D_INNER + SSD_GROUPS * SSD_STATE], axis=-1)
    dt = jax.nn.softplus(dt_raw.astype(jnp.float32) + dt_bias.astype(jnp.float32))
    a = -jnp.exp(a_log.astype(jnp.float32))
    xh = xs.reshape(B_, S_, SSD_HEADS, SSD_HEAD_DIM).astype(jnp.float32)
    y = ssd_chunked(xh, dt, a,
                    bm.reshape(B_, S_, SSD_GROUPS, SSD_STATE).astype(jnp.float32),
                    cm.reshape(B_, S_, SSD_GROUPS, SSD_STATE).astype(jnp.float32))
    y = y + d_skip.astype(jnp.float32)[:, None] * xh
    y = y.reshape(B_, S_, SSD_D_INNER).astype(hn.dtype)
    return rms_norm(y * jax.nn.silu(z), norm_w), xq


def setup_inputs(seed: int = 0) -> dict:
    key = jax.random.key(seed)
    ks = jax.random.split(key, 24)
    f32 = jnp.float32
    nrm = lambda k, shape, scale: jax.random.normal(k, shape, f32) * scale
    gain = lambda k, shape: 1.0 + 0.02 * jax.random.normal(k, shape, f32)
    dt0 = jnp.exp(jax.random.uniform(ks[13], (N_SSD, SSD_HEADS), f32)
                  * (np.log(0.1) - np.log(0.001)) + np.log(0.001)).astype(f32)
    return {
        "x": nrm(ks[0], (BATCH, SEQ, D_MODEL), 1.0),
        "mem": nrm(ks[1], (BATCH, MEM_LEN, D_MODEL), 1.0),
        "mix_norm": gain(ks[2], (DEPTH, D_MODEL)),
        "mem_norm": gain(ks[3], (DEPTH, D_MODEL)),
        "w_mem_kv": nrm(ks[4], (DEPTH, D_MODEL, 2 * XA_WIDTH), D_MODEL ** -0.5),
        "gla_w_in": nrm(ks[5], (N_GLA, D_MODEL, GLA_IN), D_MODEL ** -0.5),
        "gla_w_gate2": nrm(ks[6], (N_GLA, GLA_GATE_RANK, GLA_DK), GLA_GATE_RANK ** -0.5),
        "gla_b_gate": nrm(ks[7], (N_GLA, GLA_DK), 0.1),
        "gla_head_norm": gain(ks[8], (N_GLA, GLA_HEAD_V)),
        "gla_w_out": nrm(ks[9], (N_GLA, GLA_DV + XA_WIDTH, D_MODEL), (GLA_DV + XA_WIDTH) ** -0.5 * RESID_SCALE),
        "ssd_w_in": nrm(ks[10], (N_SSD, D_MODEL, SSD_IN), D_MODEL ** -0.5),
        "ssd_conv_w": nrm(ks[11], (N_SSD, SSD_CONV_K, SSD_CONV_DIM), SSD_CONV_K ** -0.5),
        "ssd_conv_b": nrm(ks[12], (N_SSD, SSD_CONV_DIM), 0.02),
        "ssd_dt_bias": dt0 + jnp.log(-jnp.expm1(-dt0)),
        "ssd_a_log": jnp.log(jax.random.uniform(ks[14], (N_SSD, SSD_HEADS), f32, 1.0, 16.0)),
        "ssd_d": gain(ks[15], (N_SSD, SSD_HEADS)),
        "ssd_norm": gain(ks[16], (N_SSD, SSD_D_INNER)),
        "ssd_w_out": nrm(ks[17], (N_SSD, SSD_D_INNER + XA_WIDTH, D_MODEL), (SSD_D_INNER + XA_WIDTH) ** -0.5 * RESID_SCALE),
        "ffn_norm": gain(ks[18], (DEPTH, D_MODEL)),
        "ffn_w_in": nrm(ks[19], (DEPTH, D_MODEL, 2 * FFN_HIDDEN), D_MODEL ** -0.5),
        "ffn_w_out": nrm(ks[20], (DEPTH, FFN_HIDDEN, D_MODEL), FFN_HIDDEN ** -0.5 * RESID_SCALE),
        "final_norm": gain(ks[21], (D_MODEL,)),
    }


def reference(x, mem, mix_norm, mem_norm, w_mem_kv, gla_w_in, gla_w_gate2, gla_b_gate,
              gla_head_norm, gla_w_out, ssd_w_in, ssd_conv_w, ssd_conv_b, ssd_dt_bias,
              ssd_a_log, ssd_d, ssd_norm, ssd_w_out, ffn_norm, ffn_w_in, ffn_w_out, final_norm):
    h = x
    for i in range(DEPTH):
        j = i // N_MIXERS
        hn = rms_norm(h, mix_norm[i])
        if i % N_MIXERS == 0:
            mix, xq = gla_mixer(hn, gla_w_in[j], gla_w_gate2[j], gla_b_gate[j], gla_head_norm[j])
            w_out = gla_w_out[j]
        else:
            mix, xq = ssd_mixer(hn, ssd_w_in[j], ssd_conv_w[j], ssd_conv_b[j], ssd_dt_bias[j],
                                ssd_a_log[j], ssd_d[j], ssd_norm[j])
            w_out = ssd_w_out[j]
        mem_kv = rms_norm(mem, mem_norm[i]) @ w_mem_kv[i]
        xa = memory_attention(xq, mem_kv)
        h = h + jnp.concatenate([mix, xa], axis=-1) @ w_out
        hn = rms_norm(h, ffn_norm[i])
        gate, up = jnp.split(hn @ ffn_w_in[i], 2, axis=-1)
        h = h + (jax.nn.silu(gate) * up) @ ffn_w_out[i]
    return rms_norm(h, final_norm)
```

```python
import contextlib
import numpy as np
import concourse.bass as bass
import concourse.mybir as mybir
from concourse.bass_utils import run_bass_kernel_spmd

F32 = mybir.dt.float32
BF16 = mybir.dt.bfloat16
U8 = mybir.dt.uint8
AF = mybir.ActivationFunctionType
ALU = mybir.AluOpType

T = 2048
NT = 16
D = 1024
KC = 8
FH = 2816
EPS = 1e-6
CELL = 256
ARENA_BYTES = 206 * 1024
PAIRS = [[0, 1], [2, 3], [4, 5], [6, 7]]
SAME_ENG_SYNC = True


class V:
    __slots__ = ("ap", "cells")

    def __init__(self, ap, cells):
        self.ap = ap
        self.cells = tuple(cells)


def scells(off, nbytes):
    return [("S", c) for c in range(off // CELL, (off + nbytes - 1) // CELL + 1)]


class Op:
    __slots__ = ("eng", "fn", "waits", "signal", "sigval", "seq", "dma", "dsem", "dval", "inc")

    def __init__(self, eng, fn):
        self.eng = eng
        self.fn = fn
        self.waits = []
        self.signal = False
        self.sigval = None
        self.seq = None
        self.dma = False
        self.dsem = None
        self.dval = None
        self.inc = 1


class Prog:
    ENGS = ("PE", "ACT", "DVE", "POOL", "SP")
    NDSEM = 8

    def __init__(self):
        self.ops = {e: [] for e in self.ENGS}
        self.cellw = {}
        self.cellr = {}
        self.waited_eng = {e: {} for e in self.ENGS}
        self.waited_dma = {e: {} for e in self.ENGS}
        self.dma_count = {"SP": 0, "POOL": 0, "ACT": 0}
        self.dma_hist = {"SP": [], "POOL": [], "ACT": []}

    def _need(self, op, dep):
        if dep is op:
            return
        e = op.eng
        if dep.dma:
            key = (dep.eng, dep.dsem)
            if self.waited_dma[e].get(key, 0) >= dep.dval:
                return
            self.waited_dma[e][key] = dep.dval
            op.waits.append(dep)
        else:
            if dep.eng == e and not op.dma:
                if e == "PE" or not SAME_ENG_SYNC:
                    return
            if self.waited_eng[e].get(dep.eng, -1) >= dep.seq:
                return
            self.waited_eng[e][dep.eng] = dep.seq
            dep.signal = True
            op.waits.append(dep)

    def _track(self, op, reads, writes):
        deps = []
        for v in reads:
            for c in v.cells:
                w = self.cellw.get(c)
                if w is not None:
                    deps.append(w)
        for v in writes:
            for c in v.cells:
                w = self.cellw.get(c)
                if w is not None:
                    deps.append(w)
                deps.extend(self.cellr.get(c, ()))
        seen = set()
        for d in deps:
            if id(d) in seen:
                continue
            seen.add(id(d))
            self._need(op, d)
        for v in reads:
            for c in v.cells:
                self.cellr.setdefault(c, []).append(op)
        for v in writes:
            for c in v.cells:
                self.cellw[c] = op
                self.cellr[c] = []

    def op(self, eng, fn, reads=(), writes=()):
        o = Op(eng, fn)
        o.seq = len(self.ops[eng])
        self._track(o, reads, writes)
        self.ops[eng].append(o)
        return o

    def dma(self, queue, out_ap, in_ap, reads=(), writes=(), **kw):
        o = Op(queue, lambda e: e.dma_start(out=out_ap, in_=in_ap, **kw))
        o.dma = True
        o.inc = 16
        i = self.dma_count[queue]
        self.dma_count[queue] = i + 1
        o.dsem = i % self.NDSEM
        o.dval = 16 * (i // self.NDSEM + 1)
        o.seq = len(self.ops[queue])
        hist = self.dma_hist[queue]
        if i >= self.NDSEM:
            self._need(o, hist[i - self.NDSEM])
        hist.append(o)
        self._track(o, reads, writes)
        self.ops[queue].append(o)
        return o

    def coll(self, fn, reads=(), writes=()):
        o = Op("POOL", fn)
        o.dma = True
        o.inc = 1
        self.ncoll = getattr(self, "ncoll", 0) + 1
        o.dsem = "CC"
        o.dval = self.ncoll
        o.seq = len(self.ops["POOL"])
        self._track(o, reads, writes)
        self.ops["POOL"].append(o)
        return o

    def finalize(self):
        for e in self.ENGS:
            n = 0
            for o in self.ops[e]:
                if o.dma:
                    continue
                if o.signal:
                    n += 1
                    o.sigval = n

    def emit(self, eng_name, e, esem, dsem, ccsem):
        for o in self.ops[eng_name]:
            for d in o.waits:
                if d.dma:
                    if d.dsem == "CC":
                        e.wait_ge(ccsem, d.dval)
                    else:
                        e.wait_ge(dsem[d.eng][d.dsem], d.dval)
                else:
                    e.wait_ge(esem[d.eng], d.sigval)
            inst = o.fn(e)
            if o.dma:
                if o.dsem == "CC":
                    inst.then_inc(ccsem)
                else:
                    inst.then_inc(dsem[eng_name][o.dsem], 16)
            elif o.signal:
                inst.then_inc(esem[eng_name], 1)


class Arena:
    def __init__(self, ap):
        self.ap = ap
        self.top = 0

    def alloc(self, nbytes, align=256):
        off = (self.top + align - 1) // align * align
        assert off + nbytes <= ARENA_BYTES, ("arena overflow", off, nbytes)
        self.top = off + nbytes
        self.high = max(getattr(self, "high", 0), self.top)
        return off

    def mark(self):
        return self.top

    def release(self, m):
        self.top = m


class Tile:
    def __init__(self, arena, dims, dtype, name="", off=None):
        self.dims = list(dims)
        self.dtype = dtype
        self.esz = 4 if dtype == F32 else 2
        n = int(np.prod(dims))
        self.nbytes = n * self.esz
        self.off = arena.alloc(self.nbytes) if off is None else off
        ap = arena.ap[:, self.off:self.off + self.nbytes].bitcast(dtype)
        if len(dims) == 2:
            ap = ap.rearrange("p (a b) -> p a b", a=dims[0])
        elif len(dims) == 3:
            ap = ap.rearrange("p (a b c) -> p a b c", a=dims[0], b=dims[1])
        self.ap = ap

    def full(self):
        return V(self.ap, scells(self.off, self.nbytes))

    def v(self, ap, start_elem, nelem):
        return V(ap, scells(self.off + start_elem * self.esz, nelem * self.esz))

    def vs(self, ap, segs):
        cells = []
        for (s, n) in segs:
            cells.extend(scells(self.off + s * self.esz, n * self.esz))
        return V(ap, cells)


class K:
    pass


def build(cfg):
    nc = bass.Bass("TRN2", target_bir_lowering=False)
    P = Prog()
    global LASTP
    LASTP = P
    k = K()
    k.nc, k.P = nc, P
    dr = {}

    def dram_in(name, shape, dtype=F32):
        dr[name] = nc.dram_tensor(name, list(shape), dtype, kind="ExternalInput")
        return dr[name]

    x_d = dram_in("x", [T, D])
    mem_d = dram_in("mem", [256, D])
    par_d = dram_in("par", [128, NPAR])
    bcp_d = dram_in("bcp", [128, NBCP])
    def wget(nm):
        if nm not in dr:
            dram_in(nm, WEIGHT_SHAPES[nm])
        return dr[nm]
    k.w = wget
    out_d = nc.dram_tensor("out", [T, D], F32, kind="ExternalOutput")
    k.dr = dr

    stack = contextlib.ExitStack()
    with stack:
        arena_t = stack.enter_context(nc.sbuf_tensor("arena", [128, ARENA_BYTES], U8))
        psum = [stack.enter_context(nc.psum_tensor(f"ps{i}", [128, 512], F32)) for i in range(8)]
        esem = {e: stack.enter_context(nc.semaphore("s" + e)) for e in Prog.ENGS}
        dsem = {q: [stack.enter_context(nc.semaphore(f"d{q}{i}")) for i in range(Prog.NDSEM)]
                for q in ("SP", "POOL", "ACT")}
        ccsem = stack.enter_context(nc.semaphore("cc"))
        block = stack.enter_context(nc.Block())

        A = Arena(arena_t)
        k.A = A
        k.psum = psum
        k.ps_i = 0

        emit_program(k, cfg, x_d, mem_d, par_d, bcp_d, out_d)
        P.finalize()
        nc.used_inputs = list(dr.keys())

        @block.tensor
        def _(e):
            P.emit("PE", e, esem, dsem, ccsem)

        @block.scalar
        def _(e):
            P.emit("ACT", e, esem, dsem, ccsem)

        @block.vector
        def _(e):
            P.emit("DVE", e, esem, dsem, ccsem)

        @block.gpsimd
        def _(e):
            P.emit("POOL", e, esem, dsem, ccsem)

        @block.sync
        def _(e):
            P.emit("SP", e, esem, dsem, ccsem)
    return nc


PAR = {}
_c = 0
for _i in range(4):
    for _nm in ("mixn", "memn", "ffnn"):
        PAR[(_nm, _i)] = _c
        _c += 8
for _j in range(2):
    PAR[("bgate", _j)] = _c
    _c += 4
PAR["flag"] = _c
_c += 1
NPAR = _c
BCP = {"fnorm": 0}
_c = 1024
for _j in range(2):
    BCP[("hnorm", _j)] = _c
    _c += 256
NBCP = _c

WEIGHT_SHAPES = {}
for _i in range(4):
    WEIGHT_SHAPES[f"ffn_in{_i}"] = [44, 128, 1024]
    WEIGHT_SHAPES[f"ffn_out{_i}"] = [22, 128, 1024]
    WEIGHT_SHAPES[f"kvk{_i}"] = [8, 128, 1024]
    WEIGHT_SHAPES[f"kvv{_i}"] = [2, 128, 4096]
    WEIGHT_SHAPES[f"xq{_i}"] = [8, 128, 1024]
    WEIGHT_SHAPES[f"wo{_i}"] = [16 if _i % 2 == 0 else 24, 128, 1024]
for _j in range(2):
    WEIGHT_SHAPES[f"gq{_j}"] = [4, 128, 1024]
    WEIGHT_SHAPES[f"gk{_j}"] = [4, 128, 1024]
    WEIGHT_SHAPES[f"gv{_j}"] = [4, 128, 2048]
    WEIGHT_SHAPES[f"gg{_j}"] = [4, 128, 2048]
    WEIGHT_SHAPES[f"glr{_j}"] = [128, 128]
    WEIGHT_SHAPES[f"gw2{_j}"] = [16, 512]
    WEIGHT_SHAPES[f"sz{_j}"] = [8, 128, 2048]
    WEIGHT_SHAPES[f"sxbc{_j}"] = [32, 128, 1024]
    WEIGHT_SHAPES[f"sdt{_j}"] = [128, 1024]
    WEIGHT_SHAPES[f"scv{_j}"] = [128, 164]
    WEIGHT_SHAPES[f"sdk{_j}"] = [128, 2048]
    WEIGHT_SHAPES[f"snf{_j}"] = [128, 16]


def blk_layout(w, ncol=128):
    K_, C = w.shape
    nb = C // ncol
    a = w.reshape(K_ // 128, 128, nb, ncol).transpose(2, 1, 0, 3)
    return np.ascontiguousarray(a.reshape(nb, 128, (K_ // 128) * ncol))


def fm_vec(v):
    return np.ascontiguousarray(v.reshape(-1, 128).T)


def prep_inputs(inp, used=None):
    f = lambda a: np.asarray(a, dtype=np.float32)
    need = lambda nm: used is None or nm in used
    shared = {}
    par = np.zeros((128, NPAR), np.float32)
    for i in range(4):
        par[:, PAR[("mixn", i)]:PAR[("mixn", i)] + 8] = fm_vec(f(inp["mix_norm"])[i])
        par[:, PAR[("memn", i)]:PAR[("memn", i)] + 8] = fm_vec(f(inp["mem_norm"])[i])
        par[:, PAR[("ffnn", i)]:PAR[("ffnn", i)] + 8] = fm_vec(f(inp["ffn_norm"])[i])
    for j in range(2):
        par[:, PAR[("bgate", j)]:PAR[("bgate", j)] + 4] = fm_vec(f(inp["gla_b_gate"])[j])
    bcp = np.zeros((128, NBCP), np.float32)
    bcp[:, 0:1024] = f(inp["final_norm"])[None, :]
    for j in range(2):
        bcp[:, BCP[("hnorm", j)]:BCP[("hnorm", j)] + 256] = f(inp["gla_head_norm"])[j][None, :]
    shared["bcp"] = bcp
    for i in range(4):
        if need(f"ffn_in{i}"):
            shared[f"ffn_in{i}"] = blk_layout(f(inp["ffn_w_in"])[i])
            shared[f"ffn_out{i}"] = np.ascontiguousarray(f(inp["ffn_w_out"])[i].reshape(22, 128, 1024))
        if need(f"kvk{i}"):
            wkv = f(inp["w_mem_kv"])[i]
            shared[f"kvk{i}"] = blk_layout(wkv[:, 0:1024])
            shared[f"kvv{i}"] = blk_layout(wkv[:, 1024:2048], 512)
            j = i // 2
            if i % 2 == 0:
                w = f(inp["gla_w_in"])[j]
                shared[f"xq{i}"] = blk_layout(w[:, 3088:4112])
                shared[f"wo{i}"] = np.ascontiguousarray(f(inp["gla_w_out"])[j].reshape(16, 128, 1024))
                shared[f"gq{j}"] = blk_layout(w[:, 0:512])
                shared[f"gk{j}"] = blk_layout(w[:, 512:1024])
                shared[f"gv{j}"] = blk_layout(w[:, 1024:2048], 256)
                shared[f"gg{j}"] = blk_layout(w[:, 2048:3072], 256)
                shared[f"glr{j}"] = blk_layout(w[:, 3072:3088], 16)[0]
                shared[f"gw2{j}"] = np.ascontiguousarray(f(inp["gla_w_gate2"])[j])
            else:
                w = f(inp["ssd_w_in"])[j]
                shared[f"xq{i}"] = blk_layout(w[:, 6176:7200])
                shared[f"wo{i}"] = np.ascontiguousarray(f(inp["ssd_w_out"])[j].reshape(24, 128, 1024))
                shared[f"sz{j}"] = blk_layout(w[:, 0:2048], 256)
                shared[f"sxbc{j}"] = blk_layout(w[:, 2048:6144])
                shared[f"sdt{j}"] = blk_layout(np.tile(w[:, 6144:6176], (1, 4)))[0]
                scv = np.zeros((128, 164), np.float32)
                cw = f(inp["ssd_conv_w"])[j]
                scv[:, 0:128] = cw.reshape(4, 32, 128).transpose(2, 1, 0).reshape(128, 128)
                scv[:, 128:160] = fm_vec(f(inp["ssd_conv_b"])[j])
                scv[:, 160] = np.tile(f(inp["ssd_dt_bias"])[j], 4)
                scv[:, 161] = np.tile(f(inp["ssd_a_log"])[j], 4)
                shared[f"scv{j}"] = scv
                shared[f"sdk{j}"] = np.ascontiguousarray(np.broadcast_to(np.repeat(f(inp["ssd_d"])[j], 64)[None, :], (128, 2048)))
                shared[f"snf{j}"] = fm_vec(f(inp["ssd_norm"])[j])
    x = f(inp["x"])
    mem = f(inp["mem"])
    maps = []
    for c in range(8):
        b, half = c // 2, c % 2
        m = dict(shared)
        p = par.copy()
        p[:, PAR["flag"]] = float(half)
        m["par"] = p
        m["x"] = np.ascontiguousarray(x[b, half * T:(half + 1) * T])
        m["mem"] = np.ascontiguousarray(mem[b])
        maps.append(m)
    return maps


def ps_bank(k):
    i = k.ps_i
    k.ps_i = (i + 1) % 8
    return i


def PSV(k, i, ap=None):
    return V(k.psum[i][:, :] if ap is None else ap, [("P", i)])


def emit_program(k, cfg, x_d, mem_d, par_d, bcp_d, out_d):
    P, A, nc = k.P, k.A, k.nc
    k.H = Tile(A, [NT, D], F32)
    k.PAR = Tile(A, [NPAR], F32)
    k.IDB = Tile(A, [128], BF16)
    k.IDF = Tile(A, [128], F32)
    k.RS = Tile(A, [NT], F32)
    k.SS = Tile(A, [NT], F32)
    k.XN = Tile(A, [KC, T], BF16)
    k.JUNK = Tile(A, [D], BF16)
    k.HS = Tile(A, [D], BF16)
    H = k.H

    for j in range(NT):
        hv = H.v(H.ap[:, j, :], j * D, D)
        P.dma("SP", hv.ap, x_d[j * 128:(j + 1) * 128, :], writes=[hv])
    P.dma("SP", k.PAR.ap, par_d[:, :], writes=[k.PAR.full()])

    for idt in (k.IDB, k.IDF):
        fv = idt.full()
        P.op("POOL", lambda e, a=idt.ap: e.memset(a, 0.0), writes=[fv])
        P.op("POOL", lambda e, a=idt.ap: e.affine_select(
            out=a, in_=a, pattern=[[-1, 128]], compare_op=ALU.not_equal, fill=1.0,
            base=0, channel_multiplier=1), reads=[fv], writes=[fv])

    k.ONE1 = Tile(A, [1], F32)
    k.LNQ = Tile(A, [1], F32)
    P.op("POOL", lambda e: e.memset(k.ONE1.ap, 1.0), writes=[k.ONE1.full()])
    P.op("POOL", lambda e: e.memset(k.LNQ.ap, float(np.log(128.0 ** -0.5))), writes=[k.LNQ.full()])
    emit_consts(k)
    m0 = A.mark()
    for sl in cfg["sublayers"]:
        kind, i = sl
        A.release(m0)
        if kind == "ffn":
            emit_ffn(k, i)
        elif kind == "mix":
            emit_mixer(k, i, mem_d)
    A.release(m0)
    emit_output(k, cfg, bcp_d, out_d)


def hview(k, j, c0=0, n=D):
    H = k.H
    return H.v(H.ap[:, j, c0:c0 + n], j * D + c0, n)


def emit_norm_T(k, srcs, dst, dst_T, wcol):
    P = k.P
    SS, RS = k.SS, k.RS
    n = len(srcs)
    ssf, rsf = SS.full(), RS.full()
    junk = k.JUNK.full()
    for j, hv in enumerate(srcs):
        P.op("ACT", lambda e, hv=hv, j=j: e.activation(
            out=k.JUNK.ap, in_=hv.ap, func=AF.Square, accum_out=SS.ap[:, j:j + 1]),
            reads=[hv], writes=[junk, ssf])
    P.op("DVE", lambda e: e.tensor_scalar(out=RS.ap[:, 0:n], in0=SS.ap[:, 0:n], scalar1=1.0 / D, scalar2=EPS,
                                           op0=ALU.mult, op1=ALU.add), reads=[ssf], writes=[rsf])
    P.op("DVE", lambda e: e.reciprocal(out=RS.ap[:, 0:n], in_=RS.ap[:, 0:n]), reads=[rsf], writes=[rsf])
    P.op("ACT", lambda e: e.activation(out=RS.ap[:, 0:n], in_=RS.ap[:, 0:n], func=AF.Sqrt), reads=[rsf], writes=[rsf])
    hsf = k.HS.full()
    wv = k.PAR.v(k.PAR.ap[:, wcol:wcol + 8], wcol, 8)
    for j, hv in enumerate(srcs):
        P.op("ACT", lambda e, hv=hv, j=j: e.activation(
            out=k.HS.ap, in_=hv.ap, func=AF.Copy, scale=RS.ap[:, j:j + 1]),
            reads=[hv, rsf], writes=[hsf])
        b = ps_bank(k)
        pb = k.psum[b][:, :].bitcast(BF16)
        pv = PSV(k, b)
        for kc in range(KC):
            P.op("PE", lambda e, kc=kc, pb=pb: e.transpose(
                out=pb[:, kc * 128:(kc + 1) * 128], in_=k.HS.ap[:, kc * 128:(kc + 1) * 128],
                identity=k.IDB.ap), reads=[hsf, k.IDB.full()], writes=[pv])
        xv = dst.vs(dst.ap[:, :, j * 128:(j + 1) * 128], [(kc * dst_T + j * 128, 128) for kc in range(KC)])
        P.op("DVE", lambda e, pb=pb, xv=xv, wv=wv: e.tensor_tensor(
            out=xv.ap, in0=pb.rearrange("p (a b) -> p a b", a=KC),
            in1=wv.ap.unsqueeze(2).to_broadcast([128, KC, 128]), op=ALU.mult),
            reads=[pv, wv], writes=[xv])


def emit_rmsnorm_xn(k, wcol):
    emit_norm_T(k, [hview(k, j) for j in range(NT)], k.XN, T, wcol)


def xn_blk(k, kc, t0, n):
    XN = k.XN
    return XN.v(XN.ap[:, kc, t0:t0 + n], kc * T + t0, n)


def emit_ffn(k, i):
    P, A, dr = k.P, k.A, k.dr
    emit_rmsnorm_xn(k, PAR[("ffnn", i)])
    NG = 11
    ACTT = Tile(A, [NG, T], BF16)
    WO = Tile(A, [NG, D], BF16)
    WR = [Tile(A, [KC, 128], BF16) for _ in range(4)]
    SG = [Tile(A, [512], F32) for _ in range(2)]
    win, wout = k.w(f"ffn_in{i}"), k.w(f"ffn_out{i}")
    wr_i = 0
    sg_i = 0
    for grp in range(2):
        for c in range(NG):
            wov = WO.v(WO.ap[:, c, :], c * D, D)
            P.dma("POOL", wov.ap, wout[grp * NG + c, :, :], writes=[wov])
        for c in range(NG):
            blk = grp * NG + c
            wg, wu = WR[wr_i % 4], WR[(wr_i + 1) % 4]
            wr_i += 2
            P.dma("POOL", wg.ap.rearrange("p a b -> p (a b)"), win[blk, :, :], writes=[wg.full()])
            P.dma("POOL", wu.ap.rearrange("p a b -> p (a b)"), win[22 + blk, :, :], writes=[wu.full()])
            for tb in range(4):
                bg, bu = ps_bank(k), ps_bank(k)
                for (w_, b_) in ((wg, bg), (wu, bu)):
                    for kc in range(KC):
                        xv = xn_blk(k, kc, tb * 512, 512)
                        P.op("PE", lambda e, w_=w_, b_=b_, kc=kc, xv=xv: e.matmul(
                            k.psum[b_][:, :], lhsT=w_.ap[:, kc, :], rhs=xv.ap,
                            start=(kc == 0), stop=(kc == KC - 1)),
                            reads=[w_.full(), xv], writes=[PSV(k, b_)])
                sg = SG[sg_i % 2]
                sg_i += 1
                P.op("ACT", lambda e, sg=sg, bg=bg: e.activation(
                    out=sg.ap, in_=k.psum[bg][:, :], func=AF.Silu),
                    reads=[PSV(k, bg)], writes=[sg.full()])
                av = ACTT.v(ACTT.ap[:, c, tb * 512:(tb + 1) * 512], c * T + tb * 512, 512)
                P.op("DVE", lambda e, sg=sg, bu=bu, av=av: e.tensor_tensor(
                    out=av.ap, in0=k.psum[bu][:, :], in1=sg.ap, op=ALU.mult),
                    reads=[PSV(k, bu), sg.full()], writes=[av])
        for j in range(NT):
            for nb in range(2):
                b = ps_bank(k)
                for c in range(NG):
                    av = ACTT.v(ACTT.ap[:, c, j * 128:(j + 1) * 128], c * T + j * 128, 128)
                    wov = WO.v(WO.ap[:, c, nb * 512:(nb + 1) * 512], c * D + nb * 512, 512)
                    P.op("PE", lambda e, av=av, wov=wov, b=b, c=c: e.matmul(
                        k.psum[b][:, :], lhsT=av.ap, rhs=wov.ap, start=(c == 0), stop=(c == NG - 1)),
                        reads=[av, wov], writes=[PSV(k, b)])
                hv = hview(k, j, nb * 512, 512)
                P.op("DVE", lambda e, hv=hv, b=b: e.tensor_tensor(
                    out=hv.ap, in0=k.psum[b][:, :], in1=hv.ap, op=ALU.add),
                    reads=[PSV(k, b), hv], writes=[hv])


def load_w(k, tile, src, queue="POOL"):
    ap = tile.ap
    if len(tile.dims) == 2:
        ap = ap.rearrange("p a b -> p (a b)")
    elif len(tile.dims) == 3:
        ap = ap.rearrange("p a b c -> p (a b c)")
    return k.P.dma(queue, ap, src, writes=[tile.full()], max_dma_last_dim=4096)


def inproj_fm(k, wt, t0, n, M=128, wcol0=0):
    b = ps_bank(k)
    for kc in range(KC):
        xv = xn_blk(k, kc, t0, n)
        k.P.op("PE", lambda e, kc=kc, xv=xv, b=b: e.matmul(
            k.psum[b][0:M, 0:n], lhsT=wt.ap[:, kc, wcol0:wcol0 + M], rhs=xv.ap,
            start=(kc == 0), stop=(kc == KC - 1)), reads=[wt.full(), xv], writes=[PSV(k, b)])
    return b


def inproj_tm(k, wt, j, ncols, src=None, srcT=T):
    b = ps_bank(k)
    src = src or k.XN
    for kc in range(KC):
        xv = src.v(src.ap[:, kc, j * 128:(j + 1) * 128], kc * srcT + j * 128, 128)
        k.P.op("PE", lambda e, kc=kc, xv=xv, b=b: e.matmul(
            k.psum[b][:, 0:ncols], lhsT=xv.ap, rhs=wt.ap[:, kc, 0:ncols],
            start=(kc == 0), stop=(kc == KC - 1)), reads=[wt.full(), xv], writes=[PSV(k, b)])
    return b


def emit_consts(k):
    P, A = k.P, k.A
    k.MASKU = Tile(A, [128], F32)
    k.ONESB = Tile(A, [128], BF16)
    fv = k.MASKU.full()
    P.op("POOL", lambda e: e.memset(k.MASKU.ap, 1.0), writes=[fv])
    P.op("POOL", lambda e: e.affine_select(
        out=k.MASKU.ap, in_=k.MASKU.ap, pattern=[[1, 128]], compare_op=ALU.is_ge, fill=0.0,
        base=0, channel_multiplier=-1), reads=[fv], writes=[fv])
    P.op("POOL", lambda e: e.memset(k.ONESB.ap, 1.0), writes=[k.ONESB.full()])
    k.FLAG = k.PAR.v(k.PAR.ap[:, PAR["flag"]:PAR["flag"] + 1], PAR["flag"], 1)


def emit_mixer(k, i, mem_d):
    P, A = k.P, k.A
    emit_rmsnorm_xn(k, PAR[("mixn", i)])
    m = A.mark()
    emit_xattn(k, i, mem_d)
    A.release(m)
    if i % 2 == 0:
        emit_gla(k, i)
    else:
        emit_ssd(k, i)
    A.release(m)


def emit_xattn(k, i, mem_d):
    P, A = k.P, k.A
    mixdim = 1024 if i % 2 == 0 else 2048
    MEMT = Tile(A, [2, D], F32)
    MN = Tile(A, [KC, 256], BF16)
    KmT = Tile(A, [8, 256], BF16)
    Vm = Tile(A, [2, 1024], BF16)
    WR = [Tile(A, [KC, 128], BF16) for _ in range(3)]
    m1 = A.mark()
    WV = Tile(A, [KC, 512], BF16)
    srcs = []
    for mt in range(2):
        mv = MEMT.v(MEMT.ap[:, mt, :], mt * D, D)
        P.dma("SP", mv.ap, mem_d[mt * 128:(mt + 1) * 128, :], writes=[mv])
        srcs.append(mv)
    emit_norm_T(k, srcs, MN, 256, PAR[("memn", i)])
    kvk, kvv = k.w(f"kvk{i}"), k.w(f"kvv{i}")
    wi = 0
    for blk in range(8):
        w = WR[wi % 3]
        wi += 1
        load_w(k, w, kvk[blk, :, :])
        b = ps_bank(k)
        for kc in range(KC):
            mv = MN.v(MN.ap[:, kc, :], kc * 256, 256)
            P.op("PE", lambda e, kc=kc, mv=mv, b=b, w=w: e.matmul(
                k.psum[b][:, 0:256], lhsT=w.ap[:, kc, :], rhs=mv.ap, start=(kc == 0), stop=(kc == KC - 1)),
                reads=[w.full(), mv], writes=[PSV(k, b)])
        kv = KmT.v(KmT.ap[:, blk, :], blk * 256, 256)
        P.op("ACT", lambda e, kv=kv, b=b: e.activation(out=kv.ap, in_=k.psum[b][:, 0:256], func=AF.Copy,
                                                         scale=1.0 / 16.0), reads=[PSV(k, b)], writes=[kv])
    for nb in range(2):
        load_w(k, WV, kvv[nb, :, :])
        for mt in range(2):
            b = inproj_tm(k, WV, mt, 512, src=MN, srcT=256)
            vv = Vm.v(Vm.ap[:, mt, nb * 512:(nb + 1) * 512], mt * 1024 + nb * 512, 512)
            P.op("ACT", lambda e, vv=vv, b=b: e.activation(out=vv.ap, in_=k.psum[b][:, :], func=AF.Copy),
                 reads=[PSV(k, b)], writes=[vv])
    A.release(m1)
    XQT = Tile(A, [2, T], BF16)
    XAT = Tile(A, [8, T], BF16)
    ET = [Tile(A, [512], BF16) for _ in range(2)]
    RD = Tile(A, [512], F32)
    WOX = Tile(A, [8, D], BF16)
    xqw, wo = k.w(f"xq{i}"), k.w(f"wo{i}")
    for c in range(8):
        wv = WOX.v(WOX.ap[:, c, :], c * D, D)
        P.dma("POOL", wv.ap, wo[mixdim // 128 + c, :, :], writes=[wv], max_dma_last_dim=4096)
    for a in range(4):
        for dc in range(2):
            w = WR[wi % 3]
            wi += 1
            load_w(k, w, xqw[a * 2 + dc, :, :])
            for tb in range(4):
                b = inproj_fm(k, w, tb * 512, 512)
                xv = XQT.v(XQT.ap[:, dc, tb * 512:(tb + 1) * 512], dc * T + tb * 512, 512)
                P.op("ACT", lambda e, xv=xv, b=b: e.activation(out=xv.ap, in_=k.psum[b][:, :], func=AF.Copy),
                     reads=[PSV(k, b)], writes=[xv])
        for tb in range(4):
            for mt in range(2):
                b = ps_bank(k)
                for dc in range(2):
                    kv = KmT.v(KmT.ap[:, a * 2 + dc, mt * 128:(mt + 1) * 128], (a * 2 + dc) * 256 + mt * 128, 128)
                    xv = XQT.v(XQT.ap[:, dc, tb * 512:(tb + 1) * 512], dc * T + tb * 512, 512)
                    P.op("PE", lambda e, kv=kv, xv=xv, b=b, dc=dc: e.matmul(
                        k.psum[b][:, :], lhsT=kv.ap, rhs=xv.ap, start=(dc == 0), stop=(dc == 1)),
                        reads=[kv, xv], writes=[PSV(k, b)])
                P.op("ACT", lambda e, b=b, mt=mt: e.activation(out=ET[mt].ap, in_=k.psum[b][:, :], func=AF.Exp),
                     reads=[PSV(k, b)], writes=[ET[mt].full()])
            b = ps_bank(k)
            for mt in range(2):
                P.op("PE", lambda e, b=b, mt=mt: e.matmul(
                    k.psum[b][:, :], lhsT=k.ONESB.ap, rhs=ET[mt].ap, start=(mt == 0), stop=(mt == 1)),
                    reads=[k.ONESB.full(), ET[mt].full()], writes=[PSV(k, b)])
            P.op("DVE", lambda e, b=b: e.reciprocal(out=RD.ap, in_=k.psum[b][:, :]),
                 reads=[PSV(k, b)], writes=[RD.full()])
            for dc in range(2):
                b = ps_bank(k)
                for mt in range(2):
                    c0 = a * 256 + dc * 128
                    vv = Vm.v(Vm.ap[:, mt, c0:c0 + 128], mt * 1024 + c0, 128)
                    P.op("PE", lambda e, vv=vv, b=b, mt=mt: e.matmul(
                        k.psum[b][:, :], lhsT=vv.ap, rhs=ET[mt].ap, start=(mt == 0), stop=(mt == 1)),
                        reads=[vv, ET[mt].full()], writes=[PSV(k, b)])
                xav = XAT.v(XAT.ap[:, a * 2 + dc, tb * 512:(tb + 1) * 512], (a * 2 + dc) * T + tb * 512, 512)
                P.op("DVE", lambda e, xav=xav, b=b: e.tensor_tensor(
                    out=xav.ap, in0=k.psum[b][:, :], in1=RD.ap, op=ALU.mult),
                    reads=[PSV(k, b), RD.full()], writes=[xav])
    emit_outproj(k, XAT, WOX, 8)


def emit_outproj(k, XT, WO, nk, scale_col=None):
    P = k.P
    for j in range(NT):
        for nb in range(2):
            b = ps_bank(k)
            for c in range(nk):
                av = XT.v(XT.ap[:, c, j * 128:(j + 1) * 128], c * T + j * 128, 128)
                wv = WO.v(WO.ap[:, c, nb * 512:(nb + 1) * 512], c * D + nb * 512, 512)
                P.op("PE", lambda e, av=av, wv=wv, b=b, c=c: e.matmul(
                    k.psum[b][:, :], lhsT=av.ap, rhs=wv.ap, start=(c == 0), stop=(c == nk - 1)),
                    reads=[av, wv], writes=[PSV(k, b)])
            hv = hview(k, j, nb * 512, 512)
            if scale_col is None:
                P.op("DVE", lambda e, hv=hv, b=b: e.tensor_tensor(
                    out=hv.ap, in0=k.psum[b][:, :], in1=hv.ap, op=ALU.add),
                    reads=[PSV(k, b), hv], writes=[hv])
            else:
                sc = scale_col(j)
                P.op("DVE", lambda e, hv=hv, b=b, sc=sc: e.scalar_tensor_tensor(
                    out=hv.ap, in0=k.psum[b][:, :], scalar=sc.ap, in1=hv.ap, op0=ALU.mult, op1=ALU.add),
                    reads=[PSV(k, b), hv, sc], writes=[hv])


def exchange(k, src_v, rows, cols, dst_tile, name):
    P, nc = k.P, k.nc
    ib = nc.dram_tensor(name + "_i", [rows, cols], F32)
    ob = nc.dram_tensor(name + "_o", [2 * rows, cols], F32)
    ci, co = V(None, [("D", name + "_i")]), V(None, [("D", name + "_o")])
    P.dma("POOL", ib[:, :], src_v.ap, reads=[src_v], writes=[ci])
    P.coll(lambda e: e.collective_compute("AllGather", ALU.bypass, replica_groups=PAIRS,
                                          ins=[ib.ap().opt()], outs=[ob.ap().opt()]),
           reads=[ci], writes=[co])
    dv = V(dst_tile.ap[0:rows], dst_tile.full().cells)
    P.dma("POOL", dv.ap, ob[0:rows, :], reads=[co], writes=[dv])
    P.op("DVE", lambda e: e.tensor_scalar(out=dv.ap, in0=dv.ap, scalar1=k.FLAG.ap[0:rows], scalar2=None, op0=ALU.mult),
         reads=[dv, k.FLAG], writes=[dv])


def emit_gla(k, i):
    P, A = k.P, k.A
    j = i // 2
    GLRT = Tile(A, [T], BF16)
    W2 = Tile(A, [512], BF16)
    SPL = Tile(A, [T], F32)
    CS = Tile(A, [T], F32)
    FQ = Tile(A, [512], F32)
    FK = Tile(A, [512], F32)
    FE = Tile(A, [512], F32)
    QT = Tile(A, [T], BF16, off=SPL.off)
    QT2 = Tile(A, [T], BF16, off=SPL.off + T * 2)
    KNT = Tile(A, [T], BF16)
    KET = Tile(A, [T], BF16)
    KE = Tile(A, [NT, 128], BF16)
    Vt = Tile(A, [NT, 256], BF16)
    Gt = Tile(A, [NT, 256], BF16)
    SLOC = Tile(A, [NT, 256], BF16)
    SM = Tile(A, [16 * 6 + 8], F32)
    CSS = SM.ap[:, 0:16]
    CSE = SM.ap[:, 16:32]
    DEC = SM.ap[:, 32:48]
    DCUM = SM.ap[:, 48:64]
    NBG = SM.ap[:, 64:68]
    SSo = SM.ap[:, 68:69]
    RSo = SM.ap[:, 69:70]
    smf = SM.full()
    TTs = [Tile(A, [256], F32) for _ in range(3)]
    MTOKs = [Tile(A, [256], BF16) for _ in range(3)]
    STs = [Tile(A, [128], BF16) for _ in range(3)]
    SRs = [Tile(A, [2], F32) for _ in range(3)]
    MIXT = Tile(A, [2, T], BF16)
    WOH = Tile(A, [2, D], BF16)
    HNB = Tile(A, [256], F32)
    S = Tile(A, [256], F32)
    SB32 = Tile(A, [256], F32)
    SBb = Tile(A, [256], BF16)
    WA = Tile(A, [KC, 128], BF16)
    WB = Tile(A, [KC, 128], BF16)
    WC = Tile(A, [KC, 256], BF16)
    WD = Tile(A, [KC, 256], BF16)
    gq, gk, gv, gg = k.w(f"gq{j}"), k.w(f"gk{j}"), k.w(f"gv{j}"), k.w(f"gg{j}")
    wo = k.w(f"wo{i}")
    bcp = k.dr["bcp"]
    P.dma("SP", HNB.ap, bcp[:, BCP[("hnorm", j)]:BCP[("hnorm", j)] + 256], writes=[HNB.full()])
    bg = PAR[("bgate", j)]
    pv = k.PAR.v(k.PAR.ap[:, bg:bg + 4], bg, 4)
    P.op("DVE", lambda e: e.tensor_scalar(out=NBG, in0=pv.ap, scalar1=-1.0, scalar2=None, op0=ALU.mult),
         reads=[pv], writes=[smf])
    WGL = Tile(A, [KC, 16], BF16)
    load_w(k, WGL, k.w(f"glr{j}")[:, :])
    P.dma("POOL", W2.ap[0:16, :], k.w(f"gw2{j}")[:, :], writes=[W2.full()])
    for tb in range(4):
        b = inproj_fm(k, WGL, tb * 512, 512, M=16)
        gv_ = GLRT.v(GLRT.ap[0:16, tb * 512:(tb + 1) * 512], tb * 512, 512)
        P.op("ACT", lambda e, gv_=gv_, b=b: e.activation(out=gv_.ap, in_=k.psum[b][0:16, :], func=AF.Copy),
             reads=[PSV(k, b)], writes=[gv_])
    for h in range(4):
        for tb in range(4):
            b = ps_bank(k)
            gv_ = GLRT.v(GLRT.ap[0:16, tb * 512:(tb + 1) * 512], tb * 512, 512)
            P.op("PE", lambda e, b=b, gv_=gv_, h=h: e.matmul(
                k.psum[b][:, :], lhsT=W2.ap[0:16, h * 128:(h + 1) * 128], rhs=gv_.ap, start=True, stop=True),
                reads=[W2.full(), gv_], writes=[PSV(k, b)])
            sv = SPL.v(SPL.ap[:, tb * 512:(tb + 1) * 512], tb * 512, 512)
            P.op("ACT", lambda e, b=b, sv=sv, h=h: e.activation(
                out=sv.ap, in_=k.psum[b][:, :], func=AF.Exp, scale=-1.0, bias=NBG[:, h:h + 1]),
                reads=[PSV(k, b), smf], writes=[sv])
        P.op("ACT", lambda e: e.activation(out=SPL.ap, in_=SPL.ap, func=AF.Ln, bias=1.0),
             reads=[SPL.full()], writes=[SPL.full()])
        P.op("DVE", lambda e: e.tensor_tensor_scan(
            out=CS.ap, data0=k.ONE1.ap.to_broadcast([128, T]), data1=SPL.ap, initial=0.0,
            op0=ALU.mult, op1=ALU.add), reads=[SPL.full(), k.ONE1.full()], writes=[CS.full()])
        cs3 = CS.ap.rearrange("p (n c) -> p n c", c=128)
        P.op("DVE", lambda e: e.memset(SM.ap[:, 0:1], 0.0), writes=[smf])
        P.op("DVE", lambda e: e.tensor_copy(out=SM.ap[:, 1:16], in_=cs3[:, 0:15, 127]), reads=[CS.full()], writes=[smf])
        P.op("DVE", lambda e: e.tensor_copy(out=CSE, in_=cs3[:, :, 127]), reads=[CS.full()], writes=[smf])
        P.op("DVE", lambda e: e.tensor_tensor(out=DEC, in0=CSE, in1=CSS, op=ALU.subtract), reads=[smf], writes=[smf])
        P.op("ACT", lambda e: e.activation(out=DEC, in_=DEC, func=AF.Exp, scale=-1.0 / 16), reads=[smf], writes=[smf])
        P.op("ACT", lambda e: e.activation(out=DCUM, in_=CSS, func=AF.Exp, scale=-1.0 / 16), reads=[smf], writes=[smf])
        P.op("DVE", lambda e: e.tensor_tensor(
            out=cs3, in0=cs3, in1=CSS.unsqueeze(2).to_broadcast([128, NT, 128]), op=ALU.subtract),
            reads=[CS.full(), smf], writes=[CS.full()])
        load_w(k, WB, gk[h, :, :])
        for tb in range(4):
            dv = CS.v(CS.ap[:, tb * 512:(tb + 1) * 512], tb * 512, 512)
            P.op("ACT", lambda e, dv=dv: e.activation(out=FK.ap, in_=dv.ap, func=AF.Exp, scale=1.0 / 16),
                 reads=[dv], writes=[FK.full()])
            fe3 = FE.ap.rearrange("p (n c) -> p n c", c=128)
            d3 = dv.ap.rearrange("p (n c) -> p n c", c=128)
            P.op("DVE", lambda e, d3=d3, fe3=fe3, tb=tb: e.tensor_tensor(
                out=fe3, in0=d3, in1=CSE[:, tb * 4:(tb + 1) * 4].unsqueeze(2).to_broadcast([128, 4, 128]),
                op=ALU.subtract), reads=[dv, smf], writes=[FE.full()])
            P.op("DVE", lambda e, fe3=fe3, tb=tb: e.tensor_tensor(
                out=fe3, in0=fe3, in1=CSS[:, tb * 4:(tb + 1) * 4].unsqueeze(2).to_broadcast([128, 4, 128]),
                op=ALU.add), reads=[FE.full(), smf], writes=[FE.full()])
            P.op("ACT", lambda e: e.activation(out=FE.ap, in_=FE.ap, func=AF.Exp, scale=1.0 / 16),
                 reads=[FE.full()], writes=[FE.full()])
            b = inproj_fm(k, WB, tb * 512, 512)
            kn = KNT.v(KNT.ap[:, tb * 512:(tb + 1) * 512], tb * 512, 512)
            ke = KET.v(KET.ap[:, tb * 512:(tb + 1) * 512], tb * 512, 512)
            P.op("DVE", lambda e, kn=kn, b=b: e.tensor_tensor(out=kn.ap, in0=k.psum[b][:, :], in1=FK.ap, op=ALU.mult),
                 reads=[PSV(k, b), FK.full()], writes=[kn])
            P.op("DVE", lambda e, ke=ke, b=b: e.tensor_tensor(out=ke.ap, in0=k.psum[b][:, :], in1=FE.ap, op=ALU.mult),
                 reads=[PSV(k, b), FE.full()], writes=[ke])
        for g8 in range(2):
            b = ps_bank(k)
            pb = k.psum[b][:, :].bitcast(BF16)
            for t8 in range(8):
                n = g8 * 8 + t8
                ke = KET.v(KET.ap[:, n * 128:(n + 1) * 128], n * 128, 128)
                P.op("PE", lambda e, ke=ke, pb=pb, t8=t8: e.transpose(
                    out=pb[:, t8 * 128:(t8 + 1) * 128], in_=ke.ap, identity=k.IDB.ap),
                    reads=[ke, k.IDB.full()], writes=[PSV(k, b)])
            kv = KE.v(KE.ap[:, g8 * 8:(g8 + 1) * 8, :], g8 * 8 * 128, 8 * 128)
            P.op("ACT", lambda e, kv=kv, pb=pb: e.activation(
                out=kv.ap, in_=pb.rearrange("p (a b) -> p a b", a=8), func=AF.Copy),
                reads=[PSV(k, b)], writes=[kv])
        load_w(k, WC, gv[h, :, :])
        P.op("DVE", lambda e: e.memset(S.ap, 0.0), writes=[S.full()])
        P.op("POOL", lambda e: e.memset(SLOC.ap[:, 0, :], 0.0), writes=[SLOC.v(SLOC.ap[:, 0, :], 0, 256)])

        def rec_step(n):
            b = ps_bank(k)
            kev = KE.v(KE.ap[:, n, :], n * 128, 128)
            vv = Vt.v(Vt.ap[:, n, :], n * 256, 256)
            P.op("PE", lambda e: e.matmul(k.psum[b][:, 0:256], lhsT=kev.ap, rhs=vv.ap, start=True, stop=True),
                 reads=[kev, vv], writes=[PSV(k, b)])
            P.op("DVE", lambda e: e.scalar_tensor_tensor(
                out=S.ap, in0=S.ap, scalar=DEC[:, n:n + 1], in1=k.psum[b][:, 0:256], op0=ALU.mult, op1=ALU.add),
                reads=[S.full(), smf, PSV(k, b)], writes=[S.full()])
            if n < NT - 1:
                sl = SLOC.v(SLOC.ap[:, n + 1, :], (n + 1) * 256, 256)
                P.op("ACT", lambda e: e.activation(out=sl.ap, in_=S.ap, func=AF.Copy),
                     reads=[S.full()], writes=[sl])

        for n in range(NT + 2):
            if n < NT:
                b = inproj_tm(k, WC, n, 256)
                vv = Vt.v(Vt.ap[:, n, :], n * 256, 256)
                P.op("ACT", lambda e, vv=vv, b=b: e.activation(out=vv.ap, in_=k.psum[b][:, 0:256], func=AF.Copy),
                     reads=[PSV(k, b)], writes=[vv])
            if 0 <= n - 2 < NT:
                rec_step(n - 2)
        exchange(k, S.full(), 128, 256, SB32, f"gx{i}_{h}")
        P.op("ACT", lambda e: e.activation(out=SBb.ap, in_=SB32.ap, func=AF.Copy), reads=[SB32.full()], writes=[SBb.full()])
        load_w(k, WA, gq[h, :, :])
        for tb in range(4):
            dv = CS.v(CS.ap[:, tb * 512:(tb + 1) * 512], tb * 512, 512)
            P.op("ACT", lambda e, dv=dv: e.activation(out=FQ.ap, in_=dv.ap, func=AF.Exp, scale=-1.0 / 16,
                                                      bias=k.LNQ.ap), reads=[dv, k.LNQ.full()], writes=[FQ.full()])
            b = inproj_fm(k, WA, tb * 512, 512)
            qv = QT.v(QT.ap[:, tb * 512:(tb + 1) * 512], tb * 512, 512)
            P.op("DVE", lambda e, qv=qv, b=b: e.tensor_tensor(out=qv.ap, in0=k.psum[b][:, :], in1=FQ.ap, op=ALU.mult),
                 reads=[PSV(k, b), FQ.full()], writes=[qv])
        P.op("DVE", lambda e: e.tensor_tensor(
            out=QT2.ap.rearrange("p (n c) -> p n c", c=128), in0=QT.ap.rearrange("p (n c) -> p n c", c=128),
            in1=DCUM.unsqueeze(2).to_broadcast([128, NT, 128]), op=ALU.mult),
            reads=[QT.full(), smf], writes=[QT2.full()])
        load_w(k, WD, gg[h, :, :])
        for n in range(NT):
            b = inproj_tm(k, WD, n, 256)
            gv2 = Gt.v(Gt.ap[:, n, :], n * 256, 256)
            P.op("ACT", lambda e, gv2=gv2, b=b: e.activation(out=gv2.ap, in_=k.psum[b][:, 0:256], func=AF.Silu),
                 reads=[PSV(k, b)], writes=[gv2])
        for c in range(2):
            wv = WOH.v(WOH.ap[:, c, :], c * D, D)
            P.dma("POOL", wv.ap, wo[h * 2 + c, :, :], writes=[wv], max_dma_last_dim=4096)
        NB3 = 3
        sc_bank, o_bank = {}, {}

        def g_stage1(n):
            b = ps_bank(k)
            sc_bank[n] = b
            kn = KNT.v(KNT.ap[:, n * 128:(n + 1) * 128], n * 128, 128)
            qv = QT.v(QT.ap[:, n * 128:(n + 1) * 128], n * 128, 128)
            st = STs[n % NB3]
            P.op("PE", lambda e: e.matmul(k.psum[b][:, 0:128], lhsT=kn.ap, rhs=qv.ap, start=True, stop=True),
                 reads=[kn, qv], writes=[PSV(k, b)])
            P.op("DVE", lambda e: e.tensor_tensor(out=st.ap, in0=k.psum[b][:, 0:128], in1=k.MASKU.ap, op=ALU.mult),
                 reads=[PSV(k, b), k.MASKU.full()], writes=[st.full()])

        def g_stage2(n):
            b2 = ps_bank(k)
            o_bank[n] = b2
            st, tt, mtok, sr = STs[n % NB3], TTs[n % NB3], MTOKs[n % NB3], SRs[n % NB3]
            qv = QT.v(QT.ap[:, n * 128:(n + 1) * 128], n * 128, 128)
            q2 = QT2.v(QT2.ap[:, n * 128:(n + 1) * 128], n * 128, 128)
            vv = Vt.v(Vt.ap[:, n, :], n * 256, 256)
            sl = SLOC.v(SLOC.ap[:, n, :], n * 256, 256)
            P.op("PE", lambda e: e.matmul(k.psum[b2][:, 0:256], lhsT=st.ap, rhs=vv.ap, start=True, stop=False),
                 reads=[st.full(), vv], writes=[PSV(k, b2)])
            P.op("PE", lambda e: e.matmul(k.psum[b2][:, 0:256], lhsT=qv.ap, rhs=sl.ap, start=False, stop=False),
                 reads=[qv, sl], writes=[PSV(k, b2)])
            P.op("PE", lambda e: e.matmul(k.psum[b2][:, 0:256], lhsT=q2.ap, rhs=SBb.ap, start=False, stop=True),
                 reads=[q2, SBb.full()], writes=[PSV(k, b2)])
            srf = sr.full()
            P.op("ACT", lambda e: e.activation(out=tt.ap, in_=k.psum[b2][:, 0:256], func=AF.Square, accum_out=sr.ap[:, 0:1]),
                 reads=[PSV(k, b2)], writes=[tt.full(), srf])
            P.op("DVE", lambda e: e.tensor_scalar(out=sr.ap[:, 1:2], in0=sr.ap[:, 0:1], scalar1=1.0 / 256, scalar2=EPS,
                                                   op0=ALU.mult, op1=ALU.add), reads=[srf], writes=[srf])
            P.op("DVE", lambda e: e.reciprocal(out=sr.ap[:, 1:2], in_=sr.ap[:, 1:2]), reads=[srf], writes=[srf])
            P.op("ACT", lambda e: e.activation(out=sr.ap[:, 1:2], in_=sr.ap[:, 1:2], func=AF.Sqrt), reads=[srf], writes=[srf])
            P.op("DVE", lambda e: e.scalar_tensor_tensor(
                out=tt.ap, in0=k.psum[b2][:, 0:256], scalar=sr.ap[:, 1:2], in1=HNB.ap, op0=ALU.mult, op1=ALU.mult),
                reads=[PSV(k, b2), srf, HNB.full()], writes=[tt.full()])
            gv2 = Gt.v(Gt.ap[:, n, :], n * 256, 256)
            P.op("DVE", lambda e: e.tensor_tensor(out=mtok.ap, in0=tt.ap, in1=gv2.ap, op=ALU.mult),
                 reads=[tt.full(), gv2], writes=[mtok.full()])

        def g_stage3(n):
            mtok = MTOKs[n % NB3]
            b3 = ps_bank(k)
            pb = k.psum[b3][:, :].bitcast(BF16)
            for c in range(2):
                P.op("PE", lambda e, c=c: e.transpose(
                    out=pb[:, c * 128:(c + 1) * 128], in_=mtok.ap[:, c * 128:(c + 1) * 128], identity=k.IDB.ap),
                    reads=[mtok.full(), k.IDB.full()], writes=[PSV(k, b3)])
            mv = MIXT.vs(MIXT.ap[:, :, n * 128:(n + 1) * 128], [(c * T + n * 128, 128) for c in range(2)])
            P.op("ACT", lambda e: e.activation(
                out=mv.ap, in_=pb[:, 0:256].rearrange("p (a b) -> p a b", a=2), func=AF.Copy),
                reads=[PSV(k, b3)], writes=[mv])

        for s_ in range(NT + 2):
            if s_ < NT:
                g_stage1(s_)
            if 0 <= s_ - 1 < NT:
                g_stage2(s_ - 1)
            if 0 <= s_ - 2 < NT:
                g_stage3(s_ - 2)
        emit_outproj(k, MIXT, WOH, 2)


def emit_ssd(k, i):
    P, A, nc = k.P, k.A, k.nc
    j = i // 2
    sz, sxbc = k.w(f"sz{j}"), k.w(f"sxbc{j}")
    wo = k.w(f"wo{i}")
    sdk = k.w(f"sdk{j}")
    SCV = Tile(A, [164], F32)
    QTM = Tile(A, [NT, 128], F32)
    DB = Tile(A, [2, 32, 16], F32)
    HALO = Tile(A, [96], F32)
    SSQ = Tile(A, [NT, 8], F32)
    SM = Tile(A, [64], F32)
    CST, CEN, DEC_, DCU_ = SM.ap[:, 0:16], SM.ap[:, 16:32], SM.ap[:, 32:48], SM.ap[:, 48:64]
    AV = Tile(A, [1], F32)
    MNEG = Tile(A, [128], F32)
    smf = SM.full()
    fv = MNEG.full()
    P.op("POOL", lambda e: e.memset(MNEG.ap, -1.0e5), writes=[fv])
    P.op("POOL", lambda e: e.affine_select(
        out=MNEG.ap, in_=MNEG.ap, pattern=[[-1, 128]], compare_op=ALU.is_gt, fill=0.0,
        base=0, channel_multiplier=1), reads=[fv], writes=[fv])
    P.dma("SP", SCV.ap, k.w(f"scv{j}")[:, :], writes=[SCV.full()])
    scvf = SCV.full()
    cslD = nc.dram_tensor(f"csl{i}", [32, T], F32)
    dcD = nc.dram_tensor(f"dcd{i}", [2, 32, 16], F32)
    ytD = nc.dram_tensor(f"ytd{i}", [NT, 128, 2048], BF16)
    csl_v = V(None, [("D", f"csl{i}")])
    dcd_v = V(None, [("D", f"dcd{i}")])
    m_layer = A.mark()
    WDT = Tile(A, [KC, 128], BF16)
    DTt = Tile(A, [T], F32)
    CSL = Tile(A, [T], F32)
    Q4 = Tile(A, [T], F32)
    load_w(k, WDT, k.w(f"sdt{j}")[:, :])
    for tb in range(4):
        b = inproj_fm(k, WDT, tb * 512, 512)
        dv = DTt.v(DTt.ap[:, tb * 512:(tb + 1) * 512], tb * 512, 512)
        P.op("ACT", lambda e, dv=dv, b=b: e.activation(out=dv.ap, in_=k.psum[b][:, :], func=AF.Exp,
                                                         bias=SCV.ap[:, 160:161]), reads=[PSV(k, b), scvf], writes=[dv])
    P.op("ACT", lambda e: e.activation(out=DTt.ap, in_=DTt.ap, func=AF.Ln, bias=1.0), reads=[DTt.full()], writes=[DTt.full()])
    P.op("ACT", lambda e: e.activation(out=AV.ap, in_=SCV.ap[:, 161:162], func=AF.Exp), reads=[scvf], writes=[AV.full()])
    P.op("DVE", lambda e: e.tensor_scalar(out=AV.ap, in0=AV.ap, scalar1=-1.0, scalar2=None, op0=ALU.mult),
         reads=[AV.full()], writes=[AV.full()])
    P.op("DVE", lambda e: e.tensor_scalar(out=Q4.ap, in0=DTt.ap, scalar1=AV.ap, scalar2=None, op0=ALU.mult),
         reads=[DTt.full(), AV.full()], writes=[Q4.full()])
    P.op("DVE", lambda e: e.tensor_tensor_scan(
        out=CSL.ap, data0=k.ONE1.ap.to_broadcast([128, T]), data1=Q4.ap, initial=0.0,
        op0=ALU.mult, op1=ALU.add), reads=[Q4.full(), k.ONE1.full()], writes=[CSL.full()])
    cs3 = CSL.ap.rearrange("p (n c) -> p n c", c=128)
    P.op("DVE", lambda e: e.memset(SM.ap[:, 0:1], 0.0), writes=[smf])
    P.op("DVE", lambda e: e.tensor_copy(out=SM.ap[:, 1:16], in_=cs3[:, 0:15, 127]), reads=[CSL.full()], writes=[smf])
    P.op("DVE", lambda e: e.tensor_copy(out=CEN, in_=cs3[:, :, 127]), reads=[CSL.full()], writes=[smf])
    P.op("DVE", lambda e: e.tensor_tensor(out=DEC_, in0=CEN, in1=CST, op=ALU.subtract), reads=[smf], writes=[smf])
    P.op("ACT", lambda e: e.activation(out=DEC_, in_=DEC_, func=AF.Exp), reads=[smf], writes=[smf])
    P.op("ACT", lambda e: e.activation(out=DCU_, in_=CST, func=AF.Exp), reads=[smf], writes=[smf])
    smv = V(SM.ap[0:32, 32:64].rearrange("p (a n) -> p a n", a=2), SM.full().cells)
    P.dma("SP", dcD.ap().rearrange("a h n -> h a n"), smv.ap, reads=[smv], writes=[dcd_v])
    P.dma("SP", DB.ap, dcD.ap().partition_broadcast(128), reads=[dcd_v], writes=[DB.full()])
    P.op("DVE", lambda e: e.tensor_tensor(out=CEN, in0=CEN, in1=CST, op=ALU.subtract), reads=[smf], writes=[smf])
    P.op("DVE", lambda e: e.tensor_tensor(
        out=cs3, in0=cs3, in1=CST.unsqueeze(2).to_broadcast([128, NT, 128]), op=ALU.subtract),
        reads=[CSL.full(), smf], writes=[CSL.full()])
    cslsb = V(CSL.ap[0:32, :], CSL.full().cells)
    P.dma("SP", cslD[:, :], cslsb.ap, reads=[cslsb], writes=[csl_v])
    P.op("ACT", lambda e: e.activation(out=Q4.ap, in_=DTt.ap, func=AF.Ln), reads=[DTt.full()], writes=[Q4.full()])
    P.op("DVE", lambda e: e.tensor_tensor(out=Q4.ap, in0=Q4.ap, in1=CSL.ap, op=ALU.subtract),
         reads=[Q4.full(), CSL.full()], writes=[Q4.full()])
    P.op("ACT", lambda e: e.activation(out=Q4.ap[0:32, :], in_=CSL.ap[0:32, :], func=AF.Exp),
         reads=[CSL.full()], writes=[Q4.full()])
    q3 = Q4.ap.rearrange("p (n c) -> p n c", c=128)
    P.op("DVE", lambda e: e.tensor_tensor(
        out=q3[32:64], in0=CEN[32:64].unsqueeze(2).to_broadcast([32, NT, 128]), in1=cs3[32:64], op=ALU.subtract),
        reads=[CSL.full(), smf], writes=[Q4.full()])
    P.op("ACT", lambda e: e.activation(out=Q4.ap[32:64, :], in_=Q4.ap[32:64, :], func=AF.Exp),
         reads=[Q4.full()], writes=[Q4.full()])
    P.op("DVE", lambda e: e.tensor_tensor(out=Q4.ap[32:64, :], in0=Q4.ap[32:64, :], in1=DTt.ap[32:64, :], op=ALU.mult),
         reads=[Q4.full(), DTt.full()], writes=[Q4.full()])
    P.op("ACT", lambda e: e.activation(out=Q4.ap[64:96, :], in_=CSL.ap[64:96, :], func=AF.Copy),
         reads=[CSL.full()], writes=[Q4.full()])
    for n4 in range(4):
        b = ps_bank(k)
        for t in range(4):
            n = n4 * 4 + t
            qv = Q4.v(Q4.ap[:, n * 128:(n + 1) * 128], n * 128, 128)
            P.op("PE", lambda e, qv=qv, b=b, t=t: e.transpose(
                out=k.psum[b][:, t * 128:(t + 1) * 128], in_=qv.ap, identity=k.IDF.ap),
                reads=[qv, k.IDF.full()], writes=[PSV(k, b)])
        qm = QTM.v(QTM.ap[:, n4 * 4:(n4 + 1) * 4, :], n4 * 4 * 128, 4 * 128)
        P.op("ACT", lambda e, qm=qm, b=b: e.activation(
            out=qm.ap, in_=k.psum[b][:, :].rearrange("p (a b) -> p a b", a=4), func=AF.Copy),
            reads=[PSV(k, b)], writes=[qm])
    A.release(m_layer)
    HL = Tile(A, [32, 3], F32)
    WR = [Tile(A, [KC, 128], BF16) for _ in range(2)]
    wi = 0
    for cb in range(32):
        w = WR[wi % 2]
        wi += 1
        load_w(k, w, sxbc[cb, :, :])
        b = ps_bank(k)
        for kc in range(KC):
            xv = xn_blk(k, kc, T - 3, 3)
            P.op("PE", lambda e, kc=kc, xv=xv, b=b, w=w: e.matmul(
                k.psum[b][:, 0:3], lhsT=w.ap[:, kc, :], rhs=xv.ap, start=(kc == 0), stop=(kc == KC - 1)),
                reads=[w.full(), xv], writes=[PSV(k, b)])
        hv = HL.v(HL.ap[:, cb, :], cb * 3, 3)
        P.op("ACT", lambda e, hv=hv, b=b: e.activation(out=hv.ap, in_=k.psum[b][:, 0:3], func=AF.Copy),
             reads=[PSV(k, b)], writes=[hv])
    exchange(k, V(HL.ap.rearrange("p a b -> p (a b)"), HL.full().cells), 128, 96, HALO, f"hx{i}")
    A.release(m_layer)
    WR = [Tile(A, [KC, 128], BF16) for _ in range(2)]
    WZ = Tile(A, [KC, 256], BF16)
    PRE = Tile(A, [T + 4], BF16)
    DG = [Tile(A, [4, 128], BF16) for _ in range(2)]
    DGD = Tile(A, [4, 128], BF16)
    XsT = Tile(A, [2, T], BF16)
    BT = Tile(A, [T], BF16)
    CT = Tile(A, [T], BF16)
    XS = Tile(A, [NT, 256], BF16)
    XW = Tile(A, [NT, 256], BF16)
    Btok = Tile(A, [NT, 128], BF16)
    SLOC = Tile(A, [NT, 256], BF16, off=XsT.off)
    YT = Tile(A, [2, T], BF16)
    S = Tile(A, [256], F32)
    SB32 = Tile(A, [256], F32)
    NB3 = 4
    SBns = [Tile(A, [256], BF16) for _ in range(NB3)]
    BCts = [Tile(A, [4, 128], F32) for _ in range(3)]
    CBMs = [Tile(A, [128], F32) for _ in range(NB3)]
    SEG = [Tile(A, [128], F32) for _ in range(8)]
    MTs = [[Tile(A, [128], BF16) for _ in range(4)] for _ in range(NB3)]
    T1s = [Tile(A, [256], F32, off=XW.off + r_ * 1024) for r_ in range(NB3)]
    THs = [Tile(A, [256], F32, off=XW.off + 4096 + r_ * 1024) for r_ in range(NB3)]
    YFs = [Tile(A, [256], F32, off=PRE.off + r_ * 1024) for r_ in range(NB3)]
    G2s = [Tile(A, [256], BF16) for _ in range(NB3)]
    YGBs = [Tile(A, [256], BF16) for _ in range(NB3)]
    DSK = Tile(A, [256], F32)
    dgi = 0
    pre_setup = []
    for g in range(8):
        P.dma("SP", DSK.ap, sdk[:, g * 256:(g + 1) * 256], writes=[DSK.full()])
        load_w(k, WZ, sz[g, :, :])
        for jh in range(4):
            dv_ = DGD.v(DGD.ap[:, jh, :], jh * 128, 128)
            P.op("POOL", lambda e, dv_=dv_, jh=jh: e.tensor_scalar(
                out=dv_.ap, in0=k.IDB.ap, scalar1=DSK.ap[:, jh * 64:jh * 64 + 1], scalar2=None, op0=ALU.mult),
                reads=[k.IDB.full(), DSK.full()], writes=[dv_])

        def conv_setup(cb):
            nonlocal wi, dgi
            w = WR[wi % 2]
            wi += 1
            load_w(k, w, sxbc[cb, :, :])
            dg = DG[dgi % 2]
            dgi += 1
            for tap in range(4):
                dgv = dg.v(dg.ap[:, tap, :], tap * 128, 128)
                P.op("POOL", lambda e, dgv=dgv, tap=tap: e.tensor_scalar(
                    out=dgv.ap, in0=k.IDB.ap, scalar1=SCV.ap[:, cb * 4 + tap:cb * 4 + tap + 1], scalar2=None,
                    op0=ALU.mult), reads=[k.IDB.full(), scvf], writes=[dgv])
            return w, dg

        def conv_proj(w, tb, cb=None):
            if tb == 0:
                hp = PRE.v(PRE.ap[:, 0:3], 0, 3)
                P.op("DVE", lambda e: e.tensor_copy(out=hp.ap, in_=HALO.ap[:, cb * 3:(cb + 1) * 3]),
                     reads=[HALO.full()], writes=[hp])
            b = inproj_fm(k, w, tb * 512, 512)
            pv = PRE.v(PRE.ap[:, 3 + tb * 512:3 + (tb + 1) * 512], 3 + tb * 512, 512)
            P.op("ACT", lambda e: e.activation(out=pv.ap, in_=k.psum[b][:, :], func=AF.Copy),
                 reads=[PSV(k, b)], writes=[pv])

        def conv_out(cb, dg, dst, tb):
            b = ps_bank(k)
            for tap in range(4):
                pv = PRE.v(PRE.ap[:, tap + tb * 512:tap + (tb + 1) * 512], tap + tb * 512, 512)
                P.op("PE", lambda e, pv=pv, tap=tap: e.matmul(
                    k.psum[b][:, :], lhsT=dg.ap[:, tap, :], rhs=pv.ap, start=(tap == 0), stop=(tap == 3)),
                    reads=[dg.full(), pv], writes=[PSV(k, b)])
            dv = V(dst.ap[:, tb * 512:(tb + 1) * 512], dst.cells)
            P.op("ACT", lambda e: e.activation(
                out=dv.ap, in_=k.psum[b][:, :], func=AF.Silu, bias=SCV.ap[:, 128 + cb:129 + cb]),
                reads=[PSV(k, b), scvf], writes=[dv])

        dsts = [(2 * g, XsT.v(XsT.ap[:, 0, :], 0, T)), (2 * g + 1, XsT.v(XsT.ap[:, 1, :], T, T)),
                (16 + g, BT.full())]
        for bi, (cb, dst) in enumerate(dsts):
            if bi < 2 and pre_setup:
                w, dg = pre_setup.pop(0)
            else:
                w, dg = conv_setup(cb)
            for tb in range(4):
                conv_proj(w, tb, cb)
            for tb in range(4):
                conv_out(cb, dg, dst, tb)
        cbC = 24 + g
        wC, dgC = None, None

        def tr_batch(q, g=g):
            b = ps_bank(k)
            pb = k.psum[b][:, :].bitcast(BF16)
            for t in range(4):
                n = q * 4 + t
                for c in range(2):
                    xv = XsT.v(XsT.ap[:, c, n * 128:(n + 1) * 128], c * T + n * 128, 128)
                    P.op("PE", lambda e, xv=xv, t=t, c=c: e.transpose(
                        out=pb[:, (t * 2 + c) * 128:(t * 2 + c + 1) * 128], in_=xv.ap, identity=k.IDB.ap),
                        reads=[xv, k.IDB.full()], writes=[PSV(k, b)])
            xs4 = XS.v(XS.ap[:, q * 4:(q + 1) * 4, :], q * 4 * 256, 4 * 256)
            P.op("ACT", lambda e: e.activation(
                out=xs4.ap, in_=pb.rearrange("p (a b) -> p a b", a=4), func=AF.Copy),
                reads=[PSV(k, b)], writes=[xs4])
            b2 = ps_bank(k)
            pb2 = k.psum[b2][:, :].bitcast(BF16)
            for t in range(4):
                n = q * 4 + t
                bv = BT.v(BT.ap[:, n * 128:(n + 1) * 128], n * 128, 128)
                P.op("PE", lambda e, bv=bv, t=t: e.transpose(
                    out=pb2[:, t * 128:(t + 1) * 128], in_=bv.ap, identity=k.IDB.ap),
                    reads=[bv, k.IDB.full()], writes=[PSV(k, b2)])
            b4 = Btok.v(Btok.ap[:, q * 4:(q + 1) * 4, :], q * 4 * 128, 4 * 128)
            P.op("ACT", lambda e: e.activation(
                out=b4.ap, in_=pb2[:, 0:512].rearrange("p (a b) -> p a b", a=4), func=AF.Copy),
                reads=[PSV(k, b2)], writes=[b4])
            xw4 = XW.v(XW.ap[:, q * 4:(q + 1) * 4, :], q * 4 * 256, 4 * 256)
            P.op("DVE", lambda e: e.tensor_tensor(
                out=xw4.ap.rearrange("p n (h q) -> p n h q", h=4), in0=xs4.ap.rearrange("p n (h q) -> p n h q", h=4),
                in1=QTM.ap[:, q * 4:(q + 1) * 4, 32 + 4 * g:36 + 4 * g].unsqueeze(3).to_broadcast([128, 4, 4, 64]),
                op=ALU.mult), reads=[xs4, QTM.full()], writes=[xw4])

        s3 = S.ap.rearrange("p (h q) -> p h q", h=4)

        def rec_step(n, g=g):
            b = ps_bank(k)
            bv = Btok.v(Btok.ap[:, n, :], n * 128, 128)
            xw = XW.v(XW.ap[:, n, :], n * 256, 256)
            P.op("PE", lambda e: e.matmul(k.psum[b][:, 0:256], lhsT=bv.ap, rhs=xw.ap, start=True, stop=True),
                 reads=[bv, xw], writes=[PSV(k, b)])
            P.op("DVE", lambda e: e.tensor_tensor(
                out=s3, in0=s3, in1=DB.ap[:, 0, 4 * g:4 * g + 4, n].unsqueeze(2).to_broadcast([128, 4, 64]),
                op=ALU.mult), reads=[S.full(), DB.full()], writes=[S.full()])
            P.op("DVE", lambda e: e.tensor_tensor(out=S.ap, in0=k.psum[b][:, 0:256], in1=S.ap, op=ALU.add),
                 reads=[S.full(), PSV(k, b)], writes=[S.full()])
            if n < NT - 1:
                sl = SLOC.v(SLOC.ap[:, n + 1, :], (n + 1) * 256, 256)
                P.op("ACT", lambda e: e.activation(out=sl.ap, in_=S.ap, func=AF.Copy),
                     reads=[S.full()], writes=[sl])

        P.op("DVE", lambda e: e.memset(S.ap, 0.0), writes=[S.full()])
        for q in range(4):
            tr_batch(q)
        P.op("POOL", lambda e: e.memset(SLOC.ap[:, 0, :], 0.0), writes=[SLOC.v(SLOC.ap[:, 0, :], 0, 256)])
        wC, dgC = conv_setup(cbC)
        for q in range(4):
            conv_proj(wC, q, cbC)
            for n in range(q * 4, q * 4 + 4):
                rec_step(n)
        exchange(k, S.full(), 128, 256, SB32, f"sx{i}_{g}")
        for tb in range(4):
            conv_out(cbC, dgC, CT.full(), tb)
        if g < 7:
            pre_setup.extend([conv_setup(2 * (g + 1)), conv_setup(2 * (g + 1) + 1)])
        sb3 = SB32.ap.rearrange("p (h q) -> p h q", h=4)
        def s_stage1(n, g=g):
            r = n % NB3
            bct, cbm, th, g2 = BCts[n % 3], CBMs[r], THs[r], G2s[r]
            bv = BT.v(BT.ap[:, n * 128:(n + 1) * 128], n * 128, 128)
            cv = CT.v(CT.ap[:, n * 128:(n + 1) * 128], n * 128, 128)
            b = ps_bank(k)
            P.op("PE", lambda e: e.matmul(k.psum[b][:, 0:128], lhsT=bv.ap, rhs=cv.ap, start=True, stop=True),
                 reads=[bv, cv], writes=[PSV(k, b)])
            P.op("DVE", lambda e: e.tensor_tensor(out=cbm.ap, in0=k.psum[b][:, 0:128], in1=k.MASKU.ap, op=ALU.mult),
                 reads=[PSV(k, b), k.MASKU.full()], writes=[cbm.full()])
            P.dma("SP", bct.ap, cslD[4 * g:4 * g + 4, n * 128:(n + 1) * 128].partition_broadcast(128),
                  reads=[csl_v], writes=[bct.full()])
            for jh in range(4):
                h = 4 * g + jh
                sg = SEG[(n * 4 + jh) % len(SEG)]
                mt = MTs[r][jh]
                P.op("DVE", lambda e, sg=sg, jh=jh, h=h: e.tensor_scalar(
                    out=sg.ap, in0=bct.ap[:, jh, :], scalar1=QTM.ap[:, n, 64 + h:65 + h], scalar2=None, op0=ALU.min),
                    reads=[bct.full(), QTM.full()], writes=[sg.full()])
                P.op("ACT", lambda e, sg=sg, h=h: e.activation(
                    out=sg.ap, in_=sg.ap, func=AF.Exp, bias=QTM.ap[:, n, 96 + h:97 + h]),
                    reads=[sg.full(), QTM.full()], writes=[sg.full()])
                P.op("DVE", lambda e, sg=sg, mt=mt: e.tensor_tensor(out=mt.ap, in0=sg.ap, in1=cbm.ap, op=ALU.mult),
                     reads=[sg.full(), cbm.full()], writes=[mt.full()])
            bz = inproj_tm(k, WZ, n, 256)
            P.op("ACT", lambda e: e.activation(out=th.ap, in_=k.psum[bz][:, 0:256], func=AF.Tanh, scale=0.5),
                 reads=[PSV(k, bz)], writes=[th.full()])
            P.op("DVE", lambda e: e.scalar_tensor_tensor(
                out=g2.ap, in0=th.ap, scalar=1.0, in1=k.psum[bz][:, 0:256], op0=ALU.add, op1=ALU.mult),
                reads=[th.full(), PSV(k, bz)], writes=[g2.full()])

        def s_stage2(n, g=g):
            r = n % NB3
            t1, yf, g2, ygb, sbn = T1s[r], YFs[r], G2s[r], YGBs[r], SBns[r]
            sl = SLOC.v(SLOC.ap[:, n, :], n * 256, 256)
            cv = CT.v(CT.ap[:, n * 128:(n + 1) * 128], n * 128, 128)
            P.op("DVE", lambda e: e.tensor_tensor(
                out=sbn.ap.rearrange("p (h q) -> p h q", h=4), in0=sb3,
                in1=DB.ap[:, 1, 4 * g:4 * g + 4, n].unsqueeze(2).to_broadcast([128, 4, 64]), op=ALU.mult),
                reads=[SB32.full(), DB.full()], writes=[sbn.full()])
            by = ps_bank(k)
            for jh in range(4):
                mt = MTs[r][jh]
                xs = XS.v(XS.ap[:, n, jh * 64:(jh + 1) * 64], n * 256 + jh * 64, 64)
                P.op("PE", lambda e, mt=mt, xs=xs, jh=jh: e.matmul(
                    k.psum[by][:, jh * 64:(jh + 1) * 64], lhsT=mt.ap, rhs=xs.ap, start=True, stop=False),
                    reads=[mt.full(), xs], writes=[PSV(k, by)])
                P.op("PE", lambda e, xs=xs, jh=jh: e.matmul(
                    k.psum[by][:, jh * 64:(jh + 1) * 64], lhsT=DGD.ap[:, jh, :], rhs=xs.ap, start=False, stop=True),
                    reads=[DGD.full(), xs], writes=[PSV(k, by)])
            bo = ps_bank(k)
            P.op("PE", lambda e: e.matmul(k.psum[bo][:, 0:256], lhsT=cv.ap, rhs=sl.ap, start=True, stop=False),
                 reads=[cv, sl], writes=[PSV(k, bo)])
            P.op("PE", lambda e: e.matmul(k.psum[bo][:, 0:256], lhsT=cv.ap, rhs=sbn.ap, start=False, stop=True),
                 reads=[cv, sbn.full()], writes=[PSV(k, bo)])
            P.op("DVE", lambda e: e.tensor_tensor(
                out=t1.ap.rearrange("p (h q) -> p h q", h=4),
                in0=k.psum[bo][:, 0:256].rearrange("p (h q) -> p h q", h=4),
                in1=QTM.ap[:, n, 4 * g:4 * g + 4].unsqueeze(2).to_broadcast([128, 4, 64]), op=ALU.mult),
                reads=[PSV(k, bo), QTM.full()], writes=[t1.full()])
            P.op("DVE", lambda e: e.tensor_tensor(out=yf.ap, in0=k.psum[by][:, 0:256], in1=t1.ap, op=ALU.add),
                 reads=[PSV(k, by), t1.full()], writes=[yf.full()])
            P.op("DVE", lambda e: e.tensor_tensor(out=ygb.ap, in0=yf.ap, in1=g2.ap, op=ALU.mult),
                 reads=[yf.full(), g2.full()], writes=[ygb.full()])
            sq = SSQ.v(SSQ.ap[:, n, g:g + 1], n * 8 + g, 1)
            P.op("ACT", lambda e: e.activation(out=t1.ap, in_=ygb.ap, func=AF.Square, accum_out=sq.ap),
                 reads=[ygb.full()], writes=[t1.full(), sq])

        def s_stage3(n, g=g):
            ygb = YGBs[n % NB3]
            b3 = ps_bank(k)
            pb = k.psum[b3][:, :].bitcast(BF16)
            for c in range(2):
                P.op("PE", lambda e, c=c: e.transpose(
                    out=pb[:, c * 128:(c + 1) * 128], in_=ygb.ap[:, c * 128:(c + 1) * 128], identity=k.IDB.ap),
                    reads=[ygb.full(), k.IDB.full()], writes=[PSV(k, b3)])
            yv = YT.vs(YT.ap[:, :, n * 128:(n + 1) * 128], [(c * T + n * 128, 128) for c in range(2)])
            P.op("ACT", lambda e: e.activation(
                out=yv.ap, in_=pb[:, 0:256].rearrange("p (a b) -> p a b", a=2), func=AF.Copy),
                reads=[PSV(k, b3)], writes=[yv])

        for s_ in range(NT + 3):
            if s_ < NT:
                s_stage1(s_)
            if 0 <= s_ - 2 < NT:
                s_stage2(s_ - 2)
            if 0 <= s_ - 3 < NT:
                s_stage3(s_ - 3)
        ytv = V(None, [("D", f"ytd{i}", g)])
        for c in range(2):
            P.dma("SP", ytD.ap()[:, :, (2 * g + c) * 128:(2 * g + c + 1) * 128].rearrange("n p t -> p n t"),
                  YT.ap[:, c, :].rearrange("p (n t) -> p n t", n=NT), reads=[YT.full()], writes=[ytv])
    A.release(m_layer)
    WOM = Tile(A, [16, D], BF16)
    YTt = [Tile(A, [16, 128], BF16) for _ in range(2)]
    WST = [Tile(A, [D], F32) for _ in range(2)]
    SNF = Tile(A, [16], F32)
    P.dma("SP", SNF.ap, k.w(f"snf{j}")[:, :], writes=[SNF.full()])
    for c in range(16):
        wv = WOM.v(WOM.ap[:, c, :], c * D, D)
        ws = WST[c % 2]
        P.dma("SP", ws.ap, wo[c, :, :], writes=[ws.full()])
        P.op("DVE", lambda e, wv=wv, ws=ws, c=c: e.tensor_scalar(
            out=wv.ap, in0=ws.ap, scalar1=SNF.ap[:, c:c + 1], scalar2=0.5, op0=ALU.mult, op1=ALU.mult),
            reads=[ws.full(), SNF.full()], writes=[wv])
    rsf = k.RS.full()
    P.op("DVE", lambda e: e.tensor_reduce(out=k.RS.ap, in_=SSQ.ap, axis=mybir.AxisListType.X, op=ALU.add),
         reads=[SSQ.full()], writes=[rsf])
    P.op("DVE", lambda e: e.tensor_scalar(out=k.RS.ap, in0=k.RS.ap, scalar1=0.25 / 2048, scalar2=EPS,
                                           op0=ALU.mult, op1=ALU.add), reads=[rsf], writes=[rsf])
    P.op("DVE", lambda e: e.reciprocal(out=k.RS.ap, in_=k.RS.ap), reads=[rsf], writes=[rsf])
    P.op("ACT", lambda e: e.activation(out=k.RS.ap, in_=k.RS.ap, func=AF.Sqrt), reads=[rsf], writes=[rsf])
    ytall = [V(None, [("D", f"ytd{i}", g)]) for g in range(8)]
    for n in range(NT):
        yt = YTt[n % 2]
        P.dma("SP", yt.ap.rearrange("p a b -> p (a b)"), ytD[n, :, :], reads=ytall, writes=[yt.full()])
        for nb in range(2):
            b = ps_bank(k)
            for c in range(16):
                wv = WOM.v(WOM.ap[:, c, nb * 512:(nb + 1) * 512], c * D + nb * 512, 512)
                P.op("PE", lambda e, yt=yt, wv=wv, b=b, c=c: e.matmul(
                    k.psum[b][:, :], lhsT=yt.ap[:, c, :], rhs=wv.ap, start=(c == 0), stop=(c == 15)),
                    reads=[yt.full(), wv], writes=[PSV(k, b)])
            hv = hview(k, n, nb * 512, 512)
            P.op("DVE", lambda e, hv=hv, b=b, n=n: e.scalar_tensor_tensor(
                out=hv.ap, in0=k.psum[b][:, :], scalar=k.RS.ap[:, n:n + 1], in1=hv.ap, op0=ALU.mult, op1=ALU.add),
                reads=[PSV(k, b), hv, rsf], writes=[hv])


def emit_output(k, cfg, bcp_d, out_d):
    P, A = k.P, k.A
    outs = []
    if not cfg.get("final_norm", True):
        for j in range(NT):
            hv = hview(k, j)
            outs.append(P.dma("SP", out_d[j * 128:(j + 1) * 128, :], hv.ap, reads=[hv],
                              writes=[V(None, [("D", "out", j)])]))
    else:
        FN = Tile(A, [D], F32)
        OB = [Tile(A, [D], F32) for _ in range(2)]
        P.dma("SP", FN.ap, bcp_d[:, 0:1024], writes=[FN.full()])
        SS, RS = k.SS, k.RS
        ssf, rsf = SS.full(), RS.full()
        junk = k.JUNK.full()
        for j in range(NT):
            hv = hview(k, j)
            P.op("ACT", lambda e, hv=hv, j=j: e.activation(
                out=k.JUNK.ap, in_=hv.ap, func=AF.Square, accum_out=SS.ap[:, j:j + 1]),
                reads=[hv], writes=[junk, ssf])
        P.op("DVE", lambda e: e.tensor_scalar(out=RS.ap, in0=SS.ap, scalar1=1.0 / D, scalar2=EPS,
                                               op0=ALU.mult, op1=ALU.add), reads=[ssf], writes=[rsf])
        P.op("DVE", lambda e: e.reciprocal(out=RS.ap, in_=RS.ap), reads=[rsf], writes=[rsf])
        P.op("ACT", lambda e: e.activation(out=RS.ap, in_=RS.ap, func=AF.Sqrt), reads=[rsf], writes=[rsf])
        for j in range(NT):
            hv = hview(k, j)
            ob = OB[j % 2]
            P.op("DVE", lambda e, hv=hv, ob=ob, j=j: e.scalar_tensor_tensor(
                out=ob.ap, in0=hv.ap, scalar=RS.ap[:, j:j + 1], in1=FN.ap, op0=ALU.mult, op1=ALU.mult),
                reads=[hv, rsf, FN.full()], writes=[ob.full()])
            outs.append(P.dma("SP", out_d[j * 128:(j + 1) * 128, :], ob.ap, reads=[ob.full()],
                              writes=[V(None, [("D", "out", j)])]))
    fin = Op("SP", lambda e: e.nop())
    fin.seq = len(P.ops["SP"])
    for o in outs:
        P._need(fin, o)
    P.ops["SP"].append(fin)


FULL_CFG = dict(sublayers=[("mix", 0), ("ffn", 0), ("mix", 1), ("ffn", 1),
                           ("mix", 2), ("ffn", 2), ("mix", 3), ("ffn", 3)], final_norm=True)


def run(inputs, cfg):
    nc = build(cfg)
    maps = prep_inputs(inputs, set(nc.used_inputs))
    maps = [{kk: m[kk] for kk in nc.used_inputs} for m in maps]
    res = run_bass_kernel_spmd(nc, maps, core_ids=list(range(8)))
    out = np.zeros((4, 4096, D), np.float32)
    for c in range(8):
        b, half = c // 2, c % 2
        out[b, half * T:(half + 1) * T] = res.results[c]["out"]
    return out


def kernel(**inputs):
    return run(inputs, FULL_CFG)
```

```python
import contextlib
import numpy as np
import concourse.bass as bass
import concourse.mybir as mybir
from concourse.bass_utils import run_bass_kernel_spmd

F32 = mybir.dt.float32
BF16 = mybir.dt.bfloat16
U8 = mybir.dt.uint8
AF = mybir.ActivationFunctionType
ALU = mybir.AluOpType

T = 2048
NT = 16
D = 1024
KC = 8
FH = 2816
EPS = 1e-6
CELL = 256
ARENA_BYTES = 206 * 1024
PAIRS = [[0, 1], [2, 3], [4, 5], [6, 7]]
SAME_ENG_SYNC = True


class V:
    __slots__ = ("ap", "cells")

    def __init__(self, ap, cells):
        self.ap = ap
        self.cells = tuple(cells)


def scells(off, nbytes):
    return [("S", c) for c in range(off // CELL, (off + nbytes - 1) // CELL + 1)]


class Op:
    __slots__ = ("eng", "fn", "waits", "signal", "sigval", "seq", "dma", "dsem", "dval", "inc")

    def __init__(self, eng, fn):
        self.eng = eng
        self.fn = fn
        self.waits = []
        self.signal = False
        self.sigval = None
        self.seq = None
        self.dma = False
        self.dsem = None
        self.dval = None
        self.inc = 1


class Prog:
    ENGS = ("PE", "ACT", "DVE", "POOL", "SP")
    NDSEM = 8

    def __init__(self):
        self.ops = {e: [] for e in self.ENGS}
        self.cellw = {}
        self.cellr = {}
        self.waited_eng = {e: {} for e in self.ENGS}
        self.waited_dma = {e: {} for e in self.ENGS}
        self.dma_count = {"SP": 0, "POOL": 0, "ACT": 0}
        self.dma_hist = {"SP": [], "POOL": [], "ACT": []}

    def _need(self, op, dep):
        if dep is op:
            return
        e = op.eng
        if dep.dma:
            key = (dep.eng, dep.dsem)
            if self.waited_dma[e].get(key, 0) >= dep.dval:
                return
            self.waited_dma[e][key] = dep.dval
            op.waits.append(dep)
        else:
            if dep.eng == e and not op.dma:
                if e == "PE" or not SAME_ENG_SYNC:
                    return
            if self.waited_eng[e].get(dep.eng, -1) >= dep.seq:
                return
            self.waited_eng[e][dep.eng] = dep.seq
            dep.signal = True
            op.waits.append(dep)

    def _track(self, op, reads, writes):
        deps = []
        for v in reads:
            for c in v.cells:
                w = self.cellw.get(c)
                if w is not None:
                    deps.append(w)
        for v in writes:
            for c in v.cells:
                w = self.cellw.get(c)
                if w is not None:
                    deps.append(w)
                deps.extend(self.cellr.get(c, ()))
        seen = set()
        for d in deps:
            if id(d) in seen:
                continue
            seen.add(id(d))
            self._need(op, d)
        for v in reads:
            for c in v.cells:
                self.cellr.setdefault(c, []).append(op)
        for v in writes:
            for c in v.cells:
                self.cellw[c] = op
                self.cellr[c] = []

    def op(self, eng, fn, reads=(), writes=()):
        o = Op(eng, fn)
        o.seq = len(self.ops[eng])
        self._track(o, reads, writes)
        self.ops[eng].append(o)
        return o

    def dma(self, queue, out_ap, in_ap, reads=(), writes=(), **kw):
        o = Op(queue, lambda e: e.dma_start(out=out_ap, in_=in_ap, **kw))
        o.dma = True
        o.inc = 16
        i = self.dma_count[queue]
        self.dma_count[queue] = i + 1
        o.dsem = i % self.NDSEM
        o.dval = 16 * (i // self.NDSEM + 1)
        o.seq = len(self.ops[queue])
        hist = self.dma_hist[queue]
        if i >= self.NDSEM:
            self._need(o, hist[i - self.NDSEM])
        hist.append(o)
        self._track(o, reads, writes)
        self.ops[queue].append(o)
        return o

    def coll(self, fn, reads=(), writes=()):
        o = Op("POOL", fn)
        o.dma = True
        o.inc = 1
        self.ncoll = getattr(self, "ncoll", 0) + 1
        o.dsem = "CC"
        o.dval = self.ncoll
        o.seq = len(self.ops["POOL"])
        self._track(o, reads, writes)
        self.ops["POOL"].append(o)
        return o

    def finalize(self):
        for e in self.ENGS:
            n = 0
            for o in self.ops[e]:
                if o.dma:
                    continue
                if o.signal:
                    n += 1
                    o.sigval = n

    def emit(self, eng_name, e, esem, dsem, ccsem):
        for o in self.ops[eng_name]:
            for d in o.waits:
                if d.dma:
                    if d.dsem == "CC":
                        e.wait_ge(ccsem, d.dval)
                    else:
                        e.wait_ge(dsem[d.eng][d.dsem], d.dval)
                else:
                    e.wait_ge(esem[d.eng], d.sigval)
            inst = o.fn(e)
            if o.dma:
                if o.dsem == "CC":
                    inst.then_inc(ccsem)
                else:
                    inst.then_inc(dsem[eng_name][o.dsem], 16)
            elif o.signal:
                inst.then_inc(esem[eng_name], 1)


class Arena:
    def __init__(self, ap):
        self.ap = ap
        self.top = 0

    def alloc(self, nbytes, align=256):
        off = (self.top + align - 1) // align * align
        assert off + nbytes <= ARENA_BYTES, ("arena overflow", off, nbytes)
        self.top = off + nbytes
        self.high = max(getattr(self, "high", 0), self.top)
        return off

    def mark(self):
        return self.top

    def release(self, m):
        self.top = m


class Tile:
    def __init__(self, arena, dims, dtype, name="", off=None):
        self.dims = list(dims)
        self.dtype = dtype
        self.esz = 4 if dtype == F32 else 2
        n = int(np.prod(dims))
        self.nbytes = n * self.esz
        self.off = arena.alloc(self.nbytes) if off is None else off
        ap = arena.ap[:, self.off:self.off + self.nbytes].bitcast(dtype)
        if len(dims) == 2:
            ap = ap.rearrange("p (a b) -> p a b", a=dims[0])
        elif len(dims) == 3:
            ap = ap.rearrange("p (a b c) -> p a b c", a=dims[0], b=dims[1])
        self.ap = ap

    def full(self):
        return V(self.ap, scells(self.off, self.nbytes))

    def v(self, ap, start_elem, nelem):
        return V(ap, scells(self.off + start_elem * self.esz, nelem * self.esz))

    def vs(self, ap, segs):
        cells = []
        for (s, n) in segs:
            cells.extend(scells(self.off + s * self.esz, n * self.esz))
        return V(ap, cells)


class K:
    pass


def build(cfg):
    nc = bass.Bass("TRN2", target_bir_lowering=False)
    P = Prog()
    global LASTP
    LASTP = P
    k = K()
    k.nc, k.P = nc, P
    dr = {}

    def dram_in(name, shape, dtype=F32):
        dr[name] = nc.dram_tensor(name, list(shape), dtype, kind="ExternalInput")
        return dr[name]

    x_d = dram_in("x", [T, D])
    mem_d = dram_in("mem", [256, D])
    par_d = dram_in("par", [128, NPAR])
    bcp_d = dram_in("bcp", [128, NBCP])
    def wget(nm):
        if nm not in dr:
            dram_in(nm, WEIGHT_SHAPES[nm])
        return dr[nm]
    k.w = wget
    out_d = nc.dram_tensor("out", [T, D], F32, kind="ExternalOutput")
    k.dr = dr

    stack = contextlib.ExitStack()
    with stack:
        arena_t = stack.enter_context(nc.sbuf_tensor("arena", [128, ARENA_BYTES], U8))
        psum = [stack.enter_context(nc.psum_tensor(f"ps{i}", [128, 512], F32)) for i in range(8)]
        esem = {e: stack.enter_context(nc.semaphore("s" + e)) for e in Prog.ENGS}
        dsem = {q: [stack.enter_context(nc.semaphore(f"d{q}{i}")) for i in range(Prog.NDSEM)]
                for q in ("SP", "POOL", "ACT")}
        ccsem = stack.enter_context(nc.semaphore("cc"))
        block = stack.enter_context(nc.Block())

        A = Arena(arena_t)
        k.A = A
        k.psum = psum
        k.ps_i = 0

        emit_program(k, cfg, x_d, mem_d, par_d, bcp_d, out_d)
        P.finalize()
        nc.used_inputs = list(dr.keys())

        @block.tensor
        def _(e):
            P.emit("PE", e, esem, dsem, ccsem)

        @block.scalar
        def _(e):
            P.emit("ACT", e, esem, dsem, ccsem)

        @block.vector
        def _(e):
            P.emit("DVE", e, esem, dsem, ccsem)

        @block.gpsimd
        def _(e):
            P.emit("POOL", e, esem, dsem, ccsem)

        @block.sync
        def _(e):
            P.emit("SP", e, esem, dsem, ccsem)
    return nc


PAR = {}
_c = 0
for _i in range(4):
    for _nm in ("mixn", "memn", "ffnn"):
        PAR[(_nm, _i)] = _c
        _c += 8
for _j in range(2):
    PAR[("bgate", _j)] = _c
    _c += 4
PAR["flag"] = _c
_c += 1
NPAR = _c
BCP = {"fnorm": 0}
_c = 1024
for _j in range(2):
    BCP[("hnorm", _j)] = _c
    _c += 256
NBCP = _c

WEIGHT_SHAPES = {}
for _i in range(4):
    WEIGHT_SHAPES[f"ffn_in{_i}"] = [44, 128, 1024]
    WEIGHT_SHAPES[f"ffn_out{_i}"] = [22, 128, 1024]
    WEIGHT_SHAPES[f"kvk{_i}"] = [8, 128, 1024]
    WEIGHT_SHAPES[f"kvv{_i}"] = [2, 128, 4096]
    WEIGHT_SHAPES[f"xq{_i}"] = [8, 128, 1024]
    WEIGHT_SHAPES[f"wo{_i}"] = [16 if _i % 2 == 0 else 24, 128, 1024]
for _j in range(2):
    WEIGHT_SHAPES[f"gq{_j}"] = [4, 128, 1024]
    WEIGHT_SHAPES[f"gk{_j}"] = [4, 128, 1024]
    WEIGHT_SHAPES[f"gv{_j}"] = [4, 128, 2048]
    WEIGHT_SHAPES[f"gg{_j}"] = [4, 128, 2048]
    WEIGHT_SHAPES[f"glr{_j}"] = [128, 128]
    WEIGHT_SHAPES[f"gw2{_j}"] = [16, 512]
    WEIGHT_SHAPES[f"sz{_j}"] = [8, 128, 2048]
    WEIGHT_SHAPES[f"sxbc{_j}"] = [32, 128, 1024]
    WEIGHT_SHAPES[f"sdt{_j}"] = [128, 1024]
    WEIGHT_SHAPES[f"scv{_j}"] = [128, 164]
    WEIGHT_SHAPES[f"sdk{_j}"] = [128, 2048]
    WEIGHT_SHAPES[f"snf{_j}"] = [128, 16]


def blk_layout(w, ncol=128):
    K_, C = w.shape
    nb = C // ncol
    a = w.reshape(K_ // 128, 128, nb, ncol).transpose(2, 1, 0, 3)
    return np.ascontiguousarray(a.reshape(nb, 128, (K_ // 128) * ncol))


def fm_vec(v):
    return np.ascontiguousarray(v.reshape(-1, 128).T)


def prep_inputs(inp, used=None):
    f = lambda a: np.asarray(a, dtype=np.float32)
    need = lambda nm: used is None or nm in used
    shared = {}
    par = np.zeros((128, NPAR), np.float32)
    for i in range(4):
        par[:, PAR[("mixn", i)]:PAR[("mixn", i)] + 8] = fm_vec(f(inp["mix_norm"])[i])
        par[:, PAR[("memn", i)]:PAR[("memn", i)] + 8] = fm_vec(f(inp["mem_norm"])[i])
        par[:, PAR[("ffnn", i)]:PAR[("ffnn", i)] + 8] = fm_vec(f(inp["ffn_norm"])[i])
    for j in range(2):
        par[:, PAR[("bgate", j)]:PAR[("bgate", j)] + 4] = fm_vec(f(inp["gla_b_gate"])[j])
    bcp = np.zeros((128, NBCP), np.float32)
    bcp[:, 0:1024] = f(inp["final_norm"])[None, :]
    for j in range(2):
        bcp[:, BCP[("hnorm", j)]:BCP[("hnorm", j)] + 256] = f(inp["gla_head_norm"])[j][None, :]
    shared["bcp"] = bcp
    for i in range(4):
        if need(f"ffn_in{i}"):
            shared[f"ffn_in{i}"] = blk_layout(f(inp["ffn_w_in"])[i])
            shared[f"ffn_out{i}"] = np.ascontiguousarray(f(inp["ffn_w_out"])[i].reshape(22, 128, 1024))
        if need(f"kvk{i}"):
            wkv = f(inp["w_mem_kv"])[i]
            shared[f"kvk{i}"] = blk_layout(wkv[:, 0:1024])
            shared[f"kvv{i}"] = blk_layout(wkv[:, 1024:2048], 512)
            j = i // 2
            if i % 2 == 0:
                w = f(inp["gla_w_in"])[j]
                shared[f"xq{i}"] = blk_layout(w[:, 3088:4112])
                shared[f"wo{i}"] = np.ascontiguousarray(f(inp["gla_w_out"])[j].reshape(16, 128, 1024))
                shared[f"gq{j}"] = blk_layout(w[:, 0:512])
                shared[f"gk{j}"] = blk_layout(w[:, 512:1024])
                shared[f"gv{j}"] = blk_layout(w[:, 1024:2048], 256)
                shared[f"gg{j}"] = blk_layout(w[:, 2048:3072], 256)
                shared[f"glr{j}"] = blk_layout(w[:, 3072:3088], 16)[0]
                shared[f"gw2{j}"] = np.ascontiguousarray(f(inp["gla_w_gate2"])[j])
            else:
                w = f(inp["ssd_w_in"])[j]
                shared[f"xq{i}"] = blk_layout(w[:, 6176:7200])
                shared[f"wo{i}"] = np.ascontiguousarray(f(inp["ssd_w_out"])[j].reshape(24, 128, 1024))
                shared[f"sz{j}"] = blk_layout(w[:, 0:2048], 256)
                shared[f"sxbc{j}"] = blk_layout(w[:, 2048:6144])
                shared[f"sdt{j}"] = blk_layout(np.tile(w[:, 6144:6176], (1, 4)))[0]
                scv = np.zeros((128, 164), np.float32)
                cw = f(inp["ssd_conv_w"])[j]
                scv[:, 0:128] = cw.reshape(4, 32, 128).transpose(2, 1, 0).reshape(128, 128)
                scv[:, 128:160] = fm_vec(f(inp["ssd_conv_b"])[j])
                scv[:, 160] = np.tile(f(inp["ssd_dt_bias"])[j], 4)
                scv[:, 161] = np.tile(f(inp["ssd_a_log"])[j], 4)
                shared[f"scv{j}"] = scv
                shared[f"sdk{j}"] = np.ascontiguousarray(np.broadcast_to(np.repeat(f(inp["ssd_d"])[j], 64)[None, :], (128, 2048)))
                shared[f"snf{j}"] = fm_vec(f(inp["ssd_norm"])[j])
    x = f(inp["x"])
    mem = f(inp["mem"])
    maps = []
    for c in range(8):
        b, half = c // 2, c % 2
        m = dict(shared)
        p = par.copy()
        p[:, PAR["flag"]] = float(half)
        m["par"] = p
        m["x"] = np.ascontiguousarray(x[b, half * T:(half + 1) * T])
        m["mem"] = np.ascontiguousarray(mem[b])
        maps.append(m)
    return maps


def ps_bank(k):
    i = k.ps_i
    k.ps_i = (i + 1) % 8
    return i


def PSV(k, i, ap=None):
    return V(k.psum[i][:, :] if ap is None else ap, [("P", i)])


def emit_program(k, cfg, x_d, mem_d, par_d, bcp_d, out_d):
    P, A, nc = k.P, k.A, k.nc
    k.H = Tile(A, [NT, D], F32)
    k.PAR = Tile(A, [NPAR], F32)
    k.IDB = Tile(A, [128], BF16)
    k.IDF = Tile(A, [128], F32)
    k.RS = Tile(A, [NT], F32)
    k.SS = Tile(A, [NT], F32)
    k.XN = Tile(A, [KC, T], BF16)
    k.JUNK = Tile(A, [D], BF16)
    k.HS = Tile(A, [D], BF16)
    H = k.H

    for j in range(NT):
        hv = H.v(H.ap[:, j, :], j * D, D)
        P.dma("SP", hv.ap, x_d[j * 128:(j + 1) * 128, :], writes=[hv])
    P.dma("SP", k.PAR.ap, par_d[:, :], writes=[k.PAR.full()])

    for idt in (k.IDB, k.IDF):
        fv = idt.full()
        P.op("POOL", lambda e, a=idt.ap: e.memset(a, 0.0), writes=[fv])
        P.op("POOL", lambda e, a=idt.ap: e.affine_select(
            out=a, in_=a, pattern=[[-1, 128]], compare_op=ALU.not_equal, fill=1.0,
            base=0, channel_multiplier=1), reads=[fv], writes=[fv])

    k.ONE1 = Tile(A, [1], F32)
    k.LNQ = Tile(A, [1], F32)
    k.EPSC = Tile(A, [1], F32)
    P.op("POOL", lambda e: e.memset(k.EPSC.ap, EPS), writes=[k.EPSC.full()])
    P.op("POOL", lambda e: e.memset(k.ONE1.ap, 1.0), writes=[k.ONE1.full()])
    P.op("POOL", lambda e: e.memset(k.LNQ.ap, float(np.log(128.0 ** -0.5))), writes=[k.LNQ.full()])
    emit_consts(k)
    m0 = A.mark()
    for sl in cfg["sublayers"]:
        kind, i = sl
        A.release(m0)
        if kind == "ffn":
            emit_ffn(k, i)
        elif kind == "mix":
            emit_mixer(k, i, mem_d)
    A.release(m0)
    emit_output(k, cfg, bcp_d, out_d)


def hview(k, j, c0=0, n=D):
    H = k.H
    return H.v(H.ap[:, j, c0:c0 + n], j * D + c0, n)


def emit_norm_T(k, srcs, dst, dst_T, wcol):
    P = k.P
    SS, RS = k.SS, k.RS
    n = len(srcs)
    ssf, rsf = SS.full(), RS.full()
    junk = k.JUNK.full()
    for j, hv in enumerate(srcs):
        P.op("ACT", lambda e, hv=hv, j=j: e.activation(
            out=k.JUNK.ap, in_=hv.ap, func=AF.Square, accum_out=SS.ap[:, j:j + 1]),
            reads=[hv], writes=[junk, ssf])
    P.op("DVE", lambda e: e.tensor_scalar(out=RS.ap[:, 0:n], in0=SS.ap[:, 0:n], scalar1=1.0 / D, scalar2=EPS,
                                           op0=ALU.mult, op1=ALU.add), reads=[ssf], writes=[rsf])
    P.op("DVE", lambda e: e.reciprocal(out=RS.ap[:, 0:n], in_=RS.ap[:, 0:n]), reads=[rsf], writes=[rsf])
    P.op("ACT", lambda e: e.activation(out=RS.ap[:, 0:n], in_=RS.ap[:, 0:n], func=AF.Sqrt), reads=[rsf], writes=[rsf])
    hsf = k.HS.full()
    wv = k.PAR.v(k.PAR.ap[:, wcol:wcol + 8], wcol, 8)
    for j, hv in enumerate(srcs):
        P.op("ACT", lambda e, hv=hv, j=j: e.activation(
            out=k.HS.ap, in_=hv.ap, func=AF.Copy, scale=RS.ap[:, j:j + 1]),
            reads=[hv, rsf], writes=[hsf])
        b = ps_bank(k)
        pb = k.psum[b][:, :].bitcast(BF16)
        pv = PSV(k, b)
        for kc in range(KC):
            P.op("PE", lambda e, kc=kc, pb=pb: e.transpose(
                out=pb[:, kc * 128:(kc + 1) * 128], in_=k.HS.ap[:, kc * 128:(kc + 1) * 128],
                identity=k.IDB.ap), reads=[hsf, k.IDB.full()], writes=[pv])
        xv = dst.vs(dst.ap[:, :, j * 128:(j + 1) * 128], [(kc * dst_T + j * 128, 128) for kc in range(KC)])
        P.op("DVE", lambda e, pb=pb, xv=xv, wv=wv: e.tensor_tensor(
            out=xv.ap, in0=pb.rearrange("p (a b) -> p a b", a=KC),
            in1=wv.ap.unsqueeze(2).to_broadcast([128, KC, 128]), op=ALU.mult),
            reads=[pv, wv], writes=[xv])


def emit_rmsnorm_xn(k, wcol):
    emit_norm_T(k, [hview(k, j) for j in range(NT)], k.XN, T, wcol)


def xn_blk(k, kc, t0, n):
    XN = k.XN
    return XN.v(XN.ap[:, kc, t0:t0 + n], kc * T + t0, n)


def emit_ffn(k, i):
    P, A, dr = k.P, k.A, k.dr
    emit_rmsnorm_xn(k, PAR[("ffnn", i)])
    NG = 11
    ACTT = Tile(A, [NG, T], BF16)
    WO = Tile(A, [NG, D], BF16)
    WR = [Tile(A, [KC, 128], BF16) for _ in range(4)]
    SG = [Tile(A, [512], F32) for _ in range(2)]
    win, wout = k.w(f"ffn_in{i}"), k.w(f"ffn_out{i}")
    wr_i = 0
    sg_i = 0
    for grp in range(2):
        for c in range(NG):
            wov = WO.v(WO.ap[:, c, :], c * D, D)
            P.dma("POOL", wov.ap, wout[grp * NG + c, :, :], writes=[wov])
        for c in range(NG):
            blk = grp * NG + c
            wg, wu = WR[wr_i % 4], WR[(wr_i + 1) % 4]
            wr_i += 2
            P.dma("POOL", wg.ap.rearrange("p a b -> p (a b)"), win[blk, :, :], writes=[wg.full()])
            P.dma("POOL", wu.ap.rearrange("p a b -> p (a b)"), win[22 + blk, :, :], writes=[wu.full()])
            for tb in range(4):
                bg, bu = ps_bank(k), ps_bank(k)
                for (w_, b_) in ((wg, bg), (wu, bu)):
                    for kc in range(KC):
                        xv = xn_blk(k, kc, tb * 512, 512)
                        P.op("PE", lambda e, w_=w_, b_=b_, kc=kc, xv=xv: e.matmul(
                            k.psum[b_][:, :], lhsT=w_.ap[:, kc, :], rhs=xv.ap,
                            start=(kc == 0), stop=(kc == KC - 1)),
                            reads=[w_.full(), xv], writes=[PSV(k, b_)])
                sg = SG[sg_i % 2]
                sg_i += 1
                P.op("ACT", lambda e, sg=sg, bg=bg: e.activation(
                    out=sg.ap, in_=k.psum[bg][:, :], func=AF.Silu),
                    reads=[PSV(k, bg)], writes=[sg.full()])
                av = ACTT.v(ACTT.ap[:, c, tb * 512:(tb + 1) * 512], c * T + tb * 512, 512)
                P.op("DVE", lambda e, sg=sg, bu=bu, av=av: e.tensor_tensor(
                    out=av.ap, in0=k.psum[bu][:, :], in1=sg.ap, op=ALU.mult),
                    reads=[PSV(k, bu), sg.full()], writes=[av])
        for j in range(NT):
            for nb in range(2):
                b = ps_bank(k)
                for c in range(NG):
                    av = ACTT.v(ACTT.ap[:, c, j * 128:(j + 1) * 128], c * T + j * 128, 128)
                    wov = WO.v(WO.ap[:, c, nb * 512:(nb + 1) * 512], c * D + nb * 512, 512)
                    P.op("PE", lambda e, av=av, wov=wov, b=b, c=c: e.matmul(
                        k.psum[b][:, :], lhsT=av.ap, rhs=wov.ap, start=(c == 0), stop=(c == NG - 1)),
                        reads=[av, wov], writes=[PSV(k, b)])
                hv = hview(k, j, nb * 512, 512)
                P.op("DVE", lambda e, hv=hv, b=b: e.tensor_tensor(
                    out=hv.ap, in0=k.psum[b][:, :], in1=hv.ap, op=ALU.add),
                    reads=[PSV(k, b), hv], writes=[hv])


def load_w(k, tile, src, queue="POOL"):
    ap = tile.ap
    if len(tile.dims) == 2:
        ap = ap.rearrange("p a b -> p (a b)")
    elif len(tile.dims) == 3:
        ap = ap.rearrange("p a b c -> p (a b c)")
    return k.P.dma(queue, ap, src, writes=[tile.full()], max_dma_last_dim=4096)


def inproj_fm(k, wt, t0, n, M=128, wcol0=0):
    b = ps_bank(k)
    for kc in range(KC):
        xv = xn_blk(k, kc, t0, n)
        k.P.op("PE", lambda e, kc=kc, xv=xv, b=b: e.matmul(
            k.psum[b][0:M, 0:n], lhsT=wt.ap[:, kc, wcol0:wcol0 + M], rhs=xv.ap,
            start=(kc == 0), stop=(kc == KC - 1)), reads=[wt.full(), xv], writes=[PSV(k, b)])
    return b


def inproj_tm(k, wt, j, ncols, src=None, srcT=T):
    b = ps_bank(k)
    src = src or k.XN
    for kc in range(KC):
        xv = src.v(src.ap[:, kc, j * 128:(j + 1) * 128], kc * srcT + j * 128, 128)
        k.P.op("PE", lambda e, kc=kc, xv=xv, b=b: e.matmul(
            k.psum[b][:, 0:ncols], lhsT=xv.ap, rhs=wt.ap[:, kc, 0:ncols],
            start=(kc == 0), stop=(kc == KC - 1)), reads=[wt.full(), xv], writes=[PSV(k, b)])
    return b


def emit_consts(k):
    P, A = k.P, k.A
    k.MASKU = Tile(A, [128], F32)
    k.ONESB = Tile(A, [128], BF16)
    fv = k.MASKU.full()
    P.op("POOL", lambda e: e.memset(k.MASKU.ap, 1.0), writes=[fv])
    P.op("POOL", lambda e: e.affine_select(
        out=k.MASKU.ap, in_=k.MASKU.ap, pattern=[[1, 128]], compare_op=ALU.is_ge, fill=0.0,
        base=0, channel_multiplier=-1), reads=[fv], writes=[fv])
    P.op("POOL", lambda e: e.memset(k.ONESB.ap, 1.0), writes=[k.ONESB.full()])
    k.FLAG = k.PAR.v(k.PAR.ap[:, PAR["flag"]:PAR["flag"] + 1], PAR["flag"], 1)


def emit_mixer(k, i, mem_d):
    P, A = k.P, k.A
    emit_rmsnorm_xn(k, PAR[("mixn", i)])
    m = A.mark()
    emit_xattn(k, i, mem_d)
    A.release(m)
    if i % 2 == 0:
        emit_gla(k, i)
    else:
        emit_ssd(k, i)
    A.release(m)


def emit_xattn(k, i, mem_d):
    P, A = k.P, k.A
    mixdim = 1024 if i % 2 == 0 else 2048
    MEMT = Tile(A, [2, D], F32)
    MN = Tile(A, [KC, 256], BF16)
    KmT = Tile(A, [8, 256], BF16)
    Vm = Tile(A, [2, 1024], BF16)
    WR = [Tile(A, [KC, 128], BF16) for _ in range(3)]
    m1 = A.mark()
    WV = Tile(A, [KC, 512], BF16)
    srcs = []
    for mt in range(2):
        mv = MEMT.v(MEMT.ap[:, mt, :], mt * D, D)
        P.dma("SP", mv.ap, mem_d[mt * 128:(mt + 1) * 128, :], writes=[mv])
        srcs.append(mv)
    emit_norm_T(k, srcs, MN, 256, PAR[("memn", i)])
    kvk, kvv = k.w(f"kvk{i}"), k.w(f"kvv{i}")
    wi = 0
    for blk in range(8):
        w = WR[wi % 3]
        wi += 1
        load_w(k, w, kvk[blk, :, :])
        b = ps_bank(k)
        for kc in range(KC):
            mv = MN.v(MN.ap[:, kc, :], kc * 256, 256)
            P.op("PE", lambda e, kc=kc, mv=mv, b=b, w=w: e.matmul(
                k.psum[b][:, 0:256], lhsT=w.ap[:, kc, :], rhs=mv.ap, start=(kc == 0), stop=(kc == KC - 1)),
                reads=[w.full(), mv], writes=[PSV(k, b)])
        kv = KmT.v(KmT.ap[:, blk, :], blk * 256, 256)
        P.op("ACT", lambda e, kv=kv, b=b: e.activation(out=kv.ap, in_=k.psum[b][:, 0:256], func=AF.Copy,
                                                         scale=1.0 / 16.0), reads=[PSV(k, b)], writes=[kv])
    for nb in range(2):
        load_w(k, WV, kvv[nb, :, :])
        for mt in range(2):
            b = inproj_tm(k, WV, mt, 512, src=MN, srcT=256)
            vv = Vm.v(Vm.ap[:, mt, nb * 512:(nb + 1) * 512], mt * 1024 + nb * 512, 512)
            P.op("ACT", lambda e, vv=vv, b=b: e.activation(out=vv.ap, in_=k.psum[b][:, :], func=AF.Copy),
                 reads=[PSV(k, b)], writes=[vv])
    A.release(m1)
    XQT = Tile(A, [2, T], BF16)
    XAT = Tile(A, [8, T], BF16)
    ET = [Tile(A, [512], BF16) for _ in range(2)]
    RD = Tile(A, [512], F32)
    WOX = Tile(A, [8, D], BF16)
    xqw, wo = k.w(f"xq{i}"), k.w(f"wo{i}")
    for c in range(8):
        wv = WOX.v(WOX.ap[:, c, :], c * D, D)
        P.dma("POOL", wv.ap, wo[mixdim // 128 + c, :, :], writes=[wv], max_dma_last_dim=4096)
    for a in range(4):
        for dc in range(2):
            w = WR[wi % 3]
            wi += 1
            load_w(k, w, xqw[a * 2 + dc, :, :])
            for tb in range(4):
                b = inproj_fm(k, w, tb * 512, 512)
                xv = XQT.v(XQT.ap[:, dc, tb * 512:(tb + 1) * 512], dc * T + tb * 512, 512)
                P.op("ACT", lambda e, xv=xv, b=b: e.activation(out=xv.ap, in_=k.psum[b][:, :], func=AF.Copy),
                     reads=[PSV(k, b)], writes=[xv])
        for tb in range(4):
            for mt in range(2):
                b = ps_bank(k)
                for dc in range(2):
                    kv = KmT.v(KmT.ap[:, a * 2 + dc, mt * 128:(mt + 1) * 128], (a * 2 + dc) * 256 + mt * 128, 128)
                    xv = XQT.v(XQT.ap[:, dc, tb * 512:(tb + 1) * 512], dc * T + tb * 512, 512)
                    P.op("PE", lambda e, kv=kv, xv=xv, b=b, dc=dc: e.matmul(
                        k.psum[b][:, :], lhsT=kv.ap, rhs=xv.ap, start=(dc == 0), stop=(dc == 1)),
                        reads=[kv, xv], writes=[PSV(k, b)])
                P.op("ACT", lambda e, b=b, mt=mt: e.activation(out=ET[mt].ap, in_=k.psum[b][:, :], func=AF.Exp),
                     reads=[PSV(k, b)], writes=[ET[mt].full()])
            b = ps_bank(k)
            for mt in range(2):
                P.op("PE", lambda e, b=b, mt=mt: e.matmul(
                    k.psum[b][:, :], lhsT=k.ONESB.ap, rhs=ET[mt].ap, start=(mt == 0), stop=(mt == 1)),
                    reads=[k.ONESB.full(), ET[mt].full()], writes=[PSV(k, b)])
            P.op("DVE", lambda e, b=b: e.reciprocal(out=RD.ap, in_=k.psum[b][:, :]),
                 reads=[PSV(k, b)], writes=[RD.full()])
            for dc in range(2):
                b = ps_bank(k)
                for mt in range(2):
                    c0 = a * 256 + dc * 128
                    vv = Vm.v(Vm.ap[:, mt, c0:c0 + 128], mt * 1024 + c0, 128)
                    P.op("PE", lambda e, vv=vv, b=b, mt=mt: e.matmul(
                        k.psum[b][:, :], lhsT=vv.ap, rhs=ET[mt].ap, start=(mt == 0), stop=(mt == 1)),
                        reads=[vv, ET[mt].full()], writes=[PSV(k, b)])
                xav = XAT.v(XAT.ap[:, a * 2 + dc, tb * 512:(tb + 1) * 512], (a * 2 + dc) * T + tb * 512, 512)
                P.op("DVE", lambda e, xav=xav, b=b: e.tensor_tensor(
                    out=xav.ap, in0=k.psum[b][:, :], in1=RD.ap, op=ALU.mult),
                    reads=[PSV(k, b), RD.full()], writes=[xav])
    emit_outproj(k, XAT, WOX, 8)


def emit_outproj(k, XT, WO, nk, scale_col=None):
    P = k.P
    for j in range(NT):
        for nb in range(2):
            b = ps_bank(k)
            for c in range(nk):
                av = XT.v(XT.ap[:, c, j * 128:(j + 1) * 128], c * T + j * 128, 128)
                wv = WO.v(WO.ap[:, c, nb * 512:(nb + 1) * 512], c * D + nb * 512, 512)
                P.op("PE", lambda e, av=av, wv=wv, b=b, c=c: e.matmul(
                    k.psum[b][:, :], lhsT=av.ap, rhs=wv.ap, start=(c == 0), stop=(c == nk - 1)),
                    reads=[av, wv], writes=[PSV(k, b)])
            hv = hview(k, j, nb * 512, 512)
            if scale_col is None:
                P.op("DVE", lambda e, hv=hv, b=b: e.tensor_tensor(
                    out=hv.ap, in0=k.psum[b][:, :], in1=hv.ap, op=ALU.add),
                    reads=[PSV(k, b), hv], writes=[hv])
            else:
                sc = scale_col(j)
                P.op("DVE", lambda e, hv=hv, b=b, sc=sc: e.scalar_tensor_tensor(
                    out=hv.ap, in0=k.psum[b][:, :], scalar=sc.ap, in1=hv.ap, op0=ALU.mult, op1=ALU.add),
                    reads=[PSV(k, b), hv, sc], writes=[hv])


def exchange(k, src_v, rows, cols, dst_tile, name):
    P, nc = k.P, k.nc
    ib = nc.dram_tensor(name + "_i", [rows, cols], F32)
    ob = nc.dram_tensor(name + "_o", [2 * rows, cols], F32)
    ci, co = V(None, [("D", name + "_i")]), V(None, [("D", name + "_o")])
    P.dma("POOL", ib[:, :], src_v.ap, reads=[src_v], writes=[ci])
    P.coll(lambda e: e.collective_compute("AllGather", ALU.bypass, replica_groups=PAIRS,
                                          ins=[ib.ap().opt()], outs=[ob.ap().opt()]),
           reads=[ci], writes=[co])
    dv = V(dst_tile.ap[0:rows], dst_tile.full().cells)
    P.dma("POOL", dv.ap, ob[0:rows, :], reads=[co], writes=[dv])
    P.op("DVE", lambda e: e.tensor_scalar(out=dv.ap, in0=dv.ap, scalar1=k.FLAG.ap[0:rows], scalar2=None, op0=ALU.mult),
         reads=[dv, k.FLAG], writes=[dv])


def emit_gla(k, i):
    P, A = k.P, k.A
    j = i // 2
    GLRT = Tile(A, [T], BF16)
    W2 = Tile(A, [512], BF16)
    SPL = Tile(A, [T], F32)
    CS = Tile(A, [T], F32)
    FQ = Tile(A, [512], F32)
    FK = Tile(A, [512], F32)
    FE = Tile(A, [512], F32)
    QT = Tile(A, [T], BF16, off=SPL.off)
    QT2 = Tile(A, [T], BF16, off=SPL.off + T * 2)
    KNT = Tile(A, [T], BF16)
    KET = Tile(A, [T], BF16)
    KE = Tile(A, [NT, 128], BF16)
    Vt = Tile(A, [NT, 256], BF16)
    Gt = Tile(A, [NT, 256], BF16)
    SLOC = Tile(A, [NT, 256], BF16)
    SM = Tile(A, [16 * 6 + 8], F32)
    CSS = SM.ap[:, 0:16]
    CSE = SM.ap[:, 16:32]
    DEC = SM.ap[:, 32:48]
    DCUM = SM.ap[:, 48:64]
    NBG = SM.ap[:, 64:68]
    SSo = SM.ap[:, 68:69]
    RSo = SM.ap[:, 69:70]
    smf = SM.full()
    TTs = [Tile(A, [256], F32) for _ in range(4)]
    MTOKs = [Tile(A, [256], BF16) for _ in range(4)]
    STs = [Tile(A, [128], BF16) for _ in range(4)]
    SRs = [Tile(A, [2], F32) for _ in range(4)]
    MIXT = Tile(A, [2, T], BF16)
    WOH = Tile(A, [2, D], BF16)
    HNB = Tile(A, [256], F32)
    S = Tile(A, [256], F32)
    SB32 = Tile(A, [256], F32)
    SBb = Tile(A, [256], BF16)
    WA = Tile(A, [KC, 128], BF16)
    WB = Tile(A, [KC, 128], BF16)
    WC = Tile(A, [KC, 256], BF16)
    WD = Tile(A, [KC, 256], BF16)
    gq, gk, gv, gg = k.w(f"gq{j}"), k.w(f"gk{j}"), k.w(f"gv{j}"), k.w(f"gg{j}")
    wo = k.w(f"wo{i}")
    bcp = k.dr["bcp"]
    P.dma("SP", HNB.ap, bcp[:, BCP[("hnorm", j)]:BCP[("hnorm", j)] + 256], writes=[HNB.full()])
    bg = PAR[("bgate", j)]
    pv = k.PAR.v(k.PAR.ap[:, bg:bg + 4], bg, 4)
    P.op("DVE", lambda e: e.tensor_scalar(out=NBG, in0=pv.ap, scalar1=-1.0, scalar2=None, op0=ALU.mult),
         reads=[pv], writes=[smf])
    WGL = Tile(A, [KC, 16], BF16)
    load_w(k, WGL, k.w(f"glr{j}")[:, :])
    P.dma("POOL", W2.ap[0:16, :], k.w(f"gw2{j}")[:, :], writes=[W2.full()])
    for tb in range(4):
        b = inproj_fm(k, WGL, tb * 512, 512, M=16)
        gv_ = GLRT.v(GLRT.ap[0:16, tb * 512:(tb + 1) * 512], tb * 512, 512)
        P.op("ACT", lambda e, gv_=gv_, b=b: e.activation(out=gv_.ap, in_=k.psum[b][0:16, :], func=AF.Copy),
             reads=[PSV(k, b)], writes=[gv_])
    for h in range(4):
        for tb in range(4):
            b = ps_bank(k)
            gv_ = GLRT.v(GLRT.ap[0:16, tb * 512:(tb + 1) * 512], tb * 512, 512)
            P.op("PE", lambda e, b=b, gv_=gv_, h=h: e.matmul(
                k.psum[b][:, :], lhsT=W2.ap[0:16, h * 128:(h + 1) * 128], rhs=gv_.ap, start=True, stop=True),
                reads=[W2.full(), gv_], writes=[PSV(k, b)])
            sv = SPL.v(SPL.ap[:, tb * 512:(tb + 1) * 512], tb * 512, 512)
            P.op("ACT", lambda e, b=b, sv=sv, h=h: e.activation(
                out=sv.ap, in_=k.psum[b][:, :], func=AF.Exp, scale=-1.0, bias=NBG[:, h:h + 1]),
                reads=[PSV(k, b), smf], writes=[sv])
        P.op("ACT", lambda e: e.activation(out=SPL.ap, in_=SPL.ap, func=AF.Ln, bias=1.0),
             reads=[SPL.full()], writes=[SPL.full()])
        P.op("DVE", lambda e: e.tensor_tensor_scan(
            out=CS.ap, data0=k.ONE1.ap.to_broadcast([128, T]), data1=SPL.ap, initial=0.0,
            op0=ALU.mult, op1=ALU.add), reads=[SPL.full(), k.ONE1.full()], writes=[CS.full()])
        cs3 = CS.ap.rearrange("p (n c) -> p n c", c=128)
        P.op("DVE", lambda e: e.memset(SM.ap[:, 0:1], 0.0), writes=[smf])
        P.op("DVE", lambda e: e.tensor_copy(out=SM.ap[:, 1:16], in_=cs3[:, 0:15, 127]), reads=[CS.full()], writes=[smf])
        P.op("DVE", lambda e: e.tensor_copy(out=CSE, in_=cs3[:, :, 127]), reads=[CS.full()], writes=[smf])
        P.op("DVE", lambda e: e.tensor_tensor(out=DEC, in0=CSE, in1=CSS, op=ALU.subtract), reads=[smf], writes=[smf])
        P.op("ACT", lambda e: e.activation(out=DEC, in_=DEC, func=AF.Exp, scale=-1.0 / 16), reads=[smf], writes=[smf])
        P.op("ACT", lambda e: e.activation(out=DCUM, in_=CSS, func=AF.Exp, scale=-1.0 / 16), reads=[smf], writes=[smf])
        P.op("DVE", lambda e: e.tensor_tensor(
            out=cs3, in0=cs3, in1=CSS.unsqueeze(2).to_broadcast([128, NT, 128]), op=ALU.subtract),
            reads=[CS.full(), smf], writes=[CS.full()])
        load_w(k, WB, gk[h, :, :])
        for tb in range(4):
            dv = CS.v(CS.ap[:, tb * 512:(tb + 1) * 512], tb * 512, 512)
            P.op("ACT", lambda e, dv=dv: e.activation(out=FK.ap, in_=dv.ap, func=AF.Exp, scale=1.0 / 16),
                 reads=[dv], writes=[FK.full()])
            fe3 = FE.ap.rearrange("p (n c) -> p n c", c=128)
            d3 = dv.ap.rearrange("p (n c) -> p n c", c=128)
            P.op("DVE", lambda e, d3=d3, fe3=fe3, tb=tb: e.tensor_tensor(
                out=fe3, in0=d3, in1=CSE[:, tb * 4:(tb + 1) * 4].unsqueeze(2).to_broadcast([128, 4, 128]),
                op=ALU.subtract), reads=[dv, smf], writes=[FE.full()])
            P.op("DVE", lambda e, fe3=fe3, tb=tb: e.tensor_tensor(
                out=fe3, in0=fe3, in1=CSS[:, tb * 4:(tb + 1) * 4].unsqueeze(2).to_broadcast([128, 4, 128]),
                op=ALU.add), reads=[FE.full(), smf], writes=[FE.full()])
            P.op("ACT", lambda e: e.activation(out=FE.ap, in_=FE.ap, func=AF.Exp, scale=1.0 / 16),
                 reads=[FE.full()], writes=[FE.full()])
            b = inproj_fm(k, WB, tb * 512, 512)
            kn = KNT.v(KNT.ap[:, tb * 512:(tb + 1) * 512], tb * 512, 512)
            ke = KET.v(KET.ap[:, tb * 512:(tb + 1) * 512], tb * 512, 512)
            P.op("DVE", lambda e, kn=kn, b=b: e.tensor_tensor(out=kn.ap, in0=k.psum[b][:, :], in1=FK.ap, op=ALU.mult),
                 reads=[PSV(k, b), FK.full()], writes=[kn])
            P.op("DVE", lambda e, ke=ke, b=b: e.tensor_tensor(out=ke.ap, in0=k.psum[b][:, :], in1=FE.ap, op=ALU.mult),
                 reads=[PSV(k, b), FE.full()], writes=[ke])
        for g8 in range(2):
            b = ps_bank(k)
            pb = k.psum[b][:, :].bitcast(BF16)
            for t8 in range(8):
                n = g8 * 8 + t8
                ke = KET.v(KET.ap[:, n * 128:(n + 1) * 128], n * 128, 128)
                P.op("PE", lambda e, ke=ke, pb=pb, t8=t8: e.transpose(
                    out=pb[:, t8 * 128:(t8 + 1) * 128], in_=ke.ap, identity=k.IDB.ap),
                    reads=[ke, k.IDB.full()], writes=[PSV(k, b)])
            kv = KE.v(KE.ap[:, g8 * 8:(g8 + 1) * 8, :], g8 * 8 * 128, 8 * 128)
            P.op("ACT", lambda e, kv=kv, pb=pb: e.activation(
                out=kv.ap, in_=pb.rearrange("p (a b) -> p a b", a=8), func=AF.Copy),
                reads=[PSV(k, b)], writes=[kv])
        load_w(k, WC, gv[h, :, :])
        P.op("DVE", lambda e: e.memset(S.ap, 0.0), writes=[S.full()])
        P.op("POOL", lambda e: e.memset(SLOC.ap[:, 0, :], 0.0), writes=[SLOC.v(SLOC.ap[:, 0, :], 0, 256)])

        def rec_step(n):
            b = ps_bank(k)
            kev = KE.v(KE.ap[:, n, :], n * 128, 128)
            vv = Vt.v(Vt.ap[:, n, :], n * 256, 256)
            P.op("PE", lambda e: e.matmul(k.psum[b][:, 0:256], lhsT=kev.ap, rhs=vv.ap, start=True, stop=True),
                 reads=[kev, vv], writes=[PSV(k, b)])
            P.op("DVE", lambda e: e.scalar_tensor_tensor(
                out=S.ap, in0=S.ap, scalar=DEC[:, n:n + 1], in1=k.psum[b][:, 0:256], op0=ALU.mult, op1=ALU.add),
                reads=[S.full(), smf, PSV(k, b)], writes=[S.full()])
            if n < NT - 1:
                sl = SLOC.v(SLOC.ap[:, n + 1, :], (n + 1) * 256, 256)
                P.op("ACT", lambda e: e.activation(out=sl.ap, in_=S.ap, func=AF.Copy),
                     reads=[S.full()], writes=[sl])

        for n in range(NT + 2):
            if n < NT:
                b = inproj_tm(k, WC, n, 256)
                vv = Vt.v(Vt.ap[:, n, :], n * 256, 256)
                P.op("ACT", lambda e, vv=vv, b=b: e.activation(out=vv.ap, in_=k.psum[b][:, 0:256], func=AF.Copy),
                     reads=[PSV(k, b)], writes=[vv])
            if 0 <= n - 2 < NT:
                rec_step(n - 2)
        exchange(k, S.full(), 128, 256, SB32, f"gx{i}_{h}")
        P.op("ACT", lambda e: e.activation(out=SBb.ap, in_=SB32.ap, func=AF.Copy), reads=[SB32.full()], writes=[SBb.full()])
        load_w(k, WA, gq[h, :, :])
        for tb in range(4):
            dv = CS.v(CS.ap[:, tb * 512:(tb + 1) * 512], tb * 512, 512)
            P.op("ACT", lambda e, dv=dv: e.activation(out=FQ.ap, in_=dv.ap, func=AF.Exp, scale=-1.0 / 16,
                                                      bias=k.LNQ.ap), reads=[dv, k.LNQ.full()], writes=[FQ.full()])
            b = inproj_fm(k, WA, tb * 512, 512)
            qv = QT.v(QT.ap[:, tb * 512:(tb + 1) * 512], tb * 512, 512)
            P.op("DVE", lambda e, qv=qv, b=b: e.tensor_tensor(out=qv.ap, in0=k.psum[b][:, :], in1=FQ.ap, op=ALU.mult),
                 reads=[PSV(k, b), FQ.full()], writes=[qv])
        P.op("DVE", lambda e: e.tensor_tensor(
            out=QT2.ap.rearrange("p (n c) -> p n c", c=128), in0=QT.ap.rearrange("p (n c) -> p n c", c=128),
            in1=DCUM.unsqueeze(2).to_broadcast([128, NT, 128]), op=ALU.mult),
            reads=[QT.full(), smf], writes=[QT2.full()])
        load_w(k, WD, gg[h, :, :])
        for n in range(NT):
            b = inproj_tm(k, WD, n, 256)
            gv2 = Gt.v(Gt.ap[:, n, :], n * 256, 256)
            P.op("ACT", lambda e, gv2=gv2, b=b: e.activation(out=gv2.ap, in_=k.psum[b][:, 0:256], func=AF.Silu),
                 reads=[PSV(k, b)], writes=[gv2])
        for c in range(2):
            wv = WOH.v(WOH.ap[:, c, :], c * D, D)
            P.dma("POOL", wv.ap, wo[h * 2 + c, :, :], writes=[wv], max_dma_last_dim=4096)
        NB3 = 4
        sc_bank, o_bank = {}, {}

        def g_A(n):
            b = ps_bank(k)
            kn = KNT.v(KNT.ap[:, n * 128:(n + 1) * 128], n * 128, 128)
            qv = QT.v(QT.ap[:, n * 128:(n + 1) * 128], n * 128, 128)
            st = STs[n % NB3]
            P.op("PE", lambda e: e.matmul(k.psum[b][:, 0:128], lhsT=kn.ap, rhs=qv.ap, start=True, stop=True),
                 reads=[kn, qv], writes=[PSV(k, b)])
            P.op("DVE", lambda e: e.tensor_tensor(out=st.ap, in0=k.psum[b][:, 0:128], in1=k.MASKU.ap, op=ALU.mult),
                 reads=[PSV(k, b), k.MASKU.full()], writes=[st.full()])

        def g_B(n):
            b2 = ps_bank(k)
            o_bank[n] = b2
            st = STs[n % NB3]
            qv = QT.v(QT.ap[:, n * 128:(n + 1) * 128], n * 128, 128)
            q2 = QT2.v(QT2.ap[:, n * 128:(n + 1) * 128], n * 128, 128)
            vv = Vt.v(Vt.ap[:, n, :], n * 256, 256)
            sl = SLOC.v(SLOC.ap[:, n, :], n * 256, 256)
            P.op("PE", lambda e: e.matmul(k.psum[b2][:, 0:256], lhsT=st.ap, rhs=vv.ap, start=True, stop=False),
                 reads=[st.full(), vv], writes=[PSV(k, b2)])
            P.op("PE", lambda e: e.matmul(k.psum[b2][:, 0:256], lhsT=qv.ap, rhs=sl.ap, start=False, stop=False),
                 reads=[qv, sl], writes=[PSV(k, b2)])
            P.op("PE", lambda e: e.matmul(k.psum[b2][:, 0:256], lhsT=q2.ap, rhs=SBb.ap, start=False, stop=True),
                 reads=[q2, SBb.full()], writes=[PSV(k, b2)])

        def g_C(n):
            b2 = o_bank[n]
            tt, sr = TTs[n % NB3], SRs[n % NB3]
            srf = sr.full()
            P.op("ACT", lambda e: e.activation(out=tt.ap, in_=k.psum[b2][:, 0:256], func=AF.Square, accum_out=sr.ap[:, 0:1]),
                 reads=[PSV(k, b2)], writes=[tt.full(), srf])
            P.op("ACT", lambda e: e.activation(out=sr.ap[:, 1:2], in_=sr.ap[:, 0:1], func=AF.Ln, scale=1.0 / 256,
                                               bias=k.EPSC.ap), reads=[srf, k.EPSC.full()], writes=[srf])
            P.op("ACT", lambda e: e.activation(out=sr.ap[:, 1:2], in_=sr.ap[:, 1:2], func=AF.Exp, scale=-0.5),
                 reads=[srf], writes=[srf])

        def g_D(n):
            b2 = o_bank[n]
            tt, mtok, sr = TTs[n % NB3], MTOKs[n % NB3], SRs[n % NB3]
            P.op("DVE", lambda e: e.scalar_tensor_tensor(
                out=tt.ap, in0=k.psum[b2][:, 0:256], scalar=sr.ap[:, 1:2], in1=HNB.ap, op0=ALU.mult, op1=ALU.mult),
                reads=[PSV(k, b2), sr.full(), HNB.full()], writes=[tt.full()])
            gv2 = Gt.v(Gt.ap[:, n, :], n * 256, 256)
            P.op("DVE", lambda e: e.tensor_tensor(out=mtok.ap, in0=tt.ap, in1=gv2.ap, op=ALU.mult),
                 reads=[tt.full(), gv2], writes=[mtok.full()])

        def g_E(n):
            mtok = MTOKs[n % NB3]
            b3 = ps_bank(k)
            pb = k.psum[b3][:, :].bitcast(BF16)
            for c in range(2):
                P.op("PE", lambda e, c=c: e.transpose(
                    out=pb[:, c * 128:(c + 1) * 128], in_=mtok.ap[:, c * 128:(c + 1) * 128], identity=k.IDB.ap),
                    reads=[mtok.full(), k.IDB.full()], writes=[PSV(k, b3)])
            mv = MIXT.vs(MIXT.ap[:, :, n * 128:(n + 1) * 128], [(c * T + n * 128, 128) for c in range(2)])
            P.op("ACT", lambda e: e.activation(
                out=mv.ap, in_=pb[:, 0:256].rearrange("p (a b) -> p a b", a=2), func=AF.Copy),
                reads=[PSV(k, b3)], writes=[mv])

        for s_ in range(NT + 4):
            if s_ < NT:
                g_A(s_)
            if 0 <= s_ - 1 < NT:
                g_B(s_ - 1)
            if 0 <= s_ - 2 < NT:
                g_C(s_ - 2)
            if 0 <= s_ - 3 < NT:
                g_D(s_ - 3)
            if 0 <= s_ - 4 < NT:
                g_E(s_ - 4)
        emit_outproj(k, MIXT, WOH, 2)


def emit_ssd(k, i):
    P, A, nc = k.P, k.A, k.nc
    j = i // 2
    sz, sxbc = k.w(f"sz{j}"), k.w(f"sxbc{j}")
    wo = k.w(f"wo{i}")
    sdk = k.w(f"sdk{j}")
    SCV = Tile(A, [164], F32)
    QTM = Tile(A, [NT, 128], F32)
    DB = Tile(A, [2, 32, 16], F32)
    HALO = Tile(A, [96], F32)
    SSQ = Tile(A, [NT, 8], F32)
    SM = Tile(A, [64], F32)
    CST, CEN, DEC_, DCU_ = SM.ap[:, 0:16], SM.ap[:, 16:32], SM.ap[:, 32:48], SM.ap[:, 48:64]
    AV = Tile(A, [1], F32)
    smf = SM.full()
    P.dma("SP", SCV.ap, k.w(f"scv{j}")[:, :], writes=[SCV.full()])
    scvf = SCV.full()
    cslD = nc.dram_tensor(f"csl{i}", [32, T], F32)
    dcD = nc.dram_tensor(f"dcd{i}", [2, 32, 16], F32)
    ytD = nc.dram_tensor(f"ytd{i}", [NT, 128, 2048], BF16)
    csl_v = V(None, [("D", f"csl{i}")])
    dcd_v = V(None, [("D", f"dcd{i}")])
    m_layer = A.mark()
    WDT = Tile(A, [KC, 128], BF16)
    DTt = Tile(A, [T], F32)
    CSL = Tile(A, [T], F32)
    Q4 = Tile(A, [T], F32)
    load_w(k, WDT, k.w(f"sdt{j}")[:, :])
    for tb in range(4):
        b = inproj_fm(k, WDT, tb * 512, 512)
        dv = DTt.v(DTt.ap[:, tb * 512:(tb + 1) * 512], tb * 512, 512)
        P.op("ACT", lambda e, dv=dv, b=b: e.activation(out=dv.ap, in_=k.psum[b][:, :], func=AF.Exp,
                                                         bias=SCV.ap[:, 160:161]), reads=[PSV(k, b), scvf], writes=[dv])
    P.op("ACT", lambda e: e.activation(out=DTt.ap, in_=DTt.ap, func=AF.Ln, bias=1.0), reads=[DTt.full()], writes=[DTt.full()])
    P.op("ACT", lambda e: e.activation(out=AV.ap, in_=SCV.ap[:, 161:162], func=AF.Exp), reads=[scvf], writes=[AV.full()])
    P.op("DVE", lambda e: e.tensor_scalar(out=AV.ap, in0=AV.ap, scalar1=-1.0, scalar2=None, op0=ALU.mult),
         reads=[AV.full()], writes=[AV.full()])
    P.op("DVE", lambda e: e.tensor_scalar(out=Q4.ap, in0=DTt.ap, scalar1=AV.ap, scalar2=None, op0=ALU.mult),
         reads=[DTt.full(), AV.full()], writes=[Q4.full()])
    P.op("DVE", lambda e: e.tensor_tensor_scan(
        out=CSL.ap, data0=k.ONE1.ap.to_broadcast([128, T]), data1=Q4.ap, initial=0.0,
        op0=ALU.mult, op1=ALU.add), reads=[Q4.full(), k.ONE1.full()], writes=[CSL.full()])
    cs3 = CSL.ap.rearrange("p (n c) -> p n c", c=128)
    P.op("DVE", lambda e: e.memset(SM.ap[:, 0:1], 0.0), writes=[smf])
    P.op("DVE", lambda e: e.tensor_copy(out=SM.ap[:, 1:16], in_=cs3[:, 0:15, 127]), reads=[CSL.full()], writes=[smf])
    P.op("DVE", lambda e: e.tensor_copy(out=CEN, in_=cs3[:, :, 127]), reads=[CSL.full()], writes=[smf])
    P.op("DVE", lambda e: e.tensor_tensor(out=DEC_, in0=CEN, in1=CST, op=ALU.subtract), reads=[smf], writes=[smf])
    P.op("ACT", lambda e: e.activation(out=DEC_, in_=DEC_, func=AF.Exp), reads=[smf], writes=[smf])
    P.op("ACT", lambda e: e.activation(out=DCU_, in_=CST, func=AF.Exp), reads=[smf], writes=[smf])
    smv = V(SM.ap[0:32, 32:64].rearrange("p (a n) -> p a n", a=2), SM.full().cells)
    P.dma("SP", dcD.ap().rearrange("a h n -> h a n"), smv.ap, reads=[smv], writes=[dcd_v])
    P.dma("SP", DB.ap, dcD.ap().partition_broadcast(128), reads=[dcd_v], writes=[DB.full()])
    P.op("DVE", lambda e: e.tensor_tensor(out=CEN, in0=CEN, in1=CST, op=ALU.subtract), reads=[smf], writes=[smf])
    P.op("DVE", lambda e: e.tensor_tensor(
        out=cs3, in0=cs3, in1=CST.unsqueeze(2).to_broadcast([128, NT, 128]), op=ALU.subtract),
        reads=[CSL.full(), smf], writes=[CSL.full()])
    cslsb = V(CSL.ap[0:32, :], CSL.full().cells)
    P.dma("SP", cslD[:, :], cslsb.ap, reads=[cslsb], writes=[csl_v])
    P.op("ACT", lambda e: e.activation(out=Q4.ap, in_=DTt.ap, func=AF.Ln), reads=[DTt.full()], writes=[Q4.full()])
    P.op("DVE", lambda e: e.tensor_tensor(out=Q4.ap, in0=Q4.ap, in1=CSL.ap, op=ALU.subtract),
         reads=[Q4.full(), CSL.full()], writes=[Q4.full()])
    P.op("ACT", lambda e: e.activation(out=Q4.ap[0:32, :], in_=CSL.ap[0:32, :], func=AF.Exp),
         reads=[CSL.full()], writes=[Q4.full()])
    q3 = Q4.ap.rearrange("p (n c) -> p n c", c=128)
    P.op("DVE", lambda e: e.tensor_tensor(
        out=q3[32:64], in0=CEN[32:64].unsqueeze(2).to_broadcast([32, NT, 128]), in1=cs3[32:64], op=ALU.subtract),
        reads=[CSL.full(), smf], writes=[Q4.full()])
    P.op("ACT", lambda e: e.activation(out=Q4.ap[32:64, :], in_=Q4.ap[32:64, :], func=AF.Exp),
         reads=[Q4.full()], writes=[Q4.full()])
    P.op("DVE", lambda e: e.tensor_tensor(out=Q4.ap[32:64, :], in0=Q4.ap[32:64, :], in1=DTt.ap[32:64, :], op=ALU.mult),
         reads=[Q4.full(), DTt.full()], writes=[Q4.full()])
    P.op("ACT", lambda e: e.activation(out=Q4.ap[64:96, :], in_=CSL.ap[64:96, :], func=AF.Copy),
         reads=[CSL.full()], writes=[Q4.full()])
    for n4 in range(4):
        b = ps_bank(k)
        for t in range(4):
            n = n4 * 4 + t
            qv = Q4.v(Q4.ap[:, n * 128:(n + 1) * 128], n * 128, 128)
            P.op("PE", lambda e, qv=qv, b=b, t=t: e.transpose(
                out=k.psum[b][:, t * 128:(t + 1) * 128], in_=qv.ap, identity=k.IDF.ap),
                reads=[qv, k.IDF.full()], writes=[PSV(k, b)])
        qm = QTM.v(QTM.ap[:, n4 * 4:(n4 + 1) * 4, :], n4 * 4 * 128, 4 * 128)
        P.op("ACT", lambda e, qm=qm, b=b: e.activation(
            out=qm.ap, in_=k.psum[b][:, :].rearrange("p (a b) -> p a b", a=4), func=AF.Copy),
            reads=[PSV(k, b)], writes=[qm])
    A.release(m_layer)
    HL = Tile(A, [32, 3], F32)
    WR = [Tile(A, [KC, 128], BF16) for _ in range(2)]
    wi = 0
    for cb in range(32):
        w = WR[wi % 2]
        wi += 1
        load_w(k, w, sxbc[cb, :, :])
        b = ps_bank(k)
        for kc in range(KC):
            xv = xn_blk(k, kc, T - 3, 3)
            P.op("PE", lambda e, kc=kc, xv=xv, b=b, w=w: e.matmul(
                k.psum[b][:, 0:3], lhsT=w.ap[:, kc, :], rhs=xv.ap, start=(kc == 0), stop=(kc == KC - 1)),
                reads=[w.full(), xv], writes=[PSV(k, b)])
        hv = HL.v(HL.ap[:, cb, :], cb * 3, 3)
        P.op("ACT", lambda e, hv=hv, b=b: e.activation(out=hv.ap, in_=k.psum[b][:, 0:3], func=AF.Copy),
             reads=[PSV(k, b)], writes=[hv])
    exchange(k, V(HL.ap.rearrange("p a b -> p (a b)"), HL.full().cells), 128, 96, HALO, f"hx{i}")
    A.release(m_layer)
    WR = [Tile(A, [KC, 128], BF16) for _ in range(2)]
    WZ = Tile(A, [KC, 256], BF16)
    PRE = Tile(A, [T + 4], BF16)
    DG = [Tile(A, [4, 128], BF16) for _ in range(2)]
    DGD = Tile(A, [4, 128], BF16)
    XsT = Tile(A, [2, T], BF16)
    BT = Tile(A, [T], BF16)
    CT = Tile(A, [T], BF16)
    XS = Tile(A, [NT, 256], BF16)
    XW = Tile(A, [NT, 256], BF16)
    Btok = Tile(A, [NT, 128], BF16)
    SLOC = Tile(A, [NT, 256], BF16, off=XsT.off)
    YT = Tile(A, [2, T], BF16)
    S = Tile(A, [256], F32)
    SB32 = Tile(A, [256], F32)
    NB3 = 4
    SBns = [Tile(A, [256], BF16) for _ in range(NB3)]
    BCts = [Tile(A, [4, 128], F32) for _ in range(3)]
    CBMs = [Tile(A, [128], F32) for _ in range(NB3)]
    SEG = [Tile(A, [128], F32) for _ in range(8)]
    MTs = [[Tile(A, [128], BF16) for _ in range(4)] for _ in range(NB3)]
    T1s = [Tile(A, [256], F32, off=XW.off + r_ * 1024) for r_ in range(NB3)]
    THs = [Tile(A, [256], F32, off=XW.off + 4096 + r_ * 1024) for r_ in range(NB3)]
    YFs = [Tile(A, [256], F32, off=PRE.off + r_ * 1024) for r_ in range(NB3)]
    G2s = [Tile(A, [256], BF16) for _ in range(NB3)]
    ZCs = [Tile(A, [256], BF16) for _ in range(NB3)]
    YGBs = [Tile(A, [256], BF16) for _ in range(NB3)]
    DSK = Tile(A, [256], F32)
    dgi = 0
    pre_setup = []
    for g in range(8):
        P.dma("SP", DSK.ap, sdk[:, g * 256:(g + 1) * 256], writes=[DSK.full()])
        load_w(k, WZ, sz[g, :, :])
        for jh in range(4):
            dv_ = DGD.v(DGD.ap[:, jh, :], jh * 128, 128)
            P.op("POOL", lambda e, dv_=dv_, jh=jh: e.tensor_scalar(
                out=dv_.ap, in0=k.IDB.ap, scalar1=DSK.ap[:, jh * 64:jh * 64 + 1], scalar2=None, op0=ALU.mult),
                reads=[k.IDB.full(), DSK.full()], writes=[dv_])

        def conv_setup(cb):
            nonlocal wi, dgi
            w = WR[wi % 2]
            wi += 1
            load_w(k, w, sxbc[cb, :, :])
            dg = DG[dgi % 2]
            dgi += 1
            for tap in range(4):
                dgv = dg.v(dg.ap[:, tap, :], tap * 128, 128)
                P.op("POOL", lambda e, dgv=dgv, tap=tap: e.tensor_scalar(
                    out=dgv.ap, in0=k.IDB.ap, scalar1=SCV.ap[:, cb * 4 + tap:cb * 4 + tap + 1], scalar2=None,
                    op0=ALU.mult), reads=[k.IDB.full(), scvf], writes=[dgv])
            return w, dg

        def conv_proj(w, tb, cb=None):
            if tb == 0:
                hp = PRE.v(PRE.ap[:, 0:3], 0, 3)
                P.op("DVE", lambda e: e.tensor_copy(out=hp.ap, in_=HALO.ap[:, cb * 3:(cb + 1) * 3]),
                     reads=[HALO.full()], writes=[hp])
            b = inproj_fm(k, w, tb * 512, 512)
            pv = PRE.v(PRE.ap[:, 3 + tb * 512:3 + (tb + 1) * 512], 3 + tb * 512, 512)
            P.op("ACT", lambda e: e.activation(out=pv.ap, in_=k.psum[b][:, :], func=AF.Copy),
                 reads=[PSV(k, b)], writes=[pv])

        def conv_out(cb, dg, dst, tb):
            b = ps_bank(k)
            for tap in range(4):
                pv = PRE.v(PRE.ap[:, tap + tb * 512:tap + (tb + 1) * 512], tap + tb * 512, 512)
                P.op("PE", lambda e, pv=pv, tap=tap: e.matmul(
                    k.psum[b][:, :], lhsT=dg.ap[:, tap, :], rhs=pv.ap, start=(tap == 0), stop=(tap == 3)),
                    reads=[dg.full(), pv], writes=[PSV(k, b)])
            dv = V(dst.ap[:, tb * 512:(tb + 1) * 512], dst.cells)
            P.op("ACT", lambda e: e.activation(
                out=dv.ap, in_=k.psum[b][:, :], func=AF.Silu, bias=SCV.ap[:, 128 + cb:129 + cb]),
                reads=[PSV(k, b), scvf], writes=[dv])

        dsts = [(2 * g, XsT.v(XsT.ap[:, 0, :], 0, T)), (2 * g + 1, XsT.v(XsT.ap[:, 1, :], T, T)),
                (16 + g, BT.full())]
        for bi, (cb, dst) in enumerate(dsts):
            if bi < 2 and pre_setup:
                w, dg = pre_setup.pop(0)
            else:
                w, dg = conv_setup(cb)
            for tb in range(4):
                conv_proj(w, tb, cb)
            for tb in range(4):
                conv_out(cb, dg, dst, tb)
        cbC = 24 + g
        wC, dgC = None, None

        def tr_batch(q, g=g):
            b = ps_bank(k)
            pb = k.psum[b][:, :].bitcast(BF16)
            for t in range(4):
                n = q * 4 + t
                for c in range(2):
                    xv = XsT.v(XsT.ap[:, c, n * 128:(n + 1) * 128], c * T + n * 128, 128)
                    P.op("PE", lambda e, xv=xv, t=t, c=c: e.transpose(
                        out=pb[:, (t * 2 + c) * 128:(t * 2 + c + 1) * 128], in_=xv.ap, identity=k.IDB.ap),
                        reads=[xv, k.IDB.full()], writes=[PSV(k, b)])
            xs4 = XS.v(XS.ap[:, q * 4:(q + 1) * 4, :], q * 4 * 256, 4 * 256)
            P.op("ACT", lambda e: e.activation(
                out=xs4.ap, in_=pb.rearrange("p (a b) -> p a b", a=4), func=AF.Copy),
                reads=[PSV(k, b)], writes=[xs4])
            b2 = ps_bank(k)
            pb2 = k.psum[b2][:, :].bitcast(BF16)
            for t in range(4):
                n = q * 4 + t
                bv = BT.v(BT.ap[:, n * 128:(n + 1) * 128], n * 128, 128)
                P.op("PE", lambda e, bv=bv, t=t: e.transpose(
                    out=pb2[:, t * 128:(t + 1) * 128], in_=bv.ap, identity=k.IDB.ap),
                    reads=[bv, k.IDB.full()], writes=[PSV(k, b2)])
            b4 = Btok.v(Btok.ap[:, q * 4:(q + 1) * 4, :], q * 4 * 128, 4 * 128)
            P.op("ACT", lambda e: e.activation(
                out=b4.ap, in_=pb2[:, 0:512].rearrange("p (a b) -> p a b", a=4), func=AF.Copy),
                reads=[PSV(k, b2)], writes=[b4])
            xw4 = XW.v(XW.ap[:, q * 4:(q + 1) * 4, :], q * 4 * 256, 4 * 256)
            P.op("DVE", lambda e: e.tensor_tensor(
                out=xw4.ap.rearrange("p n (h q) -> p n h q", h=4), in0=xs4.ap.rearrange("p n (h q) -> p n h q", h=4),
                in1=QTM.ap[:, q * 4:(q + 1) * 4, 32 + 4 * g:36 + 4 * g].unsqueeze(3).to_broadcast([128, 4, 4, 64]),
                op=ALU.mult), reads=[xs4, QTM.full()], writes=[xw4])

        s3 = S.ap.rearrange("p (h q) -> p h q", h=4)

        def rec_step(n, g=g):
            b = ps_bank(k)
            bv = Btok.v(Btok.ap[:, n, :], n * 128, 128)
            xw = XW.v(XW.ap[:, n, :], n * 256, 256)
            P.op("PE", lambda e: e.matmul(k.psum[b][:, 0:256], lhsT=bv.ap, rhs=xw.ap, start=True, stop=True),
                 reads=[bv, xw], writes=[PSV(k, b)])
            P.op("DVE", lambda e: e.tensor_tensor(
                out=s3, in0=s3, in1=DB.ap[:, 0, 4 * g:4 * g + 4, n].unsqueeze(2).to_broadcast([128, 4, 64]),
                op=ALU.mult), reads=[S.full(), DB.full()], writes=[S.full()])
            P.op("DVE", lambda e: e.tensor_tensor(out=S.ap, in0=k.psum[b][:, 0:256], in1=S.ap, op=ALU.add),
                 reads=[S.full(), PSV(k, b)], writes=[S.full()])
            if n < NT - 1:
                sl = SLOC.v(SLOC.ap[:, n + 1, :], (n + 1) * 256, 256)
                P.op("ACT", lambda e: e.activation(out=sl.ap, in_=S.ap, func=AF.Copy),
                     reads=[S.full()], writes=[sl])

        P.op("DVE", lambda e: e.memset(S.ap, 0.0), writes=[S.full()])
        for q in range(4):
            tr_batch(q)
        P.op("POOL", lambda e: e.memset(SLOC.ap[:, 0, :], 0.0), writes=[SLOC.v(SLOC.ap[:, 0, :], 0, 256)])
        wC, dgC = conv_setup(cbC)
        for q in range(4):
            conv_proj(wC, q, cbC)
            for n in range(q * 4, q * 4 + 4):
                rec_step(n)
        exchange(k, S.full(), 128, 256, SB32, f"sx{i}_{g}")
        for tb in range(4):
            conv_out(cbC, dgC, CT.full(), tb)
        if g < 7:
            pre_setup.extend([conv_setup(2 * (g + 1)), conv_setup(2 * (g + 1) + 1)])
        sb3 = SB32.ap.rearrange("p (h q) -> p h q", h=4)
        def s_dma(n, g=g):
            bct = BCts[n % 3]
            P.dma("SP", bct.ap, cslD[4 * g:4 * g + 4, n * 128:(n + 1) * 128].partition_broadcast(128),
                  reads=[csl_v], writes=[bct.full()])

        def s_A(n, g=g):
            r = n % NB3
            bct, cbm, th, zc = BCts[n % 3], CBMs[r], THs[r], ZCs[r]
            bv = BT.v(BT.ap[:, n * 128:(n + 1) * 128], n * 128, 128)
            cv = CT.v(CT.ap[:, n * 128:(n + 1) * 128], n * 128, 128)
            b = ps_bank(k)
            P.op("PE", lambda e: e.matmul(k.psum[b][:, 0:128], lhsT=bv.ap, rhs=cv.ap, start=True, stop=True),
                 reads=[bv, cv], writes=[PSV(k, b)])
            bz = inproj_tm(k, WZ, n, 256)
            P.op("DVE", lambda e: e.tensor_tensor(out=cbm.ap, in0=k.psum[b][:, 0:128], in1=k.MASKU.ap, op=ALU.mult),
                 reads=[PSV(k, b), k.MASKU.full()], writes=[cbm.full()])
            for jh in range(4):
                h = 4 * g + jh
                sg = SEG[(n % 2) * 4 + jh]
                P.op("DVE", lambda e, sg=sg, jh=jh, h=h: e.tensor_scalar(
                    out=sg.ap, in0=bct.ap[:, jh, :], scalar1=QTM.ap[:, n, 64 + h:65 + h], scalar2=None, op0=ALU.min),
                    reads=[bct.full(), QTM.full()], writes=[sg.full()])
            for jh in range(4):
                h = 4 * g + jh
                sg = SEG[(n % 2) * 4 + jh]
                P.op("ACT", lambda e, sg=sg, h=h: e.activation(
                    out=sg.ap, in_=sg.ap, func=AF.Exp, bias=QTM.ap[:, n, 96 + h:97 + h]),
                    reads=[sg.full(), QTM.full()], writes=[sg.full()])
            P.op("ACT", lambda e: e.activation(out=th.ap, in_=k.psum[bz][:, 0:256], func=AF.Tanh, scale=0.5),
                 reads=[PSV(k, bz)], writes=[th.full()])
            P.op("ACT", lambda e: e.activation(out=zc.ap, in_=k.psum[bz][:, 0:256], func=AF.Copy),
                 reads=[PSV(k, bz)], writes=[zc.full()])

        def s_B(n, g=g):
            r = n % NB3
            cbm, th, zc, g2, sbn = CBMs[r], THs[r], ZCs[r], G2s[r], SBns[r]
            for jh in range(4):
                sg = SEG[(n % 2) * 4 + jh]
                mt = MTs[r][jh]
                P.op("DVE", lambda e, sg=sg, mt=mt: e.tensor_tensor(out=mt.ap, in0=sg.ap, in1=cbm.ap, op=ALU.mult),
                     reads=[sg.full(), cbm.full()], writes=[mt.full()])
            P.op("DVE", lambda e: e.scalar_tensor_tensor(
                out=g2.ap, in0=th.ap, scalar=1.0, in1=zc.ap, op0=ALU.add, op1=ALU.mult),
                reads=[th.full(), zc.full()], writes=[g2.full()])
            P.op("DVE", lambda e: e.tensor_tensor(
                out=sbn.ap.rearrange("p (h q) -> p h q", h=4), in0=sb3,
                in1=DB.ap[:, 1, 4 * g:4 * g + 4, n].unsqueeze(2).to_broadcast([128, 4, 64]), op=ALU.mult),
                reads=[SB32.full(), DB.full()], writes=[sbn.full()])

        cbank = {}

        def s_C1(n, g=g):
            r = n % NB3
            sbn = SBns[r]
            sl = SLOC.v(SLOC.ap[:, n, :], n * 256, 256)
            cv = CT.v(CT.ap[:, n * 128:(n + 1) * 128], n * 128, 128)
            by = ps_bank(k)
            cbank[n] = by
            for jh in range(4):
                mt = MTs[r][jh]
                xs = XS.v(XS.ap[:, n, jh * 64:(jh + 1) * 64], n * 256 + jh * 64, 64)
                P.op("PE", lambda e, mt=mt, xs=xs, jh=jh: e.matmul(
                    k.psum[by][:, jh * 64:(jh + 1) * 64], lhsT=mt.ap, rhs=xs.ap, start=True, stop=False),
                    reads=[mt.full(), xs], writes=[PSV(k, by)])
                P.op("PE", lambda e, xs=xs, jh=jh: e.matmul(
                    k.psum[by][:, jh * 64:(jh + 1) * 64], lhsT=DGD.ap[:, jh, :], rhs=xs.ap, start=False, stop=True),
                    reads=[DGD.full(), xs], writes=[PSV(k, by)])
            P.op("PE", lambda e: e.matmul(k.psum[by][:, 256:512], lhsT=cv.ap, rhs=sl.ap, start=True, stop=False),
                 reads=[cv, sl], writes=[PSV(k, by)])
            P.op("PE", lambda e: e.matmul(k.psum[by][:, 256:512], lhsT=cv.ap, rhs=sbn.ap, start=False, stop=True),
                 reads=[cv, sbn.full()], writes=[PSV(k, by)])

        def s_C2(n, g=g):
            r = n % NB3
            t1, yf, g2, ygb = T1s[r], YFs[r], G2s[r], YGBs[r]
            by = cbank[n]
            P.op("DVE", lambda e: e.tensor_tensor(
                out=t1.ap.rearrange("p (h q) -> p h q", h=4),
                in0=k.psum[by][:, 256:512].rearrange("p (h q) -> p h q", h=4),
                in1=QTM.ap[:, n, 4 * g:4 * g + 4].unsqueeze(2).to_broadcast([128, 4, 64]), op=ALU.mult),
                reads=[PSV(k, by), QTM.full()], writes=[t1.full()])
            P.op("DVE", lambda e: e.tensor_tensor(out=yf.ap, in0=k.psum[by][:, 0:256], in1=t1.ap, op=ALU.add),
                 reads=[PSV(k, by), t1.full()], writes=[yf.full()])
            P.op("DVE", lambda e: e.tensor_tensor(out=ygb.ap, in0=yf.ap, in1=g2.ap, op=ALU.mult),
                 reads=[yf.full(), g2.full()], writes=[ygb.full()])
            sq = SSQ.v(SSQ.ap[:, n, g:g + 1], n * 8 + g, 1)
            P.op("ACT", lambda e: e.activation(out=t1.ap, in_=ygb.ap, func=AF.Square, accum_out=sq.ap),
                 reads=[ygb.full()], writes=[t1.full(), sq])

        def s_D(n, g=g):
            ygb = YGBs[n % NB3]
            b3 = ps_bank(k)
            pb = k.psum[b3][:, :].bitcast(BF16)
            for c in range(2):
                P.op("PE", lambda e, c=c: e.transpose(
                    out=pb[:, c * 128:(c + 1) * 128], in_=ygb.ap[:, c * 128:(c + 1) * 128], identity=k.IDB.ap),
                    reads=[ygb.full(), k.IDB.full()], writes=[PSV(k, b3)])
            yv = YT.vs(YT.ap[:, :, n * 128:(n + 1) * 128], [(c * T + n * 128, 128) for c in range(2)])
            P.op("ACT", lambda e: e.activation(
                out=yv.ap, in_=pb[:, 0:256].rearrange("p (a b) -> p a b", a=2), func=AF.Copy),
                reads=[PSV(k, b3)], writes=[yv])

        s_dma(0)
        for s_ in range(NT + 4):
            if s_ + 1 < NT:
                s_dma(s_ + 1)
            if s_ < NT:
                s_A(s_)
            if 0 <= s_ - 1 < NT:
                s_B(s_ - 1)
            if 0 <= s_ - 2 < NT:
                s_C1(s_ - 2)
            if 0 <= s_ - 3 < NT:
                s_C2(s_ - 3)
            if 0 <= s_ - 4 < NT:
                s_D(s_ - 4)
        ytv = V(None, [("D", f"ytd{i}", g)])
        for c in range(2):
            P.dma("SP", ytD.ap()[:, :, (2 * g + c) * 128:(2 * g + c + 1) * 128].rearrange("n p t -> p n t"),
                  YT.ap[:, c, :].rearrange("p (n t) -> p n t", n=NT), reads=[YT.full()], writes=[ytv])
    A.release(m_layer)
    WOM = Tile(A, [16, D], BF16)
    YTt = [Tile(A, [16, 128], BF16) for _ in range(2)]
    WST = [Tile(A, [D], F32) for _ in range(2)]
    SNF = Tile(A, [16], F32)
    P.dma("SP", SNF.ap, k.w(f"snf{j}")[:, :], writes=[SNF.full()])
    for c in range(16):
        wv = WOM.v(WOM.ap[:, c, :], c * D, D)
        ws = WST[c % 2]
        P.dma("SP", ws.ap, wo[c, :, :], writes=[ws.full()])
        P.op("DVE", lambda e, wv=wv, ws=ws, c=c: e.tensor_scalar(
            out=wv.ap, in0=ws.ap, scalar1=SNF.ap[:, c:c + 1], scalar2=0.5, op0=ALU.mult, op1=ALU.mult),
            reads=[ws.full(), SNF.full()], writes=[wv])
    rsf = k.RS.full()
    P.op("DVE", lambda e: e.tensor_reduce(out=k.RS.ap, in_=SSQ.ap, axis=mybir.AxisListType.X, op=ALU.add),
         reads=[SSQ.full()], writes=[rsf])
    P.op("DVE", lambda e: e.tensor_scalar(out=k.RS.ap, in0=k.RS.ap, scalar1=0.25 / 2048, scalar2=EPS,
                                           op0=ALU.mult, op1=ALU.add), reads=[rsf], writes=[rsf])
    P.op("DVE", lambda e: e.reciprocal(out=k.RS.ap, in_=k.RS.ap), reads=[rsf], writes=[rsf])
    P.op("ACT", lambda e: e.activation(out=k.RS.ap, in_=k.RS.ap, func=AF.Sqrt), reads=[rsf], writes=[rsf])
    ytall = [V(None, [("D", f"ytd{i}", g)]) for g in range(8)]
    for n in range(NT):
        yt = YTt[n % 2]
        P.dma("SP", yt.ap.rearrange("p a b -> p (a b)"), ytD[n, :, :], reads=ytall, writes=[yt.full()])
        for nb in range(2):
            b = ps_bank(k)
            for c in range(16):
                wv = WOM.v(WOM.ap[:, c, nb * 512:(nb + 1) * 512], c * D + nb * 512, 512)
                P.op("PE", lambda e, yt=yt, wv=wv, b=b, c=c: e.matmul(
                    k.psum[b][:, :], lhsT=yt.ap[:, c, :], rhs=wv.ap, start=(c == 0), stop=(c == 15)),
                    reads=[yt.full(), wv], writes=[PSV(k, b)])
            hv = hview(k, n, nb * 512, 512)
            P.op("DVE", lambda e, hv=hv, b=b, n=n: e.scalar_tensor_tensor(
                out=hv.ap, in0=k.psum[b][:, :], scalar=k.RS.ap[:, n:n + 1], in1=hv.ap, op0=ALU.mult, op1=ALU.add),
                reads=[PSV(k, b), hv, rsf], writes=[hv])


def emit_output(k, cfg, bcp_d, out_d):
    P, A = k.P, k.A
    outs = []
    if not cfg.get("final_norm", True):
        for j in range(NT):
            hv = hview(k, j)
            outs.append(P.dma("SP", out_d[j * 128:(j + 1) * 128, :], hv.ap, reads=[hv],
                              writes=[V(None, [("D", "out", j)])]))
    else:
        FN = Tile(A, [D], F32)
        OB = [Tile(A, [D], F32) for _ in range(2)]
        P.dma("SP", FN.ap, bcp_d[:, 0:1024], writes=[FN.full()])
        SS, RS = k.SS, k.RS
        ssf, rsf = SS.full(), RS.full()
        junk = k.JUNK.full()
        for j in range(NT):
            hv = hview(k, j)
            P.op("ACT", lambda e, hv=hv, j=j: e.activation(
                out=k.JUNK.ap, in_=hv.ap, func=AF.Square, accum_out=SS.ap[:, j:j + 1]),
                reads=[hv], writes=[junk, ssf])
        P.op("DVE", lambda e: e.tensor_scalar(out=RS.ap, in0=SS.ap, scalar1=1.0 / D, scalar2=EPS,
                                               op0=ALU.mult, op1=ALU.add), reads=[ssf], writes=[rsf])
        P.op("DVE", lambda e: e.reciprocal(out=RS.ap, in_=RS.ap), reads=[rsf], writes=[rsf])
        P.op("ACT", lambda e: e.activation(out=RS.ap, in_=RS.ap, func=AF.Sqrt), reads=[rsf], writes=[rsf])
        for j in range(NT):
            hv = hview(k, j)
            ob = OB[j % 2]
            P.op("DVE", lambda e, hv=hv, ob=ob, j=j: e.scalar_tensor_tensor(
                out=ob.ap, in0=hv.ap, scalar=RS.ap[:, j:j + 1], in1=FN.ap, op0=ALU.mult, op1=ALU.mult),
                reads=[hv, rsf, FN.full()], writes=[ob.full()])
            outs.append(P.dma("SP", out_d[j * 128:(j + 1) * 128, :], ob.ap, reads=[ob.full()],
                              writes=[V(None, [("D", "out", j)])]))
    fin = Op("SP", lambda e: e.nop())
    fin.seq = len(P.ops["SP"])
    for o in outs:
        P._need(fin, o)
    P.ops["SP"].append(fin)


FULL_CFG = dict(sublayers=[("mix", 0), ("ffn", 0), ("mix", 1), ("ffn", 1),
                           ("mix", 2), ("ffn", 2), ("mix", 3), ("ffn", 3)], final_norm=True)


def run(inputs, cfg):
    nc = build(cfg)
    maps = prep_inputs(inputs, set(nc.used_inputs))
    maps = [{kk: m[kk] for kk in nc.used_inputs} for m in maps]
    res = run_bass_kernel_spmd(nc, maps, core_ids=list(range(8)))
    out = np.zeros((4, 4096, D), np.float32)
    for c in range(8):
        b, half = c // 2, c % 2
        out[b, half * T:(half + 1) * T] = res.results[c]["out"]
    return out


def kernel(**inputs):
    return run(inputs, FULL_CFG)
```

```python
import contextlib
import numpy as np
import concourse.bass as bass
import concourse.mybir as mybir
from concourse.bass_utils import run_bass_kernel_spmd

F32 = mybir.dt.float32
BF16 = mybir.dt.bfloat16
U8 = mybir.dt.uint8
AF = mybir.ActivationFunctionType
ALU = mybir.AluOpType

T = 2048
NT = 16
D = 1024
KC = 8
FH = 2816
EPS = 1e-6
CELL = 256
ARENA_BYTES = 206 * 1024
PAIRS = [[0, 1], [2, 3], [4, 5], [6, 7]]
SAME_ENG_SYNC = True


class V:
    __slots__ = ("ap", "cells")

    def __init__(self, ap, cells):
        self.ap = ap
        self.cells = tuple(cells)


def scells(off, nbytes):
    return [("S", c) for c in range(off // CELL, (off + nbytes - 1) // CELL + 1)]


class Op:
    __slots__ = ("eng", "fn", "waits", "signal", "sigval", "seq", "dma", "dsem", "dval", "inc")

    def __init__(self, eng, fn):
        self.eng = eng
        self.fn = fn
        self.waits = []
        self.signal = False
        self.sigval = None
        self.seq = None
        self.dma = False
        self.dsem = None
        self.dval = None
        self.inc = 1


class Prog:
    ENGS = ("PE", "ACT", "DVE", "POOL", "SP")
    NDSEM = 8

    def __init__(self):
        self.ops = {e: [] for e in self.ENGS}
        self.cellw = {}
        self.cellr = {}
        self.waited_eng = {e: {} for e in self.ENGS}
        self.waited_dma = {e: {} for e in self.ENGS}
        self.dma_count = {"SP": 0, "POOL": 0, "ACT": 0}
        self.dma_hist = {"SP": [], "POOL": [], "ACT": []}

    def _need(self, op, dep):
        if dep is op:
            return
        e = op.eng
        if dep.dma:
            key = (dep.eng, dep.dsem)
            if self.waited_dma[e].get(key, 0) >= dep.dval:
                return
            self.waited_dma[e][key] = dep.dval
            op.waits.append(dep)
        else:
            if dep.eng == e and not op.dma:
                if e == "PE" or not SAME_ENG_SYNC:
                    return
            if self.waited_eng[e].get(dep.eng, -1) >= dep.seq:
                return
            self.waited_eng[e][dep.eng] = dep.seq
            dep.signal = True
            op.waits.append(dep)

    def _track(self, op, reads, writes):
        deps = []
        for v in reads:
            for c in v.cells:
                w = self.cellw.get(c)
                if w is not None:
                    deps.append(w)
        for v in writes:
            for c in v.cells:
                w = self.cellw.get(c)
                if w is not None:
                    deps.append(w)
                deps.extend(self.cellr.get(c, ()))
        seen = set()
        for d in deps:
            if id(d) in seen:
                continue
            seen.add(id(d))
            self._need(op, d)
        for v in reads:
            for c in v.cells:
                self.cellr.setdefault(c, []).append(op)
        for v in writes:
            for c in v.cells:
                self.cellw[c] = op
                self.cellr[c] = []

    def op(self, eng, fn, reads=(), writes=()):
        o = Op(eng, fn)
        o.seq = len(self.ops[eng])
        self._track(o, reads, writes)
        self.ops[eng].append(o)
        return o

    def dma(self, queue, out_ap, in_ap, reads=(), writes=(), **kw):
        o = Op(queue, lambda e: e.dma_start(out=out_ap, in_=in_ap, **kw))
        o.dma = True
        o.inc = 16
        i = self.dma_count[queue]
        self.dma_count[queue] = i + 1
        o.dsem = i % self.NDSEM
        o.dval = 16 * (i // self.NDSEM + 1)
        o.seq = len(self.ops[queue])
        hist = self.dma_hist[queue]
        if i >= self.NDSEM:
            self._need(o, hist[i - self.NDSEM])
        hist.append(o)
        self._track(o, reads, writes)
        self.ops[queue].append(o)
        return o

    def coll(self, fn, reads=(), writes=()):
        o = Op("POOL", fn)
        o.dma = True
        o.inc = 1
        self.ncoll = getattr(self, "ncoll", 0) + 1
        o.dsem = "CC"
        o.dval = self.ncoll
        o.seq = len(self.ops["POOL"])
        self._track(o, reads, writes)
        self.ops["POOL"].append(o)
        return o

    def finalize(self):
        for e in self.ENGS:
            n = 0
            for o in self.ops[e]:
                if o.dma:
                    continue
                if o.signal:
                    n += 1
                    o.sigval = n

    def emit(self, eng_name, e, esem, dsem, ccsem):
        for o in self.ops[eng_name]:
            for d in o.waits:
                if d.dma:
                    if d.dsem == "CC":
                        e.wait_ge(ccsem, d.dval)
                    else:
                        e.wait_ge(dsem[d.eng][d.dsem], d.dval)
                else:
                    e.wait_ge(esem[d.eng], d.sigval)
            inst = o.fn(e)
            if o.dma:
                if o.dsem == "CC":
                    inst.then_inc(ccsem)
                else:
                    inst.then_inc(dsem[eng_name][o.dsem], 16)
            elif o.signal:
                inst.then_inc(esem[eng_name], 1)


class Arena:
    def __init__(self, ap):
        self.ap = ap
        self.top = 0

    def alloc(self, nbytes, align=256):
        off = (self.top + align - 1) // align * align
        assert off + nbytes <= ARENA_BYTES, ("arena overflow", off, nbytes)
        self.top = off + nbytes
        self.high = max(getattr(self, "high", 0), self.top)
        return off

    def mark(self):
        return self.top

    def release(self, m):
        self.top = m


class Tile:
    def __init__(self, arena, dims, dtype, name="", off=None):
        self.dims = list(dims)
        self.dtype = dtype
        self.esz = 4 if dtype == F32 else 2
        n = int(np.prod(dims))
        self.nbytes = n * self.esz
        self.off = arena.alloc(self.nbytes) if off is None else off
        ap = arena.ap[:, self.off:self.off + self.nbytes].bitcast(dtype)
        if len(dims) == 2:
            ap = ap.rearrange("p (a b) -> p a b", a=dims[0])
        elif len(dims) == 3:
            ap = ap.rearrange("p (a b c) -> p a b c", a=dims[0], b=dims[1])
        self.ap = ap

    def full(self):
        return V(self.ap, scells(self.off, self.nbytes))

    def v(self, ap, start_elem, nelem):
        return V(ap, scells(self.off + start_elem * self.esz, nelem * self.esz))

    def vs(self, ap, segs):
        cells = []
        for (s, n) in segs:
            cells.extend(scells(self.off + s * self.esz, n * self.esz))
        return V(ap, cells)


class K:
    pass


def build(cfg):
    nc = bass.Bass("TRN2", target_bir_lowering=False)
    P = Prog()
    global LASTP
    LASTP = P
    k = K()
    k.nc, k.P = nc, P
    dr = {}

    def dram_in(name, shape, dtype=F32):
        dr[name] = nc.dram_tensor(name, list(shape), dtype, kind="ExternalInput")
        return dr[name]

    x_d = dram_in("x", [T, D])
    mem_d = dram_in("mem", [256, D])
    par_d = dram_in("par", [128, NPAR])
    bcp_d = dram_in("bcp", [128, NBCP])
    def wget(nm):
        if nm not in dr:
            dram_in(nm, WEIGHT_SHAPES[nm])
        return dr[nm]
    k.w = wget
    out_d = nc.dram_tensor("out", [T, D], F32, kind="ExternalOutput")
    k.dr = dr

    stack = contextlib.ExitStack()
    with stack:
        arena_t = stack.enter_context(nc.sbuf_tensor("arena", [128, ARENA_BYTES], U8))
        psum = [stack.enter_context(nc.psum_tensor(f"ps{i}", [128, 512], F32)) for i in range(8)]
        esem = {e: stack.enter_context(nc.semaphore("s" + e)) for e in Prog.ENGS}
        dsem = {q: [stack.enter_context(nc.semaphore(f"d{q}{i}")) for i in range(Prog.NDSEM)]
                for q in ("SP", "POOL", "ACT")}
        ccsem = stack.enter_context(nc.semaphore("cc"))
        block = stack.enter_context(nc.Block())

        A = Arena(arena_t)
        k.A = A
        k.psum = psum
        k.ps_i = 0

        emit_program(k, cfg, x_d, mem_d, par_d, bcp_d, out_d)
        P.finalize()
        nc.used_inputs = list(dr.keys())

        @block.tensor
        def _(e):
            P.emit("PE", e, esem, dsem, ccsem)

        @block.scalar
        def _(e):
            P.emit("ACT", e, esem, dsem, ccsem)

        @block.vector
        def _(e):
            P.emit("DVE", e, esem, dsem, ccsem)

        @block.gpsimd
        def _(e):
            P.emit("POOL", e, esem, dsem, ccsem)

        @block.sync
        def _(e):
            P.emit("SP", e, esem, dsem, ccsem)
    return nc


PAR = {}
_c = 0
for _i in range(4):
    for _nm in ("mixn", "memn", "ffnn"):
        PAR[(_nm, _i)] = _c
        _c += 8
for _j in range(2):
    PAR[("bgate", _j)] = _c
    _c += 4
PAR["flag"] = _c
_c += 1
NPAR = _c
BCP = {"fnorm": 0}
_c = 1024
for _j in range(2):
    BCP[("hnorm", _j)] = _c
    _c += 256
NBCP = _c

WEIGHT_SHAPES = {}
for _i in range(4):
    WEIGHT_SHAPES[f"ffn_in{_i}"] = [44, 128, 1024]
    WEIGHT_SHAPES[f"ffn_out{_i}"] = [22, 128, 1024]
    WEIGHT_SHAPES[f"kvk{_i}"] = [8, 128, 1024]
    WEIGHT_SHAPES[f"kvv{_i}"] = [2, 128, 4096]
    WEIGHT_SHAPES[f"xq{_i}"] = [8, 128, 1024]
    WEIGHT_SHAPES[f"wo{_i}"] = [16 if _i % 2 == 0 else 24, 128, 1024]
for _j in range(2):
    WEIGHT_SHAPES[f"gq{_j}"] = [4, 128, 1024]
    WEIGHT_SHAPES[f"gk{_j}"] = [4, 128, 1024]
    WEIGHT_SHAPES[f"gv{_j}"] = [4, 128, 2048]
    WEIGHT_SHAPES[f"gg{_j}"] = [4, 128, 2048]
    WEIGHT_SHAPES[f"glr{_j}"] = [128, 128]
    WEIGHT_SHAPES[f"gw2{_j}"] = [16, 512]
    WEIGHT_SHAPES[f"sz{_j}"] = [8, 128, 2048]
    WEIGHT_SHAPES[f"sxbc{_j}"] = [32, 128, 1024]
    WEIGHT_SHAPES[f"sdt{_j}"] = [128, 1024]
    WEIGHT_SHAPES[f"scv{_j}"] = [128, 164]
    WEIGHT_SHAPES[f"sdk{_j}"] = [128, 2048]
    WEIGHT_SHAPES[f"snf{_j}"] = [128, 16]


def blk_layout(w, ncol=128):
    K_, C = w.shape
    nb = C // ncol
    a = w.reshape(K_ // 128, 128, nb, ncol).transpose(2, 1, 0, 3)
    return np.ascontiguousarray(a.reshape(nb, 128, (K_ // 128) * ncol))


def fm_vec(v):
    return np.ascontiguousarray(v.reshape(-1, 128).T)


def prep_inputs(inp, used=None):
    f = lambda a: np.asarray(a, dtype=np.float32)
    need = lambda nm: used is None or nm in used
    shared = {}
    par = np.zeros((128, NPAR), np.float32)
    for i in range(4):
        par[:, PAR[("mixn", i)]:PAR[("mixn", i)] + 8] = fm_vec(f(inp["mix_norm"])[i])
        par[:, PAR[("memn", i)]:PAR[("memn", i)] + 8] = fm_vec(f(inp["mem_norm"])[i])
        par[:, PAR[("ffnn", i)]:PAR[("ffnn", i)] + 8] = fm_vec(f(inp["ffn_norm"])[i])
    for j in range(2):
        par[:, PAR[("bgate", j)]:PAR[("bgate", j)] + 4] = fm_vec(f(inp["gla_b_gate"])[j])
    bcp = np.zeros((128, NBCP), np.float32)
    bcp[:, 0:1024] = f(inp["final_norm"])[None, :]
    for j in range(2):
        bcp[:, BCP[("hnorm", j)]:BCP[("hnorm", j)] + 256] = f(inp["gla_head_norm"])[j][None, :]
    shared["bcp"] = bcp
    for i in range(4):
        if need(f"ffn_in{i}"):
            shared[f"ffn_in{i}"] = blk_layout(f(inp["ffn_w_in"])[i])
            shared[f"ffn_out{i}"] = np.ascontiguousarray(f(inp["ffn_w_out"])[i].reshape(22, 128, 1024))
        if need(f"kvk{i}"):
            wkv = f(inp["w_mem_kv"])[i]
            shared[f"kvk{i}"] = blk_layout(wkv[:, 0:1024])
            shared[f"kvv{i}"] = blk_layout(wkv[:, 1024:2048], 512)
            j = i // 2
            if i % 2 == 0:
                w = f(inp["gla_w_in"])[j]
                shared[f"xq{i}"] = blk_layout(w[:, 3088:4112])
                shared[f"wo{i}"] = np.ascontiguousarray(f(inp["gla_w_out"])[j].reshape(16, 128, 1024))
                shared[f"gq{j}"] = blk_layout(w[:, 0:512])
                shared[f"gk{j}"] = blk_layout(w[:, 512:1024])
                shared[f"gv{j}"] = blk_layout(w[:, 1024:2048], 256)
                shared[f"gg{j}"] = blk_layout(w[:, 2048:3072], 256)
                shared[f"glr{j}"] = blk_layout(w[:, 3072:3088], 16)[0]
                shared[f"gw2{j}"] = np.ascontiguousarray(f(inp["gla_w_gate2"])[j])
            else:
                w = f(inp["ssd_w_in"])[j]
                shared[f"xq{i}"] = blk_layout(w[:, 6176:7200])
                shared[f"wo{i}"] = np.ascontiguousarray(f(inp["ssd_w_out"])[j].reshape(24, 128, 1024))
                shared[f"sz{j}"] = blk_layout(w[:, 0:2048], 256)
                shared[f"sxbc{j}"] = blk_layout(w[:, 2048:6144])
                shared[f"sdt{j}"] = blk_layout(np.tile(w[:, 6144:6176], (1, 4)))[0]
                scv = np.zeros((128, 164), np.float32)
                cw = f(inp["ssd_conv_w"])[j]
                scv[:, 0:128] = cw.reshape(4, 32, 128).transpose(2, 1, 0).reshape(128, 128)
                scv[:, 128:160] = fm_vec(f(inp["ssd_conv_b"])[j])
                scv[:, 160] = np.tile(f(inp["ssd_dt_bias"])[j], 4)
                scv[:, 161] = np.tile(f(inp["ssd_a_log"])[j], 4)
                shared[f"scv{j}"] = scv
                shared[f"sdk{j}"] = np.ascontiguousarray(np.broadcast_to(np.repeat(f(inp["ssd_d"])[j], 64)[None, :], (128, 2048)))
                shared[f"snf{j}"] = fm_vec(f(inp["ssd_norm"])[j])
    x = f(inp["x"])
    mem = f(inp["mem"])
    maps = []
    for c in range(8):
        b, half = c // 2, c % 2
        m = dict(shared)
        p = par.copy()
        p[:, PAR["flag"]] = float(half)
        m["par"] = p
        m["x"] = np.ascontiguousarray(x[b, half * T:(half + 1) * T])
        m["mem"] = np.ascontiguousarray(mem[b])
        maps.append(m)
    return maps


def ps_bank(k):
    i = k.ps_i
    k.ps_i = (i + 1) % 8
    return i


def PSV(k, i, ap=None):
    return V(k.psum[i][:, :] if ap is None else ap, [("P", i)])


def emit_program(k, cfg, x_d, mem_d, par_d, bcp_d, out_d):
    P, A, nc = k.P, k.A, k.nc
    k.H = Tile(A, [NT, D], F32)
    k.PAR = Tile(A, [NPAR], F32)
    k.IDB = Tile(A, [128], BF16)
    k.IDF = Tile(A, [128], F32)
    k.RS = Tile(A, [NT], F32)
    k.SS = Tile(A, [NT], F32)
    k.XN = Tile(A, [KC, T], BF16)
    k.JUNK = Tile(A, [D], BF16)
    k.HS = Tile(A, [D], BF16)
    H = k.H

    for j in range(NT):
        hv = H.v(H.ap[:, j, :], j * D, D)
        P.dma("SP", hv.ap, x_d[j * 128:(j + 1) * 128, :], writes=[hv])
    P.dma("SP", k.PAR.ap, par_d[:, :], writes=[k.PAR.full()])

    for idt in (k.IDB, k.IDF):
        fv = idt.full()
        P.op("POOL", lambda e, a=idt.ap: e.memset(a, 0.0), writes=[fv])
        P.op("POOL", lambda e, a=idt.ap: e.affine_select(
            out=a, in_=a, pattern=[[-1, 128]], compare_op=ALU.not_equal, fill=1.0,
            base=0, channel_multiplier=1), reads=[fv], writes=[fv])

    k.ONE1 = Tile(A, [1], F32)
    k.LNQ = Tile(A, [1], F32)
    k.EPSC = Tile(A, [1], F32)
    P.op("POOL", lambda e: e.memset(k.EPSC.ap, EPS), writes=[k.EPSC.full()])
    P.op("POOL", lambda e: e.memset(k.ONE1.ap, 1.0), writes=[k.ONE1.full()])
    P.op("POOL", lambda e: e.memset(k.LNQ.ap, float(np.log(128.0 ** -0.5))), writes=[k.LNQ.full()])
    emit_consts(k)
    m0 = A.mark()
    for sl in cfg["sublayers"]:
        kind, i = sl
        A.release(m0)
        if kind == "ffn":
            emit_ffn(k, i)
        elif kind == "mix":
            emit_mixer(k, i, mem_d)
    A.release(m0)
    emit_output(k, cfg, bcp_d, out_d)


def hview(k, j, c0=0, n=D):
    H = k.H
    return H.v(H.ap[:, j, c0:c0 + n], j * D + c0, n)


def emit_norm_T(k, srcs, dst, dst_T, wcol):
    P = k.P
    SS, RS = k.SS, k.RS
    n = len(srcs)
    ssf, rsf = SS.full(), RS.full()
    junk = k.JUNK.full()
    for j, hv in enumerate(srcs):
        P.op("ACT", lambda e, hv=hv, j=j: e.activation(
            out=k.JUNK.ap, in_=hv.ap, func=AF.Square, accum_out=SS.ap[:, j:j + 1]),
            reads=[hv], writes=[junk, ssf])
    P.op("DVE", lambda e: e.tensor_scalar(out=RS.ap[:, 0:n], in0=SS.ap[:, 0:n], scalar1=1.0 / D, scalar2=EPS,
                                           op0=ALU.mult, op1=ALU.add), reads=[ssf], writes=[rsf])
    P.op("DVE", lambda e: e.reciprocal(out=RS.ap[:, 0:n], in_=RS.ap[:, 0:n]), reads=[rsf], writes=[rsf])
    P.op("ACT", lambda e: e.activation(out=RS.ap[:, 0:n], in_=RS.ap[:, 0:n], func=AF.Sqrt), reads=[rsf], writes=[rsf])
    hsf = k.HS.full()
    wv = k.PAR.v(k.PAR.ap[:, wcol:wcol + 8], wcol, 8)
    for j, hv in enumerate(srcs):
        P.op("ACT", lambda e, hv=hv, j=j: e.activation(
            out=k.HS.ap, in_=hv.ap, func=AF.Copy, scale=RS.ap[:, j:j + 1]),
            reads=[hv, rsf], writes=[hsf])
        b = ps_bank(k)
        pb = k.psum[b][:, :].bitcast(BF16)
        pv = PSV(k, b)
        for kc in range(KC):
            P.op("PE", lambda e, kc=kc, pb=pb: e.transpose(
                out=pb[:, kc * 128:(kc + 1) * 128], in_=k.HS.ap[:, kc * 128:(kc + 1) * 128],
                identity=k.IDB.ap), reads=[hsf, k.IDB.full()], writes=[pv])
        xv = dst.vs(dst.ap[:, :, j * 128:(j + 1) * 128], [(kc * dst_T + j * 128, 128) for kc in range(KC)])
        P.op("DVE", lambda e, pb=pb, xv=xv, wv=wv: e.tensor_tensor(
            out=xv.ap, in0=pb.rearrange("p (a b) -> p a b", a=KC),
            in1=wv.ap.unsqueeze(2).to_broadcast([128, KC, 128]), op=ALU.mult),
            reads=[pv, wv], writes=[xv])


def emit_rmsnorm_xn(k, wcol):
    emit_norm_T(k, [hview(k, j) for j in range(NT)], k.XN, T, wcol)


def xn_blk(k, kc, t0, n):
    XN = k.XN
    return XN.v(XN.ap[:, kc, t0:t0 + n], kc * T + t0, n)


def emit_ffn(k, i):
    P, A, dr = k.P, k.A, k.dr
    emit_rmsnorm_xn(k, PAR[("ffnn", i)])
    NG = 11
    ACTT = Tile(A, [NG, T], BF16)
    WOs = [Tile(A, [NG, D], BF16) for _ in range(2)]
    WR = [Tile(A, [KC, 128], BF16) for _ in range(4)]
    SG = [Tile(A, [512], F32) for _ in range(2)]
    win, wout = k.w(f"ffn_in{i}"), k.w(f"ffn_out{i}")
    wr_i = 0
    sg_i = 0
    for grp in range(2):
        WO = WOs[grp]
        for c in range(NG):
            wov = WO.v(WO.ap[:, c, :], c * D, D)
            P.dma("POOL", wov.ap, wout[grp * NG + c, :, :], writes=[wov])
    for grp in range(2):
        WO = WOs[grp]
        for c in range(NG):
            blk = grp * NG + c
            wg, wu = WR[wr_i % 4], WR[(wr_i + 1) % 4]
            wr_i += 2
            P.dma("POOL", wg.ap.rearrange("p a b -> p (a b)"), win[blk, :, :], writes=[wg.full()])
            P.dma("POOL", wu.ap.rearrange("p a b -> p (a b)"), win[22 + blk, :, :], writes=[wu.full()])
            for tb in range(4):
                bg, bu = ps_bank(k), ps_bank(k)
                for (w_, b_) in ((wg, bg), (wu, bu)):
                    for kc in range(KC):
                        xv = xn_blk(k, kc, tb * 512, 512)
                        P.op("PE", lambda e, w_=w_, b_=b_, kc=kc, xv=xv: e.matmul(
                            k.psum[b_][:, :], lhsT=w_.ap[:, kc, :], rhs=xv.ap,
                            start=(kc == 0), stop=(kc == KC - 1)),
                            reads=[w_.full(), xv], writes=[PSV(k, b_)])
                sg = SG[sg_i % 2]
                sg_i += 1
                P.op("ACT", lambda e, sg=sg, bg=bg: e.activation(
                    out=sg.ap, in_=k.psum[bg][:, :], func=AF.Silu),
                    reads=[PSV(k, bg)], writes=[sg.full()])
                av = ACTT.v(ACTT.ap[:, c, tb * 512:(tb + 1) * 512], c * T + tb * 512, 512)
                P.op("DVE", lambda e, sg=sg, bu=bu, av=av: e.tensor_tensor(
                    out=av.ap, in0=k.psum[bu][:, :], in1=sg.ap, op=ALU.mult),
                    reads=[PSV(k, bu), sg.full()], writes=[av])
        for j in range(NT):
            for nb in range(2):
                b = ps_bank(k)
                for c in range(NG):
                    av = ACTT.v(ACTT.ap[:, c, j * 128:(j + 1) * 128], c * T + j * 128, 128)
                    wov = WO.v(WO.ap[:, c, nb * 512:(nb + 1) * 512], c * D + nb * 512, 512)
                    P.op("PE", lambda e, av=av, wov=wov, b=b, c=c: e.matmul(
                        k.psum[b][:, :], lhsT=av.ap, rhs=wov.ap, start=(c == 0), stop=(c == NG - 1)),
                        reads=[av, wov], writes=[PSV(k, b)])
                hv = hview(k, j, nb * 512, 512)
                P.op("DVE", lambda e, hv=hv, b=b: e.tensor_tensor(
                    out=hv.ap, in0=k.psum[b][:, :], in1=hv.ap, op=ALU.add),
                    reads=[PSV(k, b), hv], writes=[hv])


def load_w(k, tile, src, queue="POOL"):
    ap = tile.ap
    if len(tile.dims) == 2:
        ap = ap.rearrange("p a b -> p (a b)")
    elif len(tile.dims) == 3:
        ap = ap.rearrange("p a b c -> p (a b c)")
    return k.P.dma(queue, ap, src, writes=[tile.full()], max_dma_last_dim=4096)


def inproj_fm(k, wt, t0, n, M=128, wcol0=0):
    b = ps_bank(k)
    for kc in range(KC):
        xv = xn_blk(k, kc, t0, n)
        k.P.op("PE", lambda e, kc=kc, xv=xv, b=b: e.matmul(
            k.psum[b][0:M, 0:n], lhsT=wt.ap[:, kc, wcol0:wcol0 + M], rhs=xv.ap,
            start=(kc == 0), stop=(kc == KC - 1)), reads=[wt.full(), xv], writes=[PSV(k, b)])
    return b


def inproj_tm(k, wt, j, ncols, src=None, srcT=T):
    b = ps_bank(k)
    src = src or k.XN
    for kc in range(KC):
        xv = src.v(src.ap[:, kc, j * 128:(j + 1) * 128], kc * srcT + j * 128, 128)
        k.P.op("PE", lambda e, kc=kc, xv=xv, b=b: e.matmul(
            k.psum[b][:, 0:ncols], lhsT=xv.ap, rhs=wt.ap[:, kc, 0:ncols],
            start=(kc == 0), stop=(kc == KC - 1)), reads=[wt.full(), xv], writes=[PSV(k, b)])
    return b


def emit_consts(k):
    P, A = k.P, k.A
    k.MASKU = Tile(A, [128], F32)
    k.ONESB = Tile(A, [128], BF16)
    fv = k.MASKU.full()
    P.op("POOL", lambda e: e.memset(k.MASKU.ap, 1.0), writes=[fv])
    P.op("POOL", lambda e: e.affine_select(
        out=k.MASKU.ap, in_=k.MASKU.ap, pattern=[[1, 128]], compare_op=ALU.is_ge, fill=0.0,
        base=0, channel_multiplier=-1), reads=[fv], writes=[fv])
    P.op("POOL", lambda e: e.memset(k.ONESB.ap, 1.0), writes=[k.ONESB.full()])
    k.FLAG = k.PAR.v(k.PAR.ap[:, PAR["flag"]:PAR["flag"] + 1], PAR["flag"], 1)


def emit_mixer(k, i, mem_d):
    P, A = k.P, k.A
    emit_rmsnorm_xn(k, PAR[("mixn", i)])
    xctx = None
    if i % 2 == 1:
        XH32 = Tile(A, [KC, 3], F32)
        XHR = Tile(A, [KC * 3], F32)
        k.XHB = Tile(A, [KC, 3], BF16)
        xl = k.XN.vs(k.XN.ap[:, :, T - 3:T], [(kc * T + T - 3, 3) for kc in range(KC)])
        P.op("ACT", lambda e: e.activation(out=XH32.ap, in_=xl.ap, func=AF.Copy), reads=[xl], writes=[XH32.full()])
        xctx = exchange_start(k, V(XH32.ap.rearrange("p a b -> p (a b)"), XH32.full().cells), 128, KC * 3, f"hx{i}")
    m = A.mark()
    emit_xattn(k, i, mem_d)
    A.release(m)
    if xctx is not None:
        exchange_finish(k, xctx, XHR)
        P.op("ACT", lambda e: e.activation(out=k.XHB.ap.rearrange("p a b -> p (a b)"), in_=XHR.ap, func=AF.Copy),
             reads=[XHR.full()], writes=[k.XHB.full()])
        m = A.mark()
    if i % 2 == 0:
        emit_gla(k, i)
    else:
        emit_ssd(k, i)
    A.release(m)


def emit_xattn(k, i, mem_d):
    P, A = k.P, k.A
    mixdim = 1024 if i % 2 == 0 else 2048
    MEMT = Tile(A, [2, D], F32)
    MN = Tile(A, [KC, 256], BF16)
    KmT = Tile(A, [8, 256], BF16)
    Vm = Tile(A, [2, 1024], BF16)
    WR = [Tile(A, [KC, 128], BF16) for _ in range(3)]
    m1 = A.mark()
    WV = Tile(A, [KC, 512], BF16)
    srcs = []
    for mt in range(2):
        mv = MEMT.v(MEMT.ap[:, mt, :], mt * D, D)
        P.dma("SP", mv.ap, mem_d[mt * 128:(mt + 1) * 128, :], writes=[mv])
        srcs.append(mv)
    emit_norm_T(k, srcs, MN, 256, PAR[("memn", i)])
    kvk, kvv = k.w(f"kvk{i}"), k.w(f"kvv{i}")
    wi = 0
    for blk in range(8):
        w = WR[wi % 3]
        wi += 1
        load_w(k, w, kvk[blk, :, :])
        b = ps_bank(k)
        for kc in range(KC):
            mv = MN.v(MN.ap[:, kc, :], kc * 256, 256)
            P.op("PE", lambda e, kc=kc, mv=mv, b=b, w=w: e.matmul(
                k.psum[b][:, 0:256], lhsT=w.ap[:, kc, :], rhs=mv.ap, start=(kc == 0), stop=(kc == KC - 1)),
                reads=[w.full(), mv], writes=[PSV(k, b)])
        kv = KmT.v(KmT.ap[:, blk, :], blk * 256, 256)
        P.op("ACT", lambda e, kv=kv, b=b: e.activation(out=kv.ap, in_=k.psum[b][:, 0:256], func=AF.Copy,
                                                         scale=1.0 / 16.0), reads=[PSV(k, b)], writes=[kv])
    for nb in range(2):
        load_w(k, WV, kvv[nb, :, :])
        for mt in range(2):
            b = inproj_tm(k, WV, mt, 512, src=MN, srcT=256)
            vv = Vm.v(Vm.ap[:, mt, nb * 512:(nb + 1) * 512], mt * 1024 + nb * 512, 512)
            P.op("ACT", lambda e, vv=vv, b=b: e.activation(out=vv.ap, in_=k.psum[b][:, :], func=AF.Copy),
                 reads=[PSV(k, b)], writes=[vv])
    A.release(m1)
    XQT = Tile(A, [2, T], BF16)
    XAT = Tile(A, [8, T], BF16)
    ET = [Tile(A, [512], BF16) for _ in range(2)]
    RD = Tile(A, [512], F32)
    WOX = Tile(A, [8, D], BF16)
    xqw, wo = k.w(f"xq{i}"), k.w(f"wo{i}")
    for c in range(8):
        wv = WOX.v(WOX.ap[:, c, :], c * D, D)
        P.dma("POOL", wv.ap, wo[mixdim // 128 + c, :, :], writes=[wv], max_dma_last_dim=4096)
    for a in range(4):
        for dc in range(2):
            w = WR[wi % 3]
            wi += 1
            load_w(k, w, xqw[a * 2 + dc, :, :])
            for tb in range(4):
                b = inproj_fm(k, w, tb * 512, 512)
                xv = XQT.v(XQT.ap[:, dc, tb * 512:(tb + 1) * 512], dc * T + tb * 512, 512)
                P.op("ACT", lambda e, xv=xv, b=b: e.activation(out=xv.ap, in_=k.psum[b][:, :], func=AF.Copy),
                     reads=[PSV(k, b)], writes=[xv])
        for tb in range(4):
            for mt in range(2):
                b = ps_bank(k)
                for dc in range(2):
                    kv = KmT.v(KmT.ap[:, a * 2 + dc, mt * 128:(mt + 1) * 128], (a * 2 + dc) * 256 + mt * 128, 128)
                    xv = XQT.v(XQT.ap[:, dc, tb * 512:(tb + 1) * 512], dc * T + tb * 512, 512)
                    P.op("PE", lambda e, kv=kv, xv=xv, b=b, dc=dc: e.matmul(
                        k.psum[b][:, :], lhsT=kv.ap, rhs=xv.ap, start=(dc == 0), stop=(dc == 1)),
                        reads=[kv, xv], writes=[PSV(k, b)])
                P.op("ACT", lambda e, b=b, mt=mt: e.activation(out=ET[mt].ap, in_=k.psum[b][:, :], func=AF.Exp),
                     reads=[PSV(k, b)], writes=[ET[mt].full()])
            b = ps_bank(k)
            for mt in range(2):
                P.op("PE", lambda e, b=b, mt=mt: e.matmul(
                    k.psum[b][:, :], lhsT=k.ONESB.ap, rhs=ET[mt].ap, start=(mt == 0), stop=(mt == 1)),
                    reads=[k.ONESB.full(), ET[mt].full()], writes=[PSV(k, b)])
            P.op("DVE", lambda e, b=b: e.reciprocal(out=RD.ap, in_=k.psum[b][:, :]),
                 reads=[PSV(k, b)], writes=[RD.full()])
            for dc in range(2):
                b = ps_bank(k)
                for mt in range(2):
                    c0 = a * 256 + dc * 128
                    vv = Vm.v(Vm.ap[:, mt, c0:c0 + 128], mt * 1024 + c0, 128)
                    P.op("PE", lambda e, vv=vv, b=b, mt=mt: e.matmul(
                        k.psum[b][:, :], lhsT=vv.ap, rhs=ET[mt].ap, start=(mt == 0), stop=(mt == 1)),
                        reads=[vv, ET[mt].full()], writes=[PSV(k, b)])
                xav = XAT.v(XAT.ap[:, a * 2 + dc, tb * 512:(tb + 1) * 512], (a * 2 + dc) * T + tb * 512, 512)
                P.op("DVE", lambda e, xav=xav, b=b: e.tensor_tensor(
                    out=xav.ap, in0=k.psum[b][:, :], in1=RD.ap, op=ALU.mult),
                    reads=[PSV(k, b), RD.full()], writes=[xav])
    emit_outproj(k, XAT, WOX, 8)


def emit_outproj(k, XT, WO, nk, scale_col=None):
    P = k.P
    for j in range(NT):
        for nb in range(2):
            b = ps_bank(k)
            for c in range(nk):
                av = XT.v(XT.ap[:, c, j * 128:(j + 1) * 128], c * T + j * 128, 128)
                wv = WO.v(WO.ap[:, c, nb * 512:(nb + 1) * 512], c * D + nb * 512, 512)
                P.op("PE", lambda e, av=av, wv=wv, b=b, c=c: e.matmul(
                    k.psum[b][:, :], lhsT=av.ap, rhs=wv.ap, start=(c == 0), stop=(c == nk - 1)),
                    reads=[av, wv], writes=[PSV(k, b)])
            hv = hview(k, j, nb * 512, 512)
            if scale_col is None:
                P.op("DVE", lambda e, hv=hv, b=b: e.tensor_tensor(
                    out=hv.ap, in0=k.psum[b][:, :], in1=hv.ap, op=ALU.add),
                    reads=[PSV(k, b), hv], writes=[hv])
            else:
                sc = scale_col(j)
                P.op("DVE", lambda e, hv=hv, b=b, sc=sc: e.scalar_tensor_tensor(
                    out=hv.ap, in0=k.psum[b][:, :], scalar=sc.ap, in1=hv.ap, op0=ALU.mult, op1=ALU.add),
                    reads=[PSV(k, b), hv, sc], writes=[hv])


def exchange_start(k, src_v, rows, cols, name):
    P, nc = k.P, k.nc
    ib = nc.dram_tensor(name + "_i", [rows, cols], F32)
    ob = nc.dram_tensor(name + "_o", [2 * rows, cols], F32)
    ci, co = V(None, [("D", name + "_i")]), V(None, [("D", name + "_o")])
    P.dma("POOL", ib[:, :], src_v.ap, reads=[src_v], writes=[ci])
    P.coll(lambda e: e.collective_compute("AllGather", ALU.bypass, replica_groups=PAIRS,
                                          ins=[ib.ap().opt()], outs=[ob.ap().opt()]),
           reads=[ci], writes=[co])
    return (ob, co, rows)


def exchange_finish(k, ctx, dst_tile):
    P = k.P
    ob, co, rows = ctx
    dv = V(dst_tile.ap[0:rows], dst_tile.full().cells)
    P.dma("POOL", dv.ap, ob[0:rows, :], reads=[co], writes=[dv])
    P.op("DVE", lambda e: e.tensor_scalar(out=dv.ap, in0=dv.ap, scalar1=k.FLAG.ap[0:rows], scalar2=None, op0=ALU.mult),
         reads=[dv, k.FLAG], writes=[dv])


def exchange(k, src_v, rows, cols, dst_tile, name):
    exchange_finish(k, exchange_start(k, src_v, rows, cols, name), dst_tile)


def emit_gla(k, i):
    P, A = k.P, k.A
    j = i // 2
    GLRT = Tile(A, [T], BF16)
    W2 = Tile(A, [512], BF16)
    SPL = Tile(A, [T], F32)
    CS = Tile(A, [T], F32)
    FQ = Tile(A, [512], F32)
    FK = Tile(A, [512], F32)
    FE = Tile(A, [512], F32)
    QT = Tile(A, [T], BF16, off=SPL.off)
    QT2 = Tile(A, [T], BF16, off=SPL.off + T * 2)
    KNT = Tile(A, [T], BF16)
    KET = Tile(A, [T], BF16)
    KE = Tile(A, [NT, 128], BF16)
    Vt = Tile(A, [NT, 256], BF16)
    Gt = Tile(A, [NT, 256], BF16)
    SLOC = Tile(A, [NT, 256], BF16)
    SM = Tile(A, [16 * 6 + 8], F32)
    CSS = SM.ap[:, 0:16]
    CSE = SM.ap[:, 16:32]
    DEC = SM.ap[:, 32:48]
    DCUM = SM.ap[:, 48:64]
    NBG = SM.ap[:, 64:68]
    SSo = SM.ap[:, 68:69]
    RSo = SM.ap[:, 69:70]
    smf = SM.full()
    TTs = [Tile(A, [256], F32) for _ in range(4)]
    MTOKs = [Tile(A, [256], BF16) for _ in range(4)]
    STs = [Tile(A, [128], BF16) for _ in range(4)]
    SRs = [Tile(A, [2], F32) for _ in range(4)]
    MIXT = Tile(A, [2, T], BF16)
    WOH = Tile(A, [2, D], BF16)
    HNB = Tile(A, [256], F32)
    S = Tile(A, [256], F32)
    SB32 = Tile(A, [256], F32)
    SBb = Tile(A, [256], BF16)
    WA = Tile(A, [KC, 128], BF16)
    WB = Tile(A, [KC, 128], BF16)
    WC = Tile(A, [KC, 256], BF16)
    WD = Tile(A, [KC, 256], BF16)
    gq, gk, gv, gg = k.w(f"gq{j}"), k.w(f"gk{j}"), k.w(f"gv{j}"), k.w(f"gg{j}")
    wo = k.w(f"wo{i}")
    bcp = k.dr["bcp"]
    P.dma("SP", HNB.ap, bcp[:, BCP[("hnorm", j)]:BCP[("hnorm", j)] + 256], writes=[HNB.full()])
    bg = PAR[("bgate", j)]
    pv = k.PAR.v(k.PAR.ap[:, bg:bg + 4], bg, 4)
    P.op("DVE", lambda e: e.tensor_scalar(out=NBG, in0=pv.ap, scalar1=-1.0, scalar2=None, op0=ALU.mult),
         reads=[pv], writes=[smf])
    WGL = Tile(A, [KC, 16], BF16)
    load_w(k, WGL, k.w(f"glr{j}")[:, :])
    P.dma("POOL", W2.ap[0:16, :], k.w(f"gw2{j}")[:, :], writes=[W2.full()])
    for tb in range(4):
        b = inproj_fm(k, WGL, tb * 512, 512, M=16)
        gv_ = GLRT.v(GLRT.ap[0:16, tb * 512:(tb + 1) * 512], tb * 512, 512)
        P.op("ACT", lambda e, gv_=gv_, b=b: e.activation(out=gv_.ap, in_=k.psum[b][0:16, :], func=AF.Copy),
             reads=[PSV(k, b)], writes=[gv_])
    for h in range(4):
        for tb in range(4):
            b = ps_bank(k)
            gv_ = GLRT.v(GLRT.ap[0:16, tb * 512:(tb + 1) * 512], tb * 512, 512)
            P.op("PE", lambda e, b=b, gv_=gv_, h=h: e.matmul(
                k.psum[b][:, :], lhsT=W2.ap[0:16, h * 128:(h + 1) * 128], rhs=gv_.ap, start=True, stop=True),
                reads=[W2.full(), gv_], writes=[PSV(k, b)])
            sv = SPL.v(SPL.ap[:, tb * 512:(tb + 1) * 512], tb * 512, 512)
            P.op("ACT", lambda e, b=b, sv=sv, h=h: e.activation(
                out=sv.ap, in_=k.psum[b][:, :], func=AF.Exp, scale=-1.0, bias=NBG[:, h:h + 1]),
                reads=[PSV(k, b), smf], writes=[sv])
        P.op("ACT", lambda e: e.activation(out=SPL.ap, in_=SPL.ap, func=AF.Ln, bias=1.0),
             reads=[SPL.full()], writes=[SPL.full()])
        P.op("DVE", lambda e: e.tensor_tensor_scan(
            out=CS.ap, data0=k.ONE1.ap.to_broadcast([128, T]), data1=SPL.ap, initial=0.0,
            op0=ALU.mult, op1=ALU.add), reads=[SPL.full(), k.ONE1.full()], writes=[CS.full()])
        cs3 = CS.ap.rearrange("p (n c) -> p n c", c=128)
        P.op("DVE", lambda e: e.memset(SM.ap[:, 0:1], 0.0), writes=[smf])
        P.op("DVE", lambda e: e.tensor_copy(out=SM.ap[:, 1:16], in_=cs3[:, 0:15, 127]), reads=[CS.full()], writes=[smf])
        P.op("DVE", lambda e: e.tensor_copy(out=CSE, in_=cs3[:, :, 127]), reads=[CS.full()], writes=[smf])
        P.op("DVE", lambda e: e.tensor_tensor(out=DEC, in0=CSE, in1=CSS, op=ALU.subtract), reads=[smf], writes=[smf])
        P.op("ACT", lambda e: e.activation(out=DEC, in_=DEC, func=AF.Exp, scale=-1.0 / 16), reads=[smf], writes=[smf])
        P.op("ACT", lambda e: e.activation(out=DCUM, in_=CSS, func=AF.Exp, scale=-1.0 / 16), reads=[smf], writes=[smf])
        P.op("DVE", lambda e: e.tensor_tensor(
            out=cs3, in0=cs3, in1=CSS.unsqueeze(2).to_broadcast([128, NT, 128]), op=ALU.subtract),
            reads=[CS.full(), smf], writes=[CS.full()])
        load_w(k, WB, gk[h, :, :])
        for tb in range(4):
            dv = CS.v(CS.ap[:, tb * 512:(tb + 1) * 512], tb * 512, 512)
            P.op("ACT", lambda e, dv=dv: e.activation(out=FK.ap, in_=dv.ap, func=AF.Exp, scale=1.0 / 16),
                 reads=[dv], writes=[FK.full()])
            fe3 = FE.ap.rearrange("p (n c) -> p n c", c=128)
            d3 = dv.ap.rearrange("p (n c) -> p n c", c=128)
            P.op("DVE", lambda e, d3=d3, fe3=fe3, tb=tb: e.tensor_tensor(
                out=fe3, in0=d3, in1=CSE[:, tb * 4:(tb + 1) * 4].unsqueeze(2).to_broadcast([128, 4, 128]),
                op=ALU.subtract), reads=[dv, smf], writes=[FE.full()])
            P.op("DVE", lambda e, fe3=fe3, tb=tb: e.tensor_tensor(
                out=fe3, in0=fe3, in1=CSS[:, tb * 4:(tb + 1) * 4].unsqueeze(2).to_broadcast([128, 4, 128]),
                op=ALU.add), reads=[FE.full(), smf], writes=[FE.full()])
            P.op("ACT", lambda e: e.activation(out=FE.ap, in_=FE.ap, func=AF.Exp, scale=1.0 / 16),
                 reads=[FE.full()], writes=[FE.full()])
            b = inproj_fm(k, WB, tb * 512, 512)
            kn = KNT.v(KNT.ap[:, tb * 512:(tb + 1) * 512], tb * 512, 512)
            ke = KET.v(KET.ap[:, tb * 512:(tb + 1) * 512], tb * 512, 512)
            P.op("DVE", lambda e, kn=kn, b=b: e.tensor_tensor(out=kn.ap, in0=k.psum[b][:, :], in1=FK.ap, op=ALU.mult),
                 reads=[PSV(k, b), FK.full()], writes=[kn])
            P.op("DVE", lambda e, ke=ke, b=b: e.tensor_tensor(out=ke.ap, in0=k.psum[b][:, :], in1=FE.ap, op=ALU.mult),
                 reads=[PSV(k, b), FE.full()], writes=[ke])
        for g8 in range(2):
            b = ps_bank(k)
            pb = k.psum[b][:, :].bitcast(BF16)
            for t8 in range(8):
                n = g8 * 8 + t8
                ke = KET.v(KET.ap[:, n * 128:(n + 1) * 128], n * 128, 128)
                P.op("PE", lambda e, ke=ke, pb=pb, t8=t8: e.transpose(
                    out=pb[:, t8 * 128:(t8 + 1) * 128], in_=ke.ap, identity=k.IDB.ap),
                    reads=[ke, k.IDB.full()], writes=[PSV(k, b)])
            kv = KE.v(KE.ap[:, g8 * 8:(g8 + 1) * 8, :], g8 * 8 * 128, 8 * 128)
            P.op("ACT", lambda e, kv=kv, pb=pb: e.activation(
                out=kv.ap, in_=pb.rearrange("p (a b) -> p a b", a=8), func=AF.Copy),
                reads=[PSV(k, b)], writes=[kv])
        load_w(k, WC, gv[h, :, :])
        P.op("DVE", lambda e: e.memset(S.ap, 0.0), writes=[S.full()])
        P.op("POOL", lambda e: e.memset(SLOC.ap[:, 0, :], 0.0), writes=[SLOC.v(SLOC.ap[:, 0, :], 0, 256)])

        def rec_step(n):
            b = ps_bank(k)
            kev = KE.v(KE.ap[:, n, :], n * 128, 128)
            vv = Vt.v(Vt.ap[:, n, :], n * 256, 256)
            P.op("PE", lambda e: e.matmul(k.psum[b][:, 0:256], lhsT=kev.ap, rhs=vv.ap, start=True, stop=True),
                 reads=[kev, vv], writes=[PSV(k, b)])
            P.op("DVE", lambda e: e.scalar_tensor_tensor(
                out=S.ap, in0=S.ap, scalar=DEC[:, n:n + 1], in1=k.psum[b][:, 0:256], op0=ALU.mult, op1=ALU.add),
                reads=[S.full(), smf, PSV(k, b)], writes=[S.full()])
            if n < NT - 1:
                sl = SLOC.v(SLOC.ap[:, n + 1, :], (n + 1) * 256, 256)
                P.op("ACT", lambda e: e.activation(out=sl.ap, in_=S.ap, func=AF.Copy),
                     reads=[S.full()], writes=[sl])

        for n in range(NT + 2):
            if n < NT:
                b = inproj_tm(k, WC, n, 256)
                vv = Vt.v(Vt.ap[:, n, :], n * 256, 256)
                P.op("ACT", lambda e, vv=vv, b=b: e.activation(out=vv.ap, in_=k.psum[b][:, 0:256], func=AF.Copy),
                     reads=[PSV(k, b)], writes=[vv])
            if 0 <= n - 2 < NT:
                rec_step(n - 2)
        xctx = exchange_start(k, S.full(), 128, 256, f"gx{i}_{h}")
        load_w(k, WA, gq[h, :, :])
        for tb in range(4):
            dv = CS.v(CS.ap[:, tb * 512:(tb + 1) * 512], tb * 512, 512)
            P.op("ACT", lambda e, dv=dv: e.activation(out=FQ.ap, in_=dv.ap, func=AF.Exp, scale=-1.0 / 16,
                                                      bias=k.LNQ.ap), reads=[dv, k.LNQ.full()], writes=[FQ.full()])
            b = inproj_fm(k, WA, tb * 512, 512)
            qv = QT.v(QT.ap[:, tb * 512:(tb + 1) * 512], tb * 512, 512)
            P.op("DVE", lambda e, qv=qv, b=b: e.tensor_tensor(out=qv.ap, in0=k.psum[b][:, :], in1=FQ.ap, op=ALU.mult),
                 reads=[PSV(k, b), FQ.full()], writes=[qv])
        P.op("DVE", lambda e: e.tensor_tensor(
            out=QT2.ap.rearrange("p (n c) -> p n c", c=128), in0=QT.ap.rearrange("p (n c) -> p n c", c=128),
            in1=DCUM.unsqueeze(2).to_broadcast([128, NT, 128]), op=ALU.mult),
            reads=[QT.full(), smf], writes=[QT2.full()])
        load_w(k, WD, gg[h, :, :])
        for n in range(NT):
            b = inproj_tm(k, WD, n, 256)
            gv2 = Gt.v(Gt.ap[:, n, :], n * 256, 256)
            P.op("ACT", lambda e, gv2=gv2, b=b: e.activation(out=gv2.ap, in_=k.psum[b][:, 0:256], func=AF.Silu),
                 reads=[PSV(k, b)], writes=[gv2])
        for c in range(2):
            wv = WOH.v(WOH.ap[:, c, :], c * D, D)
            P.dma("POOL", wv.ap, wo[h * 2 + c, :, :], writes=[wv], max_dma_last_dim=4096)
        exchange_finish(k, xctx, SB32)
        P.op("ACT", lambda e: e.activation(out=SBb.ap, in_=SB32.ap, func=AF.Copy), reads=[SB32.full()], writes=[SBb.full()])
        NB3 = 4
        sc_bank, o_bank = {}, {}

        def g_A(n):
            b = ps_bank(k)
            kn = KNT.v(KNT.ap[:, n * 128:(n + 1) * 128], n * 128, 128)
            qv = QT.v(QT.ap[:, n * 128:(n + 1) * 128], n * 128, 128)
            st = STs[n % NB3]
            P.op("PE", lambda e: e.matmul(k.psum[b][:, 0:128], lhsT=kn.ap, rhs=qv.ap, start=True, stop=True),
                 reads=[kn, qv], writes=[PSV(k, b)])
            P.op("DVE", lambda e: e.tensor_tensor(out=st.ap, in0=k.psum[b][:, 0:128], in1=k.MASKU.ap, op=ALU.mult),
                 reads=[PSV(k, b), k.MASKU.full()], writes=[st.full()])

        def g_B(n):
            b2 = ps_bank(k)
            o_bank[n] = b2
            st = STs[n % NB3]
            qv = QT.v(QT.ap[:, n * 128:(n + 1) * 128], n * 128, 128)
            q2 = QT2.v(QT2.ap[:, n * 128:(n + 1) * 128], n * 128, 128)
            vv = Vt.v(Vt.ap[:, n, :], n * 256, 256)
            sl = SLOC.v(SLOC.ap[:, n, :], n * 256, 256)
            P.op("PE", lambda e: e.matmul(k.psum[b2][:, 0:256], lhsT=st.ap, rhs=vv.ap, start=True, stop=False),
                 reads=[st.full(), vv], writes=[PSV(k, b2)])
            P.op("PE", lambda e: e.matmul(k.psum[b2][:, 0:256], lhsT=qv.ap, rhs=sl.ap, start=False, stop=False),
                 reads=[qv, sl], writes=[PSV(k, b2)])
            P.op("PE", lambda e: e.matmul(k.psum[b2][:, 0:256], lhsT=q2.ap, rhs=SBb.ap, start=False, stop=True),
                 reads=[q2, SBb.full()], writes=[PSV(k, b2)])

        def g_C(n):
            b2 = o_bank[n]
            tt, sr = TTs[n % NB3], SRs[n % NB3]
            srf = sr.full()
            P.op("ACT", lambda e: e.activation(out=tt.ap, in_=k.psum[b2][:, 0:256], func=AF.Square, accum_out=sr.ap[:, 0:1]),
                 reads=[PSV(k, b2)], writes=[tt.full(), srf])
            P.op("ACT", lambda e: e.activation(out=sr.ap[:, 1:2], in_=sr.ap[:, 0:1], func=AF.Ln, scale=1.0 / 256,
                                               bias=k.EPSC.ap), reads=[srf, k.EPSC.full()], writes=[srf])
            P.op("ACT", lambda e: e.activation(out=sr.ap[:, 1:2], in_=sr.ap[:, 1:2], func=AF.Exp, scale=-0.5),
                 reads=[srf], writes=[srf])

        def g_D(n):
            b2 = o_bank[n]
            tt, mtok, sr = TTs[n % NB3], MTOKs[n % NB3], SRs[n % NB3]
            P.op("DVE", lambda e: e.scalar_tensor_tensor(
                out=tt.ap, in0=k.psum[b2][:, 0:256], scalar=sr.ap[:, 1:2], in1=HNB.ap, op0=ALU.mult, op1=ALU.mult),
                reads=[PSV(k, b2), sr.full(), HNB.full()], writes=[tt.full()])
            gv2 = Gt.v(Gt.ap[:, n, :], n * 256, 256)
            P.op("DVE", lambda e: e.tensor_tensor(out=mtok.ap, in0=tt.ap, in1=gv2.ap, op=ALU.mult),
                 reads=[tt.full(), gv2], writes=[mtok.full()])

        def g_E(n):
            mtok = MTOKs[n % NB3]
            b3 = ps_bank(k)
            pb = k.psum[b3][:, :].bitcast(BF16)
            for c in range(2):
                P.op("PE", lambda e, c=c: e.transpose(
                    out=pb[:, c * 128:(c + 1) * 128], in_=mtok.ap[:, c * 128:(c + 1) * 128], identity=k.IDB.ap),
                    reads=[mtok.full(), k.IDB.full()], writes=[PSV(k, b3)])
            mv = MIXT.vs(MIXT.ap[:, :, n * 128:(n + 1) * 128], [(c * T + n * 128, 128) for c in range(2)])
            P.op("ACT", lambda e: e.activation(
                out=mv.ap, in_=pb[:, 0:256].rearrange("p (a b) -> p a b", a=2), func=AF.Copy),
                reads=[PSV(k, b3)], writes=[mv])

        for s_ in range(NT + 4):
            if s_ < NT:
                g_A(s_)
            if 0 <= s_ - 1 < NT:
                g_B(s_ - 1)
            if 0 <= s_ - 2 < NT:
                g_C(s_ - 2)
            if 0 <= s_ - 3 < NT:
                g_D(s_ - 3)
            if 0 <= s_ - 4 < NT:
                g_E(s_ - 4)
        emit_outproj(k, MIXT, WOH, 2)


def emit_ssd(k, i):
    P, A, nc = k.P, k.A, k.nc
    j = i // 2
    sz, sxbc = k.w(f"sz{j}"), k.w(f"sxbc{j}")
    wo = k.w(f"wo{i}")
    sdk = k.w(f"sdk{j}")
    SCV = Tile(A, [164], F32)
    QTM = Tile(A, [NT, 128], F32)
    DB = Tile(A, [2, 32, 16], F32)
    SSQ = Tile(A, [NT, 8], F32)
    SM = Tile(A, [64], F32)
    CST, CEN, DEC_, DCU_ = SM.ap[:, 0:16], SM.ap[:, 16:32], SM.ap[:, 32:48], SM.ap[:, 48:64]
    AV = Tile(A, [1], F32)
    smf = SM.full()
    P.dma("SP", SCV.ap, k.w(f"scv{j}")[:, :], writes=[SCV.full()])
    scvf = SCV.full()
    cslD = nc.dram_tensor(f"csl{i}", [32, T], F32)
    dcD = nc.dram_tensor(f"dcd{i}", [2, 32, 16], F32)
    ytD = nc.dram_tensor(f"ytd{i}", [NT, 128, 2048], BF16)
    csl_v = V(None, [("D", f"csl{i}")])
    dcd_v = V(None, [("D", f"dcd{i}")])
    m_layer = A.mark()
    WDT = Tile(A, [KC, 128], BF16)
    DTt = Tile(A, [T], F32)
    CSL = Tile(A, [T], F32)
    Q4 = Tile(A, [T], F32)
    load_w(k, WDT, k.w(f"sdt{j}")[:, :])
    for tb in range(4):
        b = inproj_fm(k, WDT, tb * 512, 512)
        dv = DTt.v(DTt.ap[:, tb * 512:(tb + 1) * 512], tb * 512, 512)
        P.op("ACT", lambda e, dv=dv, b=b: e.activation(out=dv.ap, in_=k.psum[b][:, :], func=AF.Exp,
                                                         bias=SCV.ap[:, 160:161]), reads=[PSV(k, b), scvf], writes=[dv])
    P.op("ACT", lambda e: e.activation(out=DTt.ap, in_=DTt.ap, func=AF.Ln, bias=1.0), reads=[DTt.full()], writes=[DTt.full()])
    P.op("ACT", lambda e: e.activation(out=AV.ap, in_=SCV.ap[:, 161:162], func=AF.Exp), reads=[scvf], writes=[AV.full()])
    P.op("DVE", lambda e: e.tensor_scalar(out=AV.ap, in0=AV.ap, scalar1=-1.0, scalar2=None, op0=ALU.mult),
         reads=[AV.full()], writes=[AV.full()])
    P.op("DVE", lambda e: e.tensor_scalar(out=Q4.ap, in0=DTt.ap, scalar1=AV.ap, scalar2=None, op0=ALU.mult),
         reads=[DTt.full(), AV.full()], writes=[Q4.full()])
    P.op("DVE", lambda e: e.tensor_tensor_scan(
        out=CSL.ap, data0=k.ONE1.ap.to_broadcast([128, T]), data1=Q4.ap, initial=0.0,
        op0=ALU.mult, op1=ALU.add), reads=[Q4.full(), k.ONE1.full()], writes=[CSL.full()])
    cs3 = CSL.ap.rearrange("p (n c) -> p n c", c=128)
    P.op("DVE", lambda e: e.memset(SM.ap[:, 0:1], 0.0), writes=[smf])
    P.op("DVE", lambda e: e.tensor_copy(out=SM.ap[:, 1:16], in_=cs3[:, 0:15, 127]), reads=[CSL.full()], writes=[smf])
    P.op("DVE", lambda e: e.tensor_copy(out=CEN, in_=cs3[:, :, 127]), reads=[CSL.full()], writes=[smf])
    P.op("DVE", lambda e: e.tensor_tensor(out=DEC_, in0=CEN, in1=CST, op=ALU.subtract), reads=[smf], writes=[smf])
    P.op("ACT", lambda e: e.activation(out=DEC_, in_=DEC_, func=AF.Exp), reads=[smf], writes=[smf])
    P.op("ACT", lambda e: e.activation(out=DCU_, in_=CST, func=AF.Exp), reads=[smf], writes=[smf])
    smv = V(SM.ap[0:32, 32:64].rearrange("p (a n) -> p a n", a=2), SM.full().cells)
    P.dma("SP", dcD.ap().rearrange("a h n -> h a n"), smv.ap, reads=[smv], writes=[dcd_v])
    P.dma("SP", DB.ap, dcD.ap().partition_broadcast(128), reads=[dcd_v], writes=[DB.full()])
    P.op("DVE", lambda e: e.tensor_tensor(out=CEN, in0=CEN, in1=CST, op=ALU.subtract), reads=[smf], writes=[smf])
    P.op("DVE", lambda e: e.tensor_tensor(
        out=cs3, in0=cs3, in1=CST.unsqueeze(2).to_broadcast([128, NT, 128]), op=ALU.subtract),
        reads=[CSL.full(), smf], writes=[CSL.full()])
    cslsb = V(CSL.ap[0:32, :], CSL.full().cells)
    P.dma("SP", cslD[:, :], cslsb.ap, reads=[cslsb], writes=[csl_v])
    P.op("ACT", lambda e: e.activation(out=Q4.ap, in_=DTt.ap, func=AF.Ln), reads=[DTt.full()], writes=[Q4.full()])
    P.op("DVE", lambda e: e.tensor_tensor(out=Q4.ap, in0=Q4.ap, in1=CSL.ap, op=ALU.subtract),
         reads=[Q4.full(), CSL.full()], writes=[Q4.full()])
    P.op("ACT", lambda e: e.activation(out=Q4.ap[0:32, :], in_=CSL.ap[0:32, :], func=AF.Exp),
         reads=[CSL.full()], writes=[Q4.full()])
    q3 = Q4.ap.rearrange("p (n c) -> p n c", c=128)
    P.op("DVE", lambda e: e.tensor_tensor(
        out=q3[32:64], in0=CEN[32:64].unsqueeze(2).to_broadcast([32, NT, 128]), in1=cs3[32:64], op=ALU.subtract),
        reads=[CSL.full(), smf], writes=[Q4.full()])
    P.op("ACT", lambda e: e.activation(out=Q4.ap[32:64, :], in_=Q4.ap[32:64, :], func=AF.Exp),
         reads=[Q4.full()], writes=[Q4.full()])
    P.op("DVE", lambda e: e.tensor_tensor(out=Q4.ap[32:64, :], in0=Q4.ap[32:64, :], in1=DTt.ap[32:64, :], op=ALU.mult),
         reads=[Q4.full(), DTt.full()], writes=[Q4.full()])
    P.op("ACT", lambda e: e.activation(out=Q4.ap[64:96, :], in_=CSL.ap[64:96, :], func=AF.Copy),
         reads=[CSL.full()], writes=[Q4.full()])
    for n4 in range(4):
        b = ps_bank(k)
        for t in range(4):
            n = n4 * 4 + t
            qv = Q4.v(Q4.ap[:, n * 128:(n + 1) * 128], n * 128, 128)
            P.op("PE", lambda e, qv=qv, b=b, t=t: e.transpose(
                out=k.psum[b][:, t * 128:(t + 1) * 128], in_=qv.ap, identity=k.IDF.ap),
                reads=[qv, k.IDF.full()], writes=[PSV(k, b)])
        qm = QTM.v(QTM.ap[:, n4 * 4:(n4 + 1) * 4, :], n4 * 4 * 128, 4 * 128)
        P.op("ACT", lambda e, qm=qm, b=b: e.activation(
            out=qm.ap, in_=k.psum[b][:, :].rearrange("p (a b) -> p a b", a=4), func=AF.Copy),
            reads=[PSV(k, b)], writes=[qm])
    A.release(m_layer)
    wi = 0
    WR = [Tile(A, [KC, 128], BF16) for _ in range(2)]
    WZ = Tile(A, [KC, 256], BF16)
    PRE = Tile(A, [T + 4], BF16)
    DG = [Tile(A, [4, 128], BF16) for _ in range(2)]
    DGD = Tile(A, [4, 128], BF16)
    XsT = Tile(A, [2, T], BF16)
    BT = Tile(A, [T], BF16)
    CT = Tile(A, [T], BF16)
    XS = Tile(A, [NT, 256], BF16)
    XW = Tile(A, [NT, 256], BF16)
    Btok = Tile(A, [NT, 128], BF16)
    SLOC = Tile(A, [NT, 256], BF16, off=XsT.off)
    YT = Tile(A, [2, T], BF16)
    S = Tile(A, [256], F32)
    SB32 = Tile(A, [256], F32)
    NB3 = 4
    SBns = [Tile(A, [256], BF16) for _ in range(NB3)]
    BCts = [Tile(A, [4, 128], F32) for _ in range(3)]
    CBMs = [Tile(A, [128], F32) for _ in range(NB3)]
    SEG = [Tile(A, [128], F32) for _ in range(8)]
    MTs = [[Tile(A, [128], BF16) for _ in range(4)] for _ in range(NB3)]
    T1s = [Tile(A, [256], F32, off=XW.off + r_ * 1024) for r_ in range(NB3)]
    THs = [Tile(A, [256], F32, off=XW.off + 4096 + r_ * 1024) for r_ in range(NB3)]
    YFs = [Tile(A, [256], F32, off=PRE.off + r_ * 1024) for r_ in range(NB3)]
    G2s = [Tile(A, [256], BF16) for _ in range(NB3)]
    ZCs = [Tile(A, [256], BF16) for _ in range(NB3)]
    YGBs = [Tile(A, [256], BF16) for _ in range(NB3)]
    DSK = Tile(A, [256], F32)
    dgi = 0
    pre_setup = []
    for g in range(8):
        P.dma("SP", DSK.ap, sdk[:, g * 256:(g + 1) * 256], writes=[DSK.full()])
        load_w(k, WZ, sz[g, :, :])
        for jh in range(4):
            dv_ = DGD.v(DGD.ap[:, jh, :], jh * 128, 128)
            P.op("POOL", lambda e, dv_=dv_, jh=jh: e.tensor_scalar(
                out=dv_.ap, in0=k.IDB.ap, scalar1=DSK.ap[:, jh * 64:jh * 64 + 1], scalar2=None, op0=ALU.mult),
                reads=[k.IDB.full(), DSK.full()], writes=[dv_])

        def conv_setup(cb):
            nonlocal wi, dgi
            w = WR[wi % 2]
            wi += 1
            load_w(k, w, sxbc[cb, :, :])
            dg = DG[dgi % 2]
            dgi += 1
            for tap in range(4):
                dgv = dg.v(dg.ap[:, tap, :], tap * 128, 128)
                P.op("POOL", lambda e, dgv=dgv, tap=tap: e.tensor_scalar(
                    out=dgv.ap, in0=k.IDB.ap, scalar1=SCV.ap[:, cb * 4 + tap:cb * 4 + tap + 1], scalar2=None,
                    op0=ALU.mult), reads=[k.IDB.full(), scvf], writes=[dgv])
            return w, dg

        def conv_proj(w, tb, cb=None):
            if tb == 0:
                hp = PRE.v(PRE.ap[:, 0:3], 0, 3)
                bh = ps_bank(k)
                for kc in range(KC):
                    P.op("PE", lambda e, kc=kc: e.matmul(
                        k.psum[bh][:, 0:3], lhsT=w.ap[:, kc, :], rhs=k.XHB.ap[:, kc, :],
                        start=(kc == 0), stop=(kc == KC - 1)), reads=[w.full(), k.XHB.full()], writes=[PSV(k, bh)])
                P.op("ACT", lambda e: e.activation(out=hp.ap, in_=k.psum[bh][:, 0:3], func=AF.Copy),
                     reads=[PSV(k, bh)], writes=[hp])
            b = inproj_fm(k, w, tb * 512, 512)
            pv = PRE.v(PRE.ap[:, 3 + tb * 512:3 + (tb + 1) * 512], 3 + tb * 512, 512)
            P.op("ACT", lambda e: e.activation(out=pv.ap, in_=k.psum[b][:, :], func=AF.Copy),
                 reads=[PSV(k, b)], writes=[pv])

        def conv_out(cb, dg, dst, tb):
            b = ps_bank(k)
            for tap in range(4):
                pv = PRE.v(PRE.ap[:, tap + tb * 512:tap + (tb + 1) * 512], tap + tb * 512, 512)
                P.op("PE", lambda e, pv=pv, tap=tap: e.matmul(
                    k.psum[b][:, :], lhsT=dg.ap[:, tap, :], rhs=pv.ap, start=(tap == 0), stop=(tap == 3)),
                    reads=[dg.full(), pv], writes=[PSV(k, b)])
            dv = V(dst.ap[:, tb * 512:(tb + 1) * 512], dst.cells)
            P.op("ACT", lambda e: e.activation(
                out=dv.ap, in_=k.psum[b][:, :], func=AF.Silu, bias=SCV.ap[:, 128 + cb:129 + cb]),
                reads=[PSV(k, b), scvf], writes=[dv])

        dsts = [(2 * g, XsT.v(XsT.ap[:, 0, :], 0, T)), (2 * g + 1, XsT.v(XsT.ap[:, 1, :], T, T)),
                (16 + g, BT.full())]
        for bi, (cb, dst) in enumerate(dsts):
            if bi < 2 and pre_setup:
                w, dg = pre_setup.pop(0)
            else:
                w, dg = conv_setup(cb)
            for tb in range(4):
                conv_proj(w, tb, cb)
            for tb in range(4):
                conv_out(cb, dg, dst, tb)
        cbC = 24 + g
        wC, dgC = None, None

        def tr_batch(q, g=g):
            b = ps_bank(k)
            pb = k.psum[b][:, :].bitcast(BF16)
            for t in range(4):
                n = q * 4 + t
                for c in range(2):
                    xv = XsT.v(XsT.ap[:, c, n * 128:(n + 1) * 128], c * T + n * 128, 128)
                    P.op("PE", lambda e, xv=xv, t=t, c=c: e.transpose(
                        out=pb[:, (t * 2 + c) * 128:(t * 2 + c + 1) * 128], in_=xv.ap, identity=k.IDB.ap),
                        reads=[xv, k.IDB.full()], writes=[PSV(k, b)])
            xs4 = XS.v(XS.ap[:, q * 4:(q + 1) * 4, :], q * 4 * 256, 4 * 256)
            P.op("ACT", lambda e: e.activation(
                out=xs4.ap, in_=pb.rearrange("p (a b) -> p a b", a=4), func=AF.Copy),
                reads=[PSV(k, b)], writes=[xs4])
            b2 = ps_bank(k)
            pb2 = k.psum[b2][:, :].bitcast(BF16)
            for t in range(4):
                n = q * 4 + t
                bv = BT.v(BT.ap[:, n * 128:(n + 1) * 128], n * 128, 128)
                P.op("PE", lambda e, bv=bv, t=t: e.transpose(
                    out=pb2[:, t * 128:(t + 1) * 128], in_=bv.ap, identity=k.IDB.ap),
                    reads=[bv, k.IDB.full()], writes=[PSV(k, b2)])
            b4 = Btok.v(Btok.ap[:, q * 4:(q + 1) * 4, :], q * 4 * 128, 4 * 128)
            P.op("ACT", lambda e: e.activation(
                out=b4.ap, in_=pb2[:, 0:512].rearrange("p (a b) -> p a b", a=4), func=AF.Copy),
                reads=[PSV(k, b2)], writes=[b4])
            xw4 = XW.v(XW.ap[:, q * 4:(q + 1) * 4, :], q * 4 * 256, 4 * 256)
            P.op("DVE", lambda e: e.tensor_tensor(
                out=xw4.ap.rearrange("p n (h q) -> p n h q", h=4), in0=xs4.ap.rearrange("p n (h q) -> p n h q", h=4),
                in1=QTM.ap[:, q * 4:(q + 1) * 4, 32 + 4 * g:36 + 4 * g].unsqueeze(3).to_broadcast([128, 4, 4, 64]),
                op=ALU.mult), reads=[xs4, QTM.full()], writes=[xw4])

        s3 = S.ap.rearrange("p (h q) -> p h q", h=4)

        def rec_step(n, g=g):
            b = ps_bank(k)
            bv = Btok.v(Btok.ap[:, n, :], n * 128, 128)
            xw = XW.v(XW.ap[:, n, :], n * 256, 256)
            P.op("PE", lambda e: e.matmul(k.psum[b][:, 0:256], lhsT=bv.ap, rhs=xw.ap, start=True, stop=True),
                 reads=[bv, xw], writes=[PSV(k, b)])
            P.op("DVE", lambda e: e.tensor_tensor(
                out=s3, in0=s3, in1=DB.ap[:, 0, 4 * g:4 * g + 4, n].unsqueeze(2).to_broadcast([128, 4, 64]),
                op=ALU.mult), reads=[S.full(), DB.full()], writes=[S.full()])
            P.op("DVE", lambda e: e.tensor_tensor(out=S.ap, in0=k.psum[b][:, 0:256], in1=S.ap, op=ALU.add),
                 reads=[S.full(), PSV(k, b)], writes=[S.full()])
            if n < NT - 1:
                sl = SLOC.v(SLOC.ap[:, n + 1, :], (n + 1) * 256, 256)
                P.op("ACT", lambda e: e.activation(out=sl.ap, in_=S.ap, func=AF.Copy),
                     reads=[S.full()], writes=[sl])

        P.op("DVE", lambda e: e.memset(S.ap, 0.0), writes=[S.full()])
        for q in range(4):
            tr_batch(q)
        P.op("POOL", lambda e: e.memset(SLOC.ap[:, 0, :], 0.0), writes=[SLOC.v(SLOC.ap[:, 0, :], 0, 256)])
        wC, dgC = conv_setup(cbC)
        for q in range(4):
            conv_proj(wC, q, cbC)
            for n in range(q * 4, q * 4 + 4):
                rec_step(n)
        xctx = exchange_start(k, S.full(), 128, 256, f"sx{i}_{g}")
        for tb in range(4):
            conv_out(cbC, dgC, CT.full(), tb)
        if g < 7:
            pre_setup.extend([conv_setup(2 * (g + 1)), conv_setup(2 * (g + 1) + 1)])
        exchange_finish(k, xctx, SB32)
        sb3 = SB32.ap.rearrange("p (h q) -> p h q", h=4)
        def s_dma(n, g=g):
            bct = BCts[n % 3]
            P.dma("SP", bct.ap, cslD[4 * g:4 * g + 4, n * 128:(n + 1) * 128].partition_broadcast(128),
                  reads=[csl_v], writes=[bct.full()])

        def s_A(n, g=g):
            r = n % NB3
            bct, cbm, th, zc = BCts[n % 3], CBMs[r], THs[r], ZCs[r]
            bv = BT.v(BT.ap[:, n * 128:(n + 1) * 128], n * 128, 128)
            cv = CT.v(CT.ap[:, n * 128:(n + 1) * 128], n * 128, 128)
            b = ps_bank(k)
            P.op("PE", lambda e: e.matmul(k.psum[b][:, 0:128], lhsT=bv.ap, rhs=cv.ap, start=True, stop=True),
                 reads=[bv, cv], writes=[PSV(k, b)])
            bz = inproj_tm(k, WZ, n, 256)
            P.op("DVE", lambda e: e.tensor_tensor(out=cbm.ap, in0=k.psum[b][:, 0:128], in1=k.MASKU.ap, op=ALU.mult),
                 reads=[PSV(k, b), k.MASKU.full()], writes=[cbm.full()])
            for jh in range(4):
                h = 4 * g + jh
                sg = SEG[(n % 2) * 4 + jh]
                P.op("DVE", lambda e, sg=sg, jh=jh, h=h: e.tensor_scalar(
                    out=sg.ap, in0=bct.ap[:, jh, :], scalar1=QTM.ap[:, n, 64 + h:65 + h], scalar2=None, op0=ALU.min),
                    reads=[bct.full(), QTM.full()], writes=[sg.full()])
            for jh in range(4):
                h = 4 * g + jh
                sg = SEG[(n % 2) * 4 + jh]
                P.op("ACT", lambda e, sg=sg, h=h: e.activation(
                    out=sg.ap, in_=sg.ap, func=AF.Exp, bias=QTM.ap[:, n, 96 + h:97 + h]),
                    reads=[sg.full(), QTM.full()], writes=[sg.full()])
            P.op("ACT", lambda e: e.activation(out=th.ap, in_=k.psum[bz][:, 0:256], func=AF.Tanh, scale=0.5),
                 reads=[PSV(k, bz)], writes=[th.full()])
            P.op("ACT", lambda e: e.activation(out=zc.ap, in_=k.psum[bz][:, 0:256], func=AF.Copy),
                 reads=[PSV(k, bz)], writes=[zc.full()])

        def s_B(n, g=g):
            r = n % NB3
            cbm, th, zc, g2, sbn = CBMs[r], THs[r], ZCs[r], G2s[r], SBns[r]
            for jh in range(4):
                sg = SEG[(n % 2) * 4 + jh]
                mt = MTs[r][jh]
                P.op("DVE", lambda e, sg=sg, mt=mt: e.tensor_tensor(out=mt.ap, in0=sg.ap, in1=cbm.ap, op=ALU.mult),
                     reads=[sg.full(), cbm.full()], writes=[mt.full()])
            P.op("DVE", lambda e: e.scalar_tensor_tensor(
                out=g2.ap, in0=th.ap, scalar=1.0, in1=zc.ap, op0=ALU.add, op1=ALU.mult),
                reads=[th.full(), zc.full()], writes=[g2.full()])
            P.op("DVE", lambda e: e.tensor_tensor(
                out=sbn.ap.rearrange("p (h q) -> p h q", h=4), in0=sb3,
                in1=DB.ap[:, 1, 4 * g:4 * g + 4, n].unsqueeze(2).to_broadcast([128, 4, 64]), op=ALU.mult),
                reads=[SB32.full(), DB.full()], writes=[sbn.full()])

        cbank = {}

        def s_C1(n, g=g):
            r = n % NB3
            sbn = SBns[r]
            sl = SLOC.v(SLOC.ap[:, n, :], n * 256, 256)
            cv = CT.v(CT.ap[:, n * 128:(n + 1) * 128], n * 128, 128)
            by = ps_bank(k)
            cbank[n] = by
            for jh in range(4):
                mt = MTs[r][jh]
                xs = XS.v(XS.ap[:, n, jh * 64:(jh + 1) * 64], n * 256 + jh * 64, 64)
                P.op("PE", lambda e, mt=mt, xs=xs, jh=jh: e.matmul(
                    k.psum[by][:, jh * 64:(jh + 1) * 64], lhsT=mt.ap, rhs=xs.ap, start=True, stop=False),
                    reads=[mt.full(), xs], writes=[PSV(k, by)])
                P.op("PE", lambda e, xs=xs, jh=jh: e.matmul(
                    k.psum[by][:, jh * 64:(jh + 1) * 64], lhsT=DGD.ap[:, jh, :], rhs=xs.ap, start=False, stop=True),
                    reads=[DGD.full(), xs], writes=[PSV(k, by)])
            P.op("PE", lambda e: e.matmul(k.psum[by][:, 256:512], lhsT=cv.ap, rhs=sl.ap, start=True, stop=False),
                 reads=[cv, sl], writes=[PSV(k, by)])
            P.op("PE", lambda e: e.matmul(k.psum[by][:, 256:512], lhsT=cv.ap, rhs=sbn.ap, start=False, stop=True),
                 reads=[cv, sbn.full()], writes=[PSV(k, by)])

        def s_C2(n, g=g):
            r = n % NB3
            t1, yf, g2, ygb = T1s[r], YFs[r], G2s[r], YGBs[r]
            by = cbank[n]
            P.op("DVE", lambda e: e.tensor_tensor(
                out=t1.ap.rearrange("p (h q) -> p h q", h=4),
                in0=k.psum[by][:, 256:512].rearrange("p (h q) -> p h q", h=4),
                in1=QTM.ap[:, n, 4 * g:4 * g + 4].unsqueeze(2).to_broadcast([128, 4, 64]), op=ALU.mult),
                reads=[PSV(k, by), QTM.full()], writes=[t1.full()])
            P.op("DVE", lambda e: e.tensor_tensor(out=yf.ap, in0=k.psum[by][:, 0:256], in1=t1.ap, op=ALU.add),
                 reads=[PSV(k, by), t1.full()], writes=[yf.full()])
            P.op("DVE", lambda e: e.tensor_tensor(out=ygb.ap, in0=yf.ap, in1=g2.ap, op=ALU.mult),
                 reads=[yf.full(), g2.full()], writes=[ygb.full()])
            sq = SSQ.v(SSQ.ap[:, n, g:g + 1], n * 8 + g, 1)
            P.op("ACT", lambda e: e.activation(out=t1.ap, in_=ygb.ap, func=AF.Square, accum_out=sq.ap),
                 reads=[ygb.full()], writes=[t1.full(), sq])

        def s_D(n, g=g):
            ygb = YGBs[n % NB3]
            b3 = ps_bank(k)
            pb = k.psum[b3][:, :].bitcast(BF16)
            for c in range(2):
                P.op("PE", lambda e, c=c: e.transpose(
                    out=pb[:, c * 128:(c + 1) * 128], in_=ygb.ap[:, c * 128:(c + 1) * 128], identity=k.IDB.ap),
                    reads=[ygb.full(), k.IDB.full()], writes=[PSV(k, b3)])
            yv = YT.vs(YT.ap[:, :, n * 128:(n + 1) * 128], [(c * T + n * 128, 128) for c in range(2)])
            P.op("ACT", lambda e: e.activation(
                out=yv.ap, in_=pb[:, 0:256].rearrange("p (a b) -> p a b", a=2), func=AF.Copy),
                reads=[PSV(k, b3)], writes=[yv])

        s_dma(0)
        for s_ in range(NT + 4):
            if s_ + 1 < NT:
                s_dma(s_ + 1)
            if s_ < NT:
                s_A(s_)
            if 0 <= s_ - 1 < NT:
                s_B(s_ - 1)
            if 0 <= s_ - 2 < NT:
                s_C1(s_ - 2)
            if 0 <= s_ - 3 < NT:
                s_C2(s_ - 3)
            if 0 <= s_ - 4 < NT:
                s_D(s_ - 4)
        ytv = V(None, [("D", f"ytd{i}", g)])
        for c in range(2):
            P.dma("SP", ytD.ap()[:, :, (2 * g + c) * 128:(2 * g + c + 1) * 128].rearrange("n p t -> p n t"),
                  YT.ap[:, c, :].rearrange("p (n t) -> p n t", n=NT), reads=[YT.full()], writes=[ytv])
    A.release(m_layer)
    WOM = Tile(A, [16, D], BF16)
    YTt = [Tile(A, [16, 128], BF16) for _ in range(2)]
    WST = [Tile(A, [D], F32) for _ in range(2)]
    SNF = Tile(A, [16], F32)
    P.dma("SP", SNF.ap, k.w(f"snf{j}")[:, :], writes=[SNF.full()])
    for c in range(16):
        wv = WOM.v(WOM.ap[:, c, :], c * D, D)
        ws = WST[c % 2]
        P.dma("SP", ws.ap, wo[c, :, :], writes=[ws.full()])
        P.op("DVE", lambda e, wv=wv, ws=ws, c=c: e.tensor_scalar(
            out=wv.ap, in0=ws.ap, scalar1=SNF.ap[:, c:c + 1], scalar2=0.5, op0=ALU.mult, op1=ALU.mult),
            reads=[ws.full(), SNF.full()], writes=[wv])
    rsf = k.RS.full()
    P.op("DVE", lambda e: e.tensor_reduce(out=k.RS.ap, in_=SSQ.ap, axis=mybir.AxisListType.X, op=ALU.add),
         reads=[SSQ.full()], writes=[rsf])
    P.op("DVE", lambda e: e.tensor_scalar(out=k.RS.ap, in0=k.RS.ap, scalar1=0.25 / 2048, scalar2=EPS,
                                           op0=ALU.mult, op1=ALU.add), reads=[rsf], writes=[rsf])
    P.op("DVE", lambda e: e.reciprocal(out=k.RS.ap, in_=k.RS.ap), reads=[rsf], writes=[rsf])
    P.op("ACT", lambda e: e.activation(out=k.RS.ap, in_=k.RS.ap, func=AF.Sqrt), reads=[rsf], writes=[rsf])
    ytall = [V(None, [("D", f"ytd{i}", g)]) for g in range(8)]
    for n in range(NT):
        yt = YTt[n % 2]
        P.dma("SP", yt.ap.rearrange("p a b -> p (a b)"), ytD[n, :, :], reads=ytall, writes=[yt.full()])
        for nb in range(2):
            b = ps_bank(k)
            for c in range(16):
                wv = WOM.v(WOM.ap[:, c, nb * 512:(nb + 1) * 512], c * D + nb * 512, 512)
                P.op("PE", lambda e, yt=yt, wv=wv, b=b, c=c: e.matmul(
                    k.psum[b][:, :], lhsT=yt.ap[:, c, :], rhs=wv.ap, start=(c == 0), stop=(c == 15)),
                    reads=[yt.full(), wv], writes=[PSV(k, b)])
            hv = hview(k, n, nb * 512, 512)
            P.op("DVE", lambda e, hv=hv, b=b, n=n: e.scalar_tensor_tensor(
                out=hv.ap, in0=k.psum[b][:, :], scalar=k.RS.ap[:, n:n + 1], in1=hv.ap, op0=ALU.mult, op1=ALU.add),
                reads=[PSV(k, b), hv, rsf], writes=[hv])


def emit_output(k, cfg, bcp_d, out_d):
    P, A = k.P, k.A
    outs = []
    if not cfg.get("final_norm", True):
        for j in range(NT):
            hv = hview(k, j)
            outs.append(P.dma("SP", out_d[j * 128:(j + 1) * 128, :], hv.ap, reads=[hv],
                              writes=[V(None, [("D", "out", j)])]))
    else:
        FN = Tile(A, [D], F32)
        OB = [Tile(A, [D], F32) for _ in range(2)]
        P.dma("SP", FN.ap, bcp_d[:, 0:1024], writes=[FN.full()])
        SS, RS = k.SS, k.RS
        ssf, rsf = SS.full(), RS.full()
        junk = k.JUNK.full()
        for j in range(NT):
            hv = hview(k, j)
            P.op("ACT", lambda e, hv=hv, j=j: e.activation(
                out=k.JUNK.ap, in_=hv.ap, func=AF.Square, accum_out=SS.ap[:, j:j + 1]),
                reads=[hv], writes=[junk, ssf])
        P.op("DVE", lambda e: e.tensor_scalar(out=RS.ap, in0=SS.ap, scalar1=1.0 / D, scalar2=EPS,
                                               op0=ALU.mult, op1=ALU.add), reads=[ssf], writes=[rsf])
        P.op("DVE", lambda e: e.reciprocal(out=RS.ap, in_=RS.ap), reads=[rsf], writes=[rsf])
        P.op("ACT", lambda e: e.activation(out=RS.ap, in_=RS.ap, func=AF.Sqrt), reads=[rsf], writes=[rsf])
        for j in range(NT):
            hv = hview(k, j)
            ob = OB[j % 2]
            P.op("DVE", lambda e, hv=hv, ob=ob, j=j: e.scalar_tensor_tensor(
                out=ob.ap, in0=hv.ap, scalar=RS.ap[:, j:j + 1], in1=FN.ap, op0=ALU.mult, op1=ALU.mult),
                reads=[hv, rsf, FN.full()], writes=[ob.full()])
            outs.append(P.dma("SP", out_d[j * 128:(j + 1) * 128, :], ob.ap, reads=[ob.full()],
                              writes=[V(None, [("D", "out", j)])]))
    fin = Op("SP", lambda e: e.nop())
    fin.seq = len(P.ops["SP"])
    for o in outs:
        P._need(fin, o)
    P.ops["SP"].append(fin)


FULL_CFG = dict(sublayers=[("mix", 0), ("ffn", 0), ("mix", 1), ("ffn", 1),
                           ("mix", 2), ("ffn", 2), ("mix", 3), ("ffn", 3)], final_norm=True)


def run(inputs, cfg):
    nc = build(cfg)
    maps = prep_inputs(inputs, set(nc.used_inputs))
    maps = [{kk: m[kk] for kk in nc.used_inputs} for m in maps]
    res = run_bass_kernel_spmd(nc, maps, core_ids=list(range(8)))
    out = np.zeros((4, 4096, D), np.float32)
    for c in range(8):
        b, half = c // 2, c % 2
        out[b, half * T:(half + 1) * T] = res.results[c]["out"]
    return out


def kernel(**inputs):
    return run(inputs, FULL_CFG)
```

```python
import contextlib
import numpy as np
import concourse.bass as bass
import concourse.mybir as mybir
from concourse.bass_utils import run_bass_kernel_spmd

F32 = mybir.dt.float32
BF16 = mybir.dt.bfloat16
U8 = mybir.dt.uint8
AF = mybir.ActivationFunctionType
ALU = mybir.AluOpType

T = 2048
NT = 16
D = 1024
KC = 8
FH = 2816
EPS = 1e-6
CELL = 256
ARENA_BYTES = 206 * 1024
PAIRS = [[0, 1], [2, 3], [4, 5], [6, 7]]
SAME_ENG_SYNC = True


class V:
    __slots__ = ("ap", "cells")

    def __init__(self, ap, cells):
        self.ap = ap
        self.cells = tuple(cells)


def scells(off, nbytes):
    return [("S", c) for c in range(off // CELL, (off + nbytes - 1) // CELL + 1)]


class Op:
    __slots__ = ("eng", "fn", "waits", "signal", "sigval", "seq", "dma", "dsem", "dval", "inc")

    def __init__(self, eng, fn):
        self.eng = eng
        self.fn = fn
        self.waits = []
        self.signal = False
        self.sigval = None
        self.seq = None
        self.dma = False
        self.dsem = None
        self.dval = None
        self.inc = 1


class Prog:
    ENGS = ("PE", "ACT", "DVE", "POOL", "SP")
    NDSEM = 8

    def __init__(self):
        self.ops = {e: [] for e in self.ENGS}
        self.cellw = {}
        self.cellr = {}
        self.waited_eng = {e: {} for e in self.ENGS}
        self.waited_dma = {e: {} for e in self.ENGS}
        self.dma_count = {"SP": 0, "POOL": 0, "ACT": 0}
        self.dma_hist = {"SP": [], "POOL": [], "ACT": []}

    def _need(self, op, dep):
        if dep is op:
            return
        e = op.eng
        if dep.dma:
            key = (dep.eng, dep.dsem)
            if self.waited_dma[e].get(key, 0) >= dep.dval:
                return
            self.waited_dma[e][key] = dep.dval
            op.waits.append(dep)
        else:
            if dep.eng == e and not op.dma:
                if e == "PE" or not SAME_ENG_SYNC:
                    return
            if self.waited_eng[e].get(dep.eng, -1) >= dep.seq:
                return
            self.waited_eng[e][dep.eng] = dep.seq
            dep.signal = True
            op.waits.append(dep)

    def _track(self, op, reads, writes):
        deps = []
        for v in reads:
            for c in v.cells:
                w = self.cellw.get(c)
                if w is not None:
                    deps.append(w)
        for v in writes:
            for c in v.cells:
                w = self.cellw.get(c)
                if w is not None:
                    deps.append(w)
                deps.extend(self.cellr.get(c, ()))
        seen = set()
        for d in deps:
            if id(d) in seen:
                continue
            seen.add(id(d))
            self._need(op, d)
        for v in reads:
            for c in v.cells:
                self.cellr.setdefault(c, []).append(op)
        for v in writes:
            for c in v.cells:
                self.cellw[c] = op
                self.cellr[c] = []

    def op(self, eng, fn, reads=(), writes=()):
        o = Op(eng, fn)
        o.seq = len(self.ops[eng])
        self._track(o, reads, writes)
        self.ops[eng].append(o)
        return o

    def dma(self, queue, out_ap, in_ap, reads=(), writes=(), **kw):
        o = Op(queue, lambda e: e.dma_start(out=out_ap, in_=in_ap, **kw))
        o.dma = True
        o.inc = 16
        i = self.dma_count[queue]
        self.dma_count[queue] = i + 1
        o.dsem = i % self.NDSEM
        o.dval = 16 * (i // self.NDSEM + 1)
        o.seq = len(self.ops[queue])
        hist = self.dma_hist[queue]
        if i >= self.NDSEM:
            self._need(o, hist[i - self.NDSEM])
        hist.append(o)
        self._track(o, reads, writes)
        self.ops[queue].append(o)
        return o

    def coll(self, fn, reads=(), writes=()):
        o = Op("POOL", fn)
        o.dma = True
        o.inc = 1
        self.ncoll = getattr(self, "ncoll", 0) + 1
        o.dsem = "CC"
        o.dval = self.ncoll
        o.seq = len(self.ops["POOL"])
        self._track(o, reads, writes)
        self.ops["POOL"].append(o)
        return o

    def finalize(self):
        for e in self.ENGS:
            n = 0
            for o in self.ops[e]:
                if o.dma:
                    continue
                if o.signal:
                    n += 1
                    o.sigval = n

    def emit(self, eng_name, e, esem, dsem, ccsem):
        for o in self.ops[eng_name]:
            for d in o.waits:
                if d.dma:
                    if d.dsem == "CC":
                        e.wait_ge(ccsem, d.dval)
                    else:
                        e.wait_ge(dsem[d.eng][d.dsem], d.dval)
                else:
                    e.wait_ge(esem[d.eng], d.sigval)
            inst = o.fn(e)
            if o.dma:
                if o.dsem == "CC":
                    inst.then_inc(ccsem)
                else:
                    inst.then_inc(dsem[eng_name][o.dsem], 16)
            elif o.signal:
                inst.then_inc(esem[eng_name], 1)


class Arena:
    def __init__(self, ap):
        self.ap = ap
        self.top = 0

    def alloc(self, nbytes, align=256):
        off = (self.top + align - 1) // align * align
        assert off + nbytes <= ARENA_BYTES, ("arena overflow", off, nbytes)
        self.top = off + nbytes
        self.high = max(getattr(self, "high", 0), self.top)
        return off

    def mark(self):
        return self.top

    def release(self, m):
        self.top = m


class Tile:
    def __init__(self, arena, dims, dtype, name="", off=None):
        self.dims = list(dims)
        self.dtype = dtype
        self.esz = 4 if dtype == F32 else 2
        n = int(np.prod(dims))
        self.nbytes = n * self.esz
        self.off = arena.alloc(self.nbytes) if off is None else off
        ap = arena.ap[:, self.off:self.off + self.nbytes].bitcast(dtype)
        if len(dims) == 2:
            ap = ap.rearrange("p (a b) -> p a b", a=dims[0])
        elif len(dims) == 3:
            ap = ap.rearrange("p (a b c) -> p a b c", a=dims[0], b=dims[1])
        self.ap = ap

    def full(self):
        return V(self.ap, scells(self.off, self.nbytes))

    def v(self, ap, start_elem, nelem):
        return V(ap, scells(self.off + start_elem * self.esz, nelem * self.esz))

    def vs(self, ap, segs):
        cells = []
        for (s, n) in segs:
            cells.extend(scells(self.off + s * self.esz, n * self.esz))
        return V(ap, cells)


class K:
    pass


def build(cfg):
    nc = bass.Bass("TRN2", target_bir_lowering=False)
    P = Prog()
    global LASTP
    LASTP = P
    k = K()
    k.nc, k.P = nc, P
    dr = {}

    def dram_in(name, shape, dtype=F32):
        dr[name] = nc.dram_tensor(name, list(shape), dtype, kind="ExternalInput")
        return dr[name]

    x_d = dram_in("x", [T, D])
    mem_d = dram_in("mem", [256, D])
    par_d = dram_in("par", [128, NPAR])
    bcp_d = dram_in("bcp", [128, NBCP])
    def wget(nm):
        if nm not in dr:
            dram_in(nm, WEIGHT_SHAPES[nm])
        return dr[nm]
    k.w = wget
    out_d = nc.dram_tensor("out", [T, D], F32, kind="ExternalOutput")
    k.dr = dr

    stack = contextlib.ExitStack()
    with stack:
        arena_t = stack.enter_context(nc.sbuf_tensor("arena", [128, ARENA_BYTES], U8))
        psum = [stack.enter_context(nc.psum_tensor(f"ps{i}", [128, 512], F32)) for i in range(8)]
        esem = {e: stack.enter_context(nc.semaphore("s" + e)) for e in Prog.ENGS}
        dsem = {q: [stack.enter_context(nc.semaphore(f"d{q}{i}")) for i in range(Prog.NDSEM)]
                for q in ("SP", "POOL", "ACT")}
        ccsem = stack.enter_context(nc.semaphore("cc"))
        block = stack.enter_context(nc.Block())

        A = Arena(arena_t)
        k.A = A
        k.psum = psum
        k.ps_i = 0

        emit_program(k, cfg, x_d, mem_d, par_d, bcp_d, out_d)
        P.finalize()
        nc.used_inputs = list(dr.keys())

        @block.tensor
        def _(e):
            P.emit("PE", e, esem, dsem, ccsem)

        @block.scalar
        def _(e):
            P.emit("ACT", e, esem, dsem, ccsem)

        @block.vector
        def _(e):
            P.emit("DVE", e, esem, dsem, ccsem)

        @block.gpsimd
        def _(e):
            P.emit("POOL", e, esem, dsem, ccsem)

        @block.sync
        def _(e):
            P.emit("SP", e, esem, dsem, ccsem)
    return nc


PAR = {}
_c = 0
for _i in range(4):
    for _nm in ("mixn", "memn", "ffnn"):
        PAR[(_nm, _i)] = _c
        _c += 8
for _j in range(2):
    PAR[("bgate", _j)] = _c
    _c += 4
PAR["flag"] = _c
_c += 1
NPAR = _c
BCP = {"fnorm": 0}
_c = 1024
for _j in range(2):
    BCP[("hnorm", _j)] = _c
    _c += 256
NBCP = _c

WEIGHT_SHAPES = {}
for _i in range(4):
    WEIGHT_SHAPES[f"ffn_in{_i}"] = [44, 128, 1024]
    WEIGHT_SHAPES[f"ffn_out{_i}"] = [22, 128, 1024]
    WEIGHT_SHAPES[f"kvk{_i}"] = [8, 128, 1024]
    WEIGHT_SHAPES[f"kvv{_i}"] = [2, 128, 4096]
    WEIGHT_SHAPES[f"xq{_i}"] = [8, 128, 1024]
    WEIGHT_SHAPES[f"wo{_i}"] = [16 if _i % 2 == 0 else 24, 128, 1024]
for _j in range(2):
    WEIGHT_SHAPES[f"gq{_j}"] = [4, 128, 1024]
    WEIGHT_SHAPES[f"gk{_j}"] = [4, 128, 1024]
    WEIGHT_SHAPES[f"gv{_j}"] = [4, 128, 2048]
    WEIGHT_SHAPES[f"gg{_j}"] = [4, 128, 2048]
    WEIGHT_SHAPES[f"glr{_j}"] = [128, 128]
    WEIGHT_SHAPES[f"gw2{_j}"] = [16, 512]
    WEIGHT_SHAPES[f"sz{_j}"] = [8, 128, 2048]
    WEIGHT_SHAPES[f"sxbc{_j}"] = [32, 128, 1024]
    WEIGHT_SHAPES[f"sdt{_j}"] = [128, 1024]
    WEIGHT_SHAPES[f"scv{_j}"] = [128, 164]
    WEIGHT_SHAPES[f"sdk{_j}"] = [128, 2048]
    WEIGHT_SHAPES[f"snf{_j}"] = [128, 16]


def blk_layout(w, ncol=128):
    K_, C = w.shape
    nb = C // ncol
    a = w.reshape(K_ // 128, 128, nb, ncol).transpose(2, 1, 0, 3)
    return np.ascontiguousarray(a.reshape(nb, 128, (K_ // 128) * ncol))


def fm_vec(v):
    return np.ascontiguousarray(v.reshape(-1, 128).T)


def prep_inputs(inp, used=None):
    f = lambda a: np.asarray(a, dtype=np.float32)
    need = lambda nm: used is None or nm in used
    shared = {}
    par = np.zeros((128, NPAR), np.float32)
    for i in range(4):
        par[:, PAR[("mixn", i)]:PAR[("mixn", i)] + 8] = fm_vec(f(inp["mix_norm"])[i])
        par[:, PAR[("memn", i)]:PAR[("memn", i)] + 8] = fm_vec(f(inp["mem_norm"])[i])
        par[:, PAR[("ffnn", i)]:PAR[("ffnn", i)] + 8] = fm_vec(f(inp["ffn_norm"])[i])
    for j in range(2):
        par[:, PAR[("bgate", j)]:PAR[("bgate", j)] + 4] = fm_vec(f(inp["gla_b_gate"])[j])
    bcp = np.zeros((128, NBCP), np.float32)
    bcp[:, 0:1024] = f(inp["final_norm"])[None, :]
    for j in range(2):
        bcp[:, BCP[("hnorm", j)]:BCP[("hnorm", j)] + 256] = f(inp["gla_head_norm"])[j][None, :]
    shared["bcp"] = bcp
    for i in range(4):
        if need(f"ffn_in{i}"):
            shared[f"ffn_in{i}"] = blk_layout(f(inp["ffn_w_in"])[i])
            shared[f"ffn_out{i}"] = np.ascontiguousarray(f(inp["ffn_w_out"])[i].reshape(22, 128, 1024))
        if need(f"kvk{i}"):
            wkv = f(inp["w_mem_kv"])[i]
            shared[f"kvk{i}"] = blk_layout(wkv[:, 0:1024])
            shared[f"kvv{i}"] = blk_layout(wkv[:, 1024:2048], 512)
            j = i // 2
            if i % 2 == 0:
                w = f(inp["gla_w_in"])[j]
                shared[f"xq{i}"] = blk_layout(w[:, 3088:4112])
                shared[f"wo{i}"] = np.ascontiguousarray(f(inp["gla_w_out"])[j].reshape(16, 128, 1024))
                shared[f"gq{j}"] = blk_layout(w[:, 0:512])
                shared[f"gk{j}"] = blk_layout(w[:, 512:1024])
                shared[f"gv{j}"] = blk_layout(w[:, 1024:2048], 256)
                shared[f"gg{j}"] = blk_layout(w[:, 2048:3072], 256)
                shared[f"glr{j}"] = blk_layout(w[:, 3072:3088], 16)[0]
                shared[f"gw2{j}"] = np.ascontiguousarray(f(inp["gla_w_gate2"])[j])
            else:
                w = f(inp["ssd_w_in"])[j]
                shared[f"xq{i}"] = blk_layout(w[:, 6176:7200])
                shared[f"wo{i}"] = np.ascontiguousarray(f(inp["ssd_w_out"])[j].reshape(24, 128, 1024))
                shared[f"sz{j}"] = blk_layout(w[:, 0:2048], 256)
                shared[f"sxbc{j}"] = blk_layout(w[:, 2048:6144])
                shared[f"sdt{j}"] = blk_layout(np.tile(w[:, 6144:6176], (1, 4)))[0]
                scv = np.zeros((128, 164), np.float32)
                cw = f(inp["ssd_conv_w"])[j]
                scv[:, 0:128] = cw.reshape(4, 32, 128).transpose(2, 1, 0).reshape(128, 128)
                scv[:, 128:160] = fm_vec(f(inp["ssd_conv_b"])[j])
                scv[:, 160] = np.tile(f(inp["ssd_dt_bias"])[j], 4)
                scv[:, 161] = np.tile(f(inp["ssd_a_log"])[j], 4)
                shared[f"scv{j}"] = scv
                shared[f"sdk{j}"] = np.ascontiguousarray(np.broadcast_to(np.repeat(f(inp["ssd_d"])[j], 64)[None, :], (128, 2048)))
                shared[f"snf{j}"] = fm_vec(f(inp["ssd_norm"])[j])
    x = f(inp["x"])
    mem = f(inp["mem"])
    maps = []
    for c in range(8):
        b, half = c // 2, c % 2
        m = dict(shared)
        p = par.copy()
        p[:, PAR["flag"]] = float(half)
        m["par"] = p
        m["x"] = np.ascontiguousarray(x[b, half * T:(half + 1) * T])
        m["mem"] = np.ascontiguousarray(mem[b])
        maps.append(m)
    return maps


def ps_bank(k):
    i = k.ps_i
    k.ps_i = (i + 1) % 8
    return i


def PSV(k, i, ap=None):
    return V(k.psum[i][:, :] if ap is None else ap, [("P", i)])


def emit_program(k, cfg, x_d, mem_d, par_d, bcp_d, out_d):
    P, A, nc = k.P, k.A, k.nc
    k.H = Tile(A, [NT, D], F32)
    k.PAR = Tile(A, [NPAR], F32)
    k.IDB = Tile(A, [128], BF16)
    k.IDF = Tile(A, [128], F32)
    k.RS = Tile(A, [NT], F32)
    k.SS = Tile(A, [NT], F32)
    k.XN = Tile(A, [KC, T], BF16)
    k.JUNK = Tile(A, [D], BF16)
    k.HS = Tile(A, [D], BF16)
    H = k.H

    for j in range(NT):
        hv = H.v(H.ap[:, j, :], j * D, D)
        P.dma("SP", hv.ap, x_d[j * 128:(j + 1) * 128, :], writes=[hv])
    P.dma("SP", k.PAR.ap, par_d[:, :], writes=[k.PAR.full()])

    for idt in (k.IDB, k.IDF):
        fv = idt.full()
        P.op("POOL", lambda e, a=idt.ap: e.memset(a, 0.0), writes=[fv])
        P.op("POOL", lambda e, a=idt.ap: e.affine_select(
            out=a, in_=a, pattern=[[-1, 128]], compare_op=ALU.not_equal, fill=1.0,
            base=0, channel_multiplier=1), reads=[fv], writes=[fv])

    k.ONE1 = Tile(A, [1], F32)
    k.LNQ = Tile(A, [1], F32)
    k.EPSC = Tile(A, [1], F32)
    P.op("POOL", lambda e: e.memset(k.EPSC.ap, EPS), writes=[k.EPSC.full()])
    P.op("POOL", lambda e: e.memset(k.ONE1.ap, 1.0), writes=[k.ONE1.full()])
    P.op("POOL", lambda e: e.memset(k.LNQ.ap, float(np.log(128.0 ** -0.5))), writes=[k.LNQ.full()])
    emit_consts(k)
    m0 = A.mark()
    for sl in cfg["sublayers"]:
        kind, i = sl
        A.release(m0)
        if kind == "ffn":
            emit_ffn(k, i)
        elif kind == "mix":
            emit_mixer(k, i, mem_d)
    A.release(m0)
    emit_output(k, cfg, bcp_d, out_d)


def hview(k, j, c0=0, n=D):
    H = k.H
    return H.v(H.ap[:, j, c0:c0 + n], j * D + c0, n)


def emit_norm_T(k, srcs, dst, dst_T, wcol):
    P = k.P
    SS, RS = k.SS, k.RS
    n = len(srcs)
    ssf, rsf = SS.full(), RS.full()
    junk = k.JUNK.full()
    for j, hv in enumerate(srcs):
        P.op("ACT", lambda e, hv=hv, j=j: e.activation(
            out=k.JUNK.ap, in_=hv.ap, func=AF.Square, accum_out=SS.ap[:, j:j + 1]),
            reads=[hv], writes=[junk, ssf])
    P.op("DVE", lambda e: e.tensor_scalar(out=RS.ap[:, 0:n], in0=SS.ap[:, 0:n], scalar1=1.0 / D, scalar2=EPS,
                                           op0=ALU.mult, op1=ALU.add), reads=[ssf], writes=[rsf])
    P.op("DVE", lambda e: e.reciprocal(out=RS.ap[:, 0:n], in_=RS.ap[:, 0:n]), reads=[rsf], writes=[rsf])
    P.op("ACT", lambda e: e.activation(out=RS.ap[:, 0:n], in_=RS.ap[:, 0:n], func=AF.Sqrt), reads=[rsf], writes=[rsf])
    hsf = k.HS.full()
    wv = k.PAR.v(k.PAR.ap[:, wcol:wcol + 8], wcol, 8)
    for j, hv in enumerate(srcs):
        P.op("ACT", lambda e, hv=hv, j=j: e.activation(
            out=k.HS.ap, in_=hv.ap, func=AF.Copy, scale=RS.ap[:, j:j + 1]),
            reads=[hv, rsf], writes=[hsf])
        b = ps_bank(k)
        pb = k.psum[b][:, :].bitcast(BF16)
        pv = PSV(k, b)
        for kc in range(KC):
            P.op("PE", lambda e, kc=kc, pb=pb: e.transpose(
                out=pb[:, kc * 128:(kc + 1) * 128], in_=k.HS.ap[:, kc * 128:(kc + 1) * 128],
                identity=k.IDB.ap), reads=[hsf, k.IDB.full()], writes=[pv])
        xv = dst.vs(dst.ap[:, :, j * 128:(j + 1) * 128], [(kc * dst_T + j * 128, 128) for kc in range(KC)])
        P.op("DVE", lambda e, pb=pb, xv=xv, wv=wv: e.tensor_tensor(
            out=xv.ap, in0=pb.rearrange("p (a b) -> p a b", a=KC),
            in1=wv.ap.unsqueeze(2).to_broadcast([128, KC, 128]), op=ALU.mult),
            reads=[pv, wv], writes=[xv])


def emit_rmsnorm_xn(k, wcol):
    emit_norm_T(k, [hview(k, j) for j in range(NT)], k.XN, T, wcol)


def xn_blk(k, kc, t0, n):
    XN = k.XN
    return XN.v(XN.ap[:, kc, t0:t0 + n], kc * T + t0, n)


def emit_ffn(k, i):
    P, A, dr = k.P, k.A, k.dr
    emit_rmsnorm_xn(k, PAR[("ffnn", i)])
    NG = 11
    ACTT = Tile(A, [NG, T], BF16)
    WOs = [Tile(A, [NG, D], BF16) for _ in range(2)]
    WR = [Tile(A, [KC, 128], BF16) for _ in range(4)]
    SG = [Tile(A, [512], F32) for _ in range(2)]
    win, wout = k.w(f"ffn_in{i}"), k.w(f"ffn_out{i}")
    wr_i = 0
    sg_i = 0
    for grp in range(2):
        WO = WOs[grp]
        for c in range(NG):
            wov = WO.v(WO.ap[:, c, :], c * D, D)
            P.dma("POOL", wov.ap, wout[grp * NG + c, :, :], writes=[wov])
    for grp in range(2):
        WO = WOs[grp]
        for c in range(NG):
            blk = grp * NG + c
            wg, wu = WR[wr_i % 4], WR[(wr_i + 1) % 4]
            wr_i += 2
            P.dma("POOL", wg.ap.rearrange("p a b -> p (a b)"), win[blk, :, :], writes=[wg.full()])
            P.dma("POOL", wu.ap.rearrange("p a b -> p (a b)"), win[22 + blk, :, :], writes=[wu.full()])
            for tb in range(4):
                bg, bu = ps_bank(k), ps_bank(k)
                for (w_, b_) in ((wg, bg), (wu, bu)):
                    for kc in range(KC):
                        xv = xn_blk(k, kc, tb * 512, 512)
                        P.op("PE", lambda e, w_=w_, b_=b_, kc=kc, xv=xv: e.matmul(
                            k.psum[b_][:, :], lhsT=w_.ap[:, kc, :], rhs=xv.ap,
                            start=(kc == 0), stop=(kc == KC - 1)),
                            reads=[w_.full(), xv], writes=[PSV(k, b_)])
                sg = SG[sg_i % 2]
                sg_i += 1
                P.op("ACT", lambda e, sg=sg, bg=bg: e.activation(
                    out=sg.ap, in_=k.psum[bg][:, :], func=AF.Silu),
                    reads=[PSV(k, bg)], writes=[sg.full()])
                av = ACTT.v(ACTT.ap[:, c, tb * 512:(tb + 1) * 512], c * T + tb * 512, 512)
                P.op("DVE", lambda e, sg=sg, bu=bu, av=av: e.tensor_tensor(
                    out=av.ap, in0=k.psum[bu][:, :], in1=sg.ap, op=ALU.mult),
                    reads=[PSV(k, bu), sg.full()], writes=[av])
        for j in range(NT):
            for nb in range(2):
                b = ps_bank(k)
                for c in range(NG):
                    av = ACTT.v(ACTT.ap[:, c, j * 128:(j + 1) * 128], c * T + j * 128, 128)
                    wov = WO.v(WO.ap[:, c, nb * 512:(nb + 1) * 512], c * D + nb * 512, 512)
                    P.op("PE", lambda e, av=av, wov=wov, b=b, c=c: e.matmul(
                        k.psum[b][:, :], lhsT=av.ap, rhs=wov.ap, start=(c == 0), stop=(c == NG - 1)),
                        reads=[av, wov], writes=[PSV(k, b)])
                hv = hview(k, j, nb * 512, 512)
                P.op("DVE", lambda e, hv=hv, b=b: e.tensor_tensor(
                    out=hv.ap, in0=k.psum[b][:, :], in1=hv.ap, op=ALU.add),
                    reads=[PSV(k, b), hv], writes=[hv])


def load_w(k, tile, src, queue="POOL"):
    ap = tile.ap
    if len(tile.dims) == 2:
        ap = ap.rearrange("p a b -> p (a b)")
    elif len(tile.dims) == 3:
        ap = ap.rearrange("p a b c -> p (a b c)")
    return k.P.dma(queue, ap, src, writes=[tile.full()], max_dma_last_dim=4096)


def inproj_fm(k, wt, t0, n, M=128, wcol0=0):
    b = ps_bank(k)
    for kc in range(KC):
        xv = xn_blk(k, kc, t0, n)
        k.P.op("PE", lambda e, kc=kc, xv=xv, b=b: e.matmul(
            k.psum[b][0:M, 0:n], lhsT=wt.ap[:, kc, wcol0:wcol0 + M], rhs=xv.ap,
            start=(kc == 0), stop=(kc == KC - 1)), reads=[wt.full(), xv], writes=[PSV(k, b)])
    return b


def inproj_tm(k, wt, j, ncols, src=None, srcT=T):
    b = ps_bank(k)
    src = src or k.XN
    for kc in range(KC):
        xv = src.v(src.ap[:, kc, j * 128:(j + 1) * 128], kc * srcT + j * 128, 128)
        k.P.op("PE", lambda e, kc=kc, xv=xv, b=b: e.matmul(
            k.psum[b][:, 0:ncols], lhsT=xv.ap, rhs=wt.ap[:, kc, 0:ncols],
            start=(kc == 0), stop=(kc == KC - 1)), reads=[wt.full(), xv], writes=[PSV(k, b)])
    return b


def emit_consts(k):
    P, A = k.P, k.A
    k.MASKU = Tile(A, [128], F32)
    k.ONESB = Tile(A, [128], BF16)
    fv = k.MASKU.full()
    P.op("POOL", lambda e: e.memset(k.MASKU.ap, 1.0), writes=[fv])
    P.op("POOL", lambda e: e.affine_select(
        out=k.MASKU.ap, in_=k.MASKU.ap, pattern=[[1, 128]], compare_op=ALU.is_ge, fill=0.0,
        base=0, channel_multiplier=-1), reads=[fv], writes=[fv])
    P.op("POOL", lambda e: e.memset(k.ONESB.ap, 1.0), writes=[k.ONESB.full()])
    k.FLAG = k.PAR.v(k.PAR.ap[:, PAR["flag"]:PAR["flag"] + 1], PAR["flag"], 1)


def emit_mixer(k, i, mem_d):
    P, A = k.P, k.A
    emit_rmsnorm_xn(k, PAR[("mixn", i)])
    xctx = None
    if i % 2 == 1:
        XH32 = Tile(A, [KC, 3], F32)
        XHR = Tile(A, [KC * 3], F32)
        k.XHB = Tile(A, [KC, 3], BF16)
        xl = k.XN.vs(k.XN.ap[:, :, T - 3:T], [(kc * T + T - 3, 3) for kc in range(KC)])
        P.op("ACT", lambda e: e.activation(out=XH32.ap, in_=xl.ap, func=AF.Copy), reads=[xl], writes=[XH32.full()])
        xctx = exchange_start(k, V(XH32.ap.rearrange("p a b -> p (a b)"), XH32.full().cells), 128, KC * 3, f"hx{i}")
    m = A.mark()
    emit_xattn(k, i, mem_d)
    A.release(m)
    if xctx is not None:
        exchange_finish(k, xctx, XHR)
        P.op("ACT", lambda e: e.activation(out=k.XHB.ap.rearrange("p a b -> p (a b)"), in_=XHR.ap, func=AF.Copy),
             reads=[XHR.full()], writes=[k.XHB.full()])
        m = A.mark()
    if i % 2 == 0:
        emit_gla(k, i)
    else:
        emit_ssd(k, i)
    A.release(m)


def emit_xattn(k, i, mem_d):
    P, A = k.P, k.A
    mixdim = 1024 if i % 2 == 0 else 2048
    MEMT = Tile(A, [2, D], F32)
    MN = Tile(A, [KC, 256], BF16)
    KmT = Tile(A, [8, 256], BF16)
    Vm = Tile(A, [2, 1024], BF16)
    WR = [Tile(A, [KC, 128], BF16) for _ in range(3)]
    m1 = A.mark()
    WV = Tile(A, [KC, 512], BF16)
    srcs = []
    for mt in range(2):
        mv = MEMT.v(MEMT.ap[:, mt, :], mt * D, D)
        P.dma("SP", mv.ap, mem_d[mt * 128:(mt + 1) * 128, :], writes=[mv])
        srcs.append(mv)
    emit_norm_T(k, srcs, MN, 256, PAR[("memn", i)])
    kvk, kvv = k.w(f"kvk{i}"), k.w(f"kvv{i}")
    wi = 0
    for blk in range(8):
        w = WR[wi % 3]
        wi += 1
        load_w(k, w, kvk[blk, :, :])
        b = ps_bank(k)
        for kc in range(KC):
            mv = MN.v(MN.ap[:, kc, :], kc * 256, 256)
            P.op("PE", lambda e, kc=kc, mv=mv, b=b, w=w: e.matmul(
                k.psum[b][:, 0:256], lhsT=w.ap[:, kc, :], rhs=mv.ap, start=(kc == 0), stop=(kc == KC - 1)),
                reads=[w.full(), mv], writes=[PSV(k, b)])
        kv = KmT.v(KmT.ap[:, blk, :], blk * 256, 256)
        P.op("ACT", lambda e, kv=kv, b=b: e.activation(out=kv.ap, in_=k.psum[b][:, 0:256], func=AF.Copy,
                                                         scale=1.0 / 16.0), reads=[PSV(k, b)], writes=[kv])
    for nb in range(2):
        load_w(k, WV, kvv[nb, :, :])
        for mt in range(2):
            b = inproj_tm(k, WV, mt, 512, src=MN, srcT=256)
            vv = Vm.v(Vm.ap[:, mt, nb * 512:(nb + 1) * 512], mt * 1024 + nb * 512, 512)
            P.op("ACT", lambda e, vv=vv, b=b: e.activation(out=vv.ap, in_=k.psum[b][:, :], func=AF.Copy),
                 reads=[PSV(k, b)], writes=[vv])
    A.release(m1)
    XQT = Tile(A, [2, T], BF16)
    XAT = Tile(A, [8, T], BF16)
    ET = [Tile(A, [512], BF16) for _ in range(2)]
    RD = Tile(A, [512], F32)
    WOX = Tile(A, [8, D], BF16)
    xqw, wo = k.w(f"xq{i}"), k.w(f"wo{i}")
    for c in range(8):
        wv = WOX.v(WOX.ap[:, c, :], c * D, D)
        P.dma("POOL", wv.ap, wo[mixdim // 128 + c, :, :], writes=[wv], max_dma_last_dim=4096)
    for a in range(4):
        for dc in range(2):
            w = WR[wi % 3]
            wi += 1
            load_w(k, w, xqw[a * 2 + dc, :, :])
            for tb in range(4):
                b = inproj_fm(k, w, tb * 512, 512)
                xv = XQT.v(XQT.ap[:, dc, tb * 512:(tb + 1) * 512], dc * T + tb * 512, 512)
                P.op("ACT", lambda e, xv=xv, b=b: e.activation(out=xv.ap, in_=k.psum[b][:, :], func=AF.Copy),
                     reads=[PSV(k, b)], writes=[xv])
        for tb in range(4):
            for mt in range(2):
                b = ps_bank(k)
                for dc in range(2):
                    kv = KmT.v(KmT.ap[:, a * 2 + dc, mt * 128:(mt + 1) * 128], (a * 2 + dc) * 256 + mt * 128, 128)
                    xv = XQT.v(XQT.ap[:, dc, tb * 512:(tb + 1) * 512], dc * T + tb * 512, 512)
                    P.op("PE", lambda e, kv=kv, xv=xv, b=b, dc=dc: e.matmul(
                        k.psum[b][:, :], lhsT=kv.ap, rhs=xv.ap, start=(dc == 0), stop=(dc == 1)),
                        reads=[kv, xv], writes=[PSV(k, b)])
                P.op("ACT", lambda e, b=b, mt=mt: e.activation(out=ET[mt].ap, in_=k.psum[b][:, :], func=AF.Exp),
                     reads=[PSV(k, b)], writes=[ET[mt].full()])
            b = ps_bank(k)
            for mt in range(2):
                P.op("PE", lambda e, b=b, mt=mt: e.matmul(
                    k.psum[b][:, :], lhsT=k.ONESB.ap, rhs=ET[mt].ap, start=(mt == 0), stop=(mt == 1)),
                    reads=[k.ONESB.full(), ET[mt].full()], writes=[PSV(k, b)])
            P.op("DVE", lambda e, b=b: e.reciprocal(out=RD.ap, in_=k.psum[b][:, :]),
                 reads=[PSV(k, b)], writes=[RD.full()])
            for dc in range(2):
                b = ps_bank(k)
                for mt in range(2):
                    c0 = a * 256 + dc * 128
                    vv = Vm.v(Vm.ap[:, mt, c0:c0 + 128], mt * 1024 + c0, 128)
                    P.op("PE", lambda e, vv=vv, b=b, mt=mt: e.matmul(
                        k.psum[b][:, :], lhsT=vv.ap, rhs=ET[mt].ap, start=(mt == 0), stop=(mt == 1)),
                        reads=[vv, ET[mt].full()], writes=[PSV(k, b)])
                xav = XAT.v(XAT.ap[:, a * 2 + dc, tb * 512:(tb + 1) * 512], (a * 2 + dc) * T + tb * 512, 512)
                P.op("DVE", lambda e, xav=xav, b=b: e.tensor_tensor(
                    out=xav.ap, in0=k.psum[b][:, :], in1=RD.ap, op=ALU.mult),
                    reads=[PSV(k, b), RD.full()], writes=[xav])
    emit_outproj(k, XAT, WOX, 8)


def emit_outproj(k, XT, WO, nk, scale_col=None):
    P = k.P
    for j in range(NT):
        for nb in range(2):
            b = ps_bank(k)
            for c in range(nk):
                av = XT.v(XT.ap[:, c, j * 128:(j + 1) * 128], c * T + j * 128, 128)
                wv = WO.v(WO.ap[:, c, nb * 512:(nb + 1) * 512], c * D + nb * 512, 512)
                P.op("PE", lambda e, av=av, wv=wv, b=b, c=c: e.matmul(
                    k.psum[b][:, :], lhsT=av.ap, rhs=wv.ap, start=(c == 0), stop=(c == nk - 1)),
                    reads=[av, wv], writes=[PSV(k, b)])
            hv = hview(k, j, nb * 512, 512)
            if scale_col is None:
                P.op("DVE", lambda e, hv=hv, b=b: e.tensor_tensor(
                    out=hv.ap, in0=k.psum[b][:, :], in1=hv.ap, op=ALU.add),
                    reads=[PSV(k, b), hv], writes=[hv])
            else:
                sc = scale_col(j)
                P.op("DVE", lambda e, hv=hv, b=b, sc=sc: e.scalar_tensor_tensor(
                    out=hv.ap, in0=k.psum[b][:, :], scalar=sc.ap, in1=hv.ap, op0=ALU.mult, op1=ALU.add),
                    reads=[PSV(k, b), hv, sc], writes=[hv])


def exchange_start(k, src_v, rows, cols, name):
    P, nc = k.P, k.nc
    ib = nc.dram_tensor(name + "_i", [rows, cols], F32)
    ob = nc.dram_tensor(name + "_o", [2 * rows, cols], F32)
    ci, co = V(None, [("D", name + "_i")]), V(None, [("D", name + "_o")])
    P.dma("POOL", ib[:, :], src_v.ap, reads=[src_v], writes=[ci])
    P.coll(lambda e: e.collective_compute("AllGather", ALU.bypass, replica_groups=PAIRS,
                                          ins=[ib.ap().opt()], outs=[ob.ap().opt()]),
           reads=[ci], writes=[co])
    return (ob, co, rows)


def exchange_finish(k, ctx, dst_tile):
    P = k.P
    ob, co, rows = ctx
    dv = V(dst_tile.ap[0:rows], dst_tile.full().cells)
    P.dma("POOL", dv.ap, ob[0:rows, :], reads=[co], writes=[dv])
    P.op("DVE", lambda e: e.tensor_scalar(out=dv.ap, in0=dv.ap, scalar1=k.FLAG.ap[0:rows], scalar2=None, op0=ALU.mult),
         reads=[dv, k.FLAG], writes=[dv])


def exchange(k, src_v, rows, cols, dst_tile, name):
    exchange_finish(k, exchange_start(k, src_v, rows, cols, name), dst_tile)


def emit_gla(k, i):
    P, A = k.P, k.A
    j = i // 2
    GLRT = Tile(A, [T], BF16)
    W2 = Tile(A, [512], BF16)
    SPL = Tile(A, [T], F32)
    CS = Tile(A, [T], F32)
    FQ = Tile(A, [512], F32)
    FK = Tile(A, [512], F32)
    FE = Tile(A, [512], F32)
    QT = Tile(A, [T], BF16, off=SPL.off)
    QT2 = Tile(A, [T], BF16, off=SPL.off + T * 2)
    KNT = Tile(A, [T], BF16)
    KET = Tile(A, [T], BF16)
    KE = Tile(A, [NT, 128], BF16)
    Vt = Tile(A, [NT, 256], BF16)
    Gt = Tile(A, [NT, 256], BF16)
    SLOC = Tile(A, [NT, 256], BF16)
    SM = Tile(A, [16 * 6 + 8], F32)
    CSS = SM.ap[:, 0:16]
    CSE = SM.ap[:, 16:32]
    DEC = SM.ap[:, 32:48]
    DCUM = SM.ap[:, 48:64]
    NBG = SM.ap[:, 64:68]
    SSo = SM.ap[:, 68:69]
    RSo = SM.ap[:, 69:70]
    smf = SM.full()
    TTs = [Tile(A, [256], F32) for _ in range(4)]
    MTOKs = [Tile(A, [256], BF16) for _ in range(4)]
    STs = [Tile(A, [128], BF16) for _ in range(4)]
    SRs = [Tile(A, [2], F32) for _ in range(4)]
    MIXT = Tile(A, [2, T], BF16)
    WOH = Tile(A, [2, D], BF16)
    HNB = Tile(A, [256], F32)
    S = Tile(A, [256], F32)
    SB32 = Tile(A, [256], F32)
    SBb = Tile(A, [256], BF16)
    WA = Tile(A, [KC, 128], BF16)
    WB = Tile(A, [KC, 128], BF16)
    WC = Tile(A, [KC, 256], BF16)
    WD = Tile(A, [KC, 256], BF16)
    gq, gk, gv, gg = k.w(f"gq{j}"), k.w(f"gk{j}"), k.w(f"gv{j}"), k.w(f"gg{j}")
    wo = k.w(f"wo{i}")
    bcp = k.dr["bcp"]
    P.dma("SP", HNB.ap, bcp[:, BCP[("hnorm", j)]:BCP[("hnorm", j)] + 256], writes=[HNB.full()])
    bg = PAR[("bgate", j)]
    pv = k.PAR.v(k.PAR.ap[:, bg:bg + 4], bg, 4)
    P.op("DVE", lambda e: e.tensor_scalar(out=NBG, in0=pv.ap, scalar1=-1.0, scalar2=None, op0=ALU.mult),
         reads=[pv], writes=[smf])
    WGL = Tile(A, [KC, 16], BF16)
    load_w(k, WGL, k.w(f"glr{j}")[:, :])
    P.dma("POOL", W2.ap[0:16, :], k.w(f"gw2{j}")[:, :], writes=[W2.full()])
    for tb in range(4):
        b = inproj_fm(k, WGL, tb * 512, 512, M=16)
        gv_ = GLRT.v(GLRT.ap[0:16, tb * 512:(tb + 1) * 512], tb * 512, 512)
        P.op("ACT", lambda e, gv_=gv_, b=b: e.activation(out=gv_.ap, in_=k.psum[b][0:16, :], func=AF.Copy),
             reads=[PSV(k, b)], writes=[gv_])
    for h in range(4):
        for tb in range(4):
            b = ps_bank(k)
            gv_ = GLRT.v(GLRT.ap[0:16, tb * 512:(tb + 1) * 512], tb * 512, 512)
            P.op("PE", lambda e, b=b, gv_=gv_, h=h: e.matmul(
                k.psum[b][:, :], lhsT=W2.ap[0:16, h * 128:(h + 1) * 128], rhs=gv_.ap, start=True, stop=True),
                reads=[W2.full(), gv_], writes=[PSV(k, b)])
            sv = SPL.v(SPL.ap[:, tb * 512:(tb + 1) * 512], tb * 512, 512)
            P.op("ACT", lambda e, b=b, sv=sv, h=h: e.activation(
                out=sv.ap, in_=k.psum[b][:, :], func=AF.Exp, scale=-1.0, bias=NBG[:, h:h + 1]),
                reads=[PSV(k, b), smf], writes=[sv])
        P.op("ACT", lambda e: e.activation(out=SPL.ap, in_=SPL.ap, func=AF.Ln, bias=1.0),
             reads=[SPL.full()], writes=[SPL.full()])
        P.op("DVE", lambda e: e.tensor_tensor_scan(
            out=CS.ap, data0=k.ONE1.ap.to_broadcast([128, T]), data1=SPL.ap, initial=0.0,
            op0=ALU.mult, op1=ALU.add), reads=[SPL.full(), k.ONE1.full()], writes=[CS.full()])
        cs3 = CS.ap.rearrange("p (n c) -> p n c", c=128)
        P.op("DVE", lambda e: e.memset(SM.ap[:, 0:1], 0.0), writes=[smf])
        P.op("DVE", lambda e: e.tensor_copy(out=SM.ap[:, 1:16], in_=cs3[:, 0:15, 127]), reads=[CS.full()], writes=[smf])
        P.op("DVE", lambda e: e.tensor_copy(out=CSE, in_=cs3[:, :, 127]), reads=[CS.full()], writes=[smf])
        P.op("DVE", lambda e: e.tensor_tensor(out=DEC, in0=CSE, in1=CSS, op=ALU.subtract), reads=[smf], writes=[smf])
        P.op("ACT", lambda e: e.activation(out=DEC, in_=DEC, func=AF.Exp, scale=-1.0 / 16), reads=[smf], writes=[smf])
        P.op("ACT", lambda e: e.activation(out=DCUM, in_=CSS, func=AF.Exp, scale=-1.0 / 16), reads=[smf], writes=[smf])
        P.op("DVE", lambda e: e.tensor_tensor(
            out=cs3, in0=cs3, in1=CSS.unsqueeze(2).to_broadcast([128, NT, 128]), op=ALU.subtract),
            reads=[CS.full(), smf], writes=[CS.full()])
        load_w(k, WB, gk[h, :, :])
        for tb in range(4):
            dv = CS.v(CS.ap[:, tb * 512:(tb + 1) * 512], tb * 512, 512)
            P.op("ACT", lambda e, dv=dv: e.activation(out=FK.ap, in_=dv.ap, func=AF.Exp, scale=1.0 / 16),
                 reads=[dv], writes=[FK.full()])
            fe3 = FE.ap.rearrange("p (n c) -> p n c", c=128)
            d3 = dv.ap.rearrange("p (n c) -> p n c", c=128)
            P.op("DVE", lambda e, d3=d3, fe3=fe3, tb=tb: e.tensor_tensor(
                out=fe3, in0=d3, in1=CSE[:, tb * 4:(tb + 1) * 4].unsqueeze(2).to_broadcast([128, 4, 128]),
                op=ALU.subtract), reads=[dv, smf], writes=[FE.full()])
            P.op("DVE", lambda e, fe3=fe3, tb=tb: e.tensor_tensor(
                out=fe3, in0=fe3, in1=CSS[:, tb * 4:(tb + 1) * 4].unsqueeze(2).to_broadcast([128, 4, 128]),
                op=ALU.add), reads=[FE.full(), smf], writes=[FE.full()])
            P.op("ACT", lambda e: e.activation(out=FE.ap, in_=FE.ap, func=AF.Exp, scale=1.0 / 16),
                 reads=[FE.full()], writes=[FE.full()])
            b = inproj_fm(k, WB, tb * 512, 512)
            kn = KNT.v(KNT.ap[:, tb * 512:(tb + 1) * 512], tb * 512, 512)
            ke = KET.v(KET.ap[:, tb * 512:(tb + 1) * 512], tb * 512, 512)
            P.op("DVE", lambda e, kn=kn, b=b: e.tensor_tensor(out=kn.ap, in0=k.psum[b][:, :], in1=FK.ap, op=ALU.mult),
                 reads=[PSV(k, b), FK.full()], writes=[kn])
            P.op("DVE", lambda e, ke=ke, b=b: e.tensor_tensor(out=ke.ap, in0=k.psum[b][:, :], in1=FE.ap, op=ALU.mult),
                 reads=[PSV(k, b), FE.full()], writes=[ke])
        for g8 in range(2):
            b = ps_bank(k)
            pb = k.psum[b][:, :].bitcast(BF16)
            for t8 in range(8):
                n = g8 * 8 + t8
                ke = KET.v(KET.ap[:, n * 128:(n + 1) * 128], n * 128, 128)
                P.op("PE", lambda e, ke=ke, pb=pb, t8=t8: e.transpose(
                    out=pb[:, t8 * 128:(t8 + 1) * 128], in_=ke.ap, identity=k.IDB.ap),
                    reads=[ke, k.IDB.full()], writes=[PSV(k, b)])
            kv = KE.v(KE.ap[:, g8 * 8:(g8 + 1) * 8, :], g8 * 8 * 128, 8 * 128)
            P.op("ACT", lambda e, kv=kv, pb=pb: e.activation(
                out=kv.ap, in_=pb.rearrange("p (a b) -> p a b", a=8), func=AF.Copy),
                reads=[PSV(k, b)], writes=[kv])
        load_w(k, WC, gv[h, :, :])
        P.op("DVE", lambda e: e.memset(S.ap, 0.0), writes=[S.full()])
        P.op("POOL", lambda e: e.memset(SLOC.ap[:, 0, :], 0.0), writes=[SLOC.v(SLOC.ap[:, 0, :], 0, 256)])

        def rec_step(n):
            b = ps_bank(k)
            kev = KE.v(KE.ap[:, n, :], n * 128, 128)
            vv = Vt.v(Vt.ap[:, n, :], n * 256, 256)
            P.op("PE", lambda e: e.matmul(k.psum[b][:, 0:256], lhsT=kev.ap, rhs=vv.ap, start=True, stop=True),
                 reads=[kev, vv], writes=[PSV(k, b)])
            P.op("DVE", lambda e: e.scalar_tensor_tensor(
                out=S.ap, in0=S.ap, scalar=DEC[:, n:n + 1], in1=k.psum[b][:, 0:256], op0=ALU.mult, op1=ALU.add),
                reads=[S.full(), smf, PSV(k, b)], writes=[S.full()])
            if n < NT - 1:
                sl = SLOC.v(SLOC.ap[:, n + 1, :], (n + 1) * 256, 256)
                P.op("ACT", lambda e: e.activation(out=sl.ap, in_=S.ap, func=AF.Copy),
                     reads=[S.full()], writes=[sl])

        for n in range(NT + 2):
            if n < NT:
                b = inproj_tm(k, WC, n, 256)
                vv = Vt.v(Vt.ap[:, n, :], n * 256, 256)
                P.op("ACT", lambda e, vv=vv, b=b: e.activation(out=vv.ap, in_=k.psum[b][:, 0:256], func=AF.Copy),
                     reads=[PSV(k, b)], writes=[vv])
            if 0 <= n - 2 < NT:
                rec_step(n - 2)
        xctx = exchange_start(k, S.full(), 128, 256, f"gx{i}_{h}")
        load_w(k, WA, gq[h, :, :])
        for tb in range(4):
            dv = CS.v(CS.ap[:, tb * 512:(tb + 1) * 512], tb * 512, 512)
            P.op("ACT", lambda e, dv=dv: e.activation(out=FQ.ap, in_=dv.ap, func=AF.Exp, scale=-1.0 / 16,
                                                      bias=k.LNQ.ap), reads=[dv, k.LNQ.full()], writes=[FQ.full()])
            b = inproj_fm(k, WA, tb * 512, 512)
            qv = QT.v(QT.ap[:, tb * 512:(tb + 1) * 512], tb * 512, 512)
            P.op("DVE", lambda e, qv=qv, b=b: e.tensor_tensor(out=qv.ap, in0=k.psum[b][:, :], in1=FQ.ap, op=ALU.mult),
                 reads=[PSV(k, b), FQ.full()], writes=[qv])
        P.op("DVE", lambda e: e.tensor_tensor(
            out=QT2.ap.rearrange("p (n c) -> p n c", c=128), in0=QT.ap.rearrange("p (n c) -> p n c", c=128),
            in1=DCUM.unsqueeze(2).to_broadcast([128, NT, 128]), op=ALU.mult),
            reads=[QT.full(), smf], writes=[QT2.full()])
        load_w(k, WD, gg[h, :, :])
        for n in range(NT):
            b = inproj_tm(k, WD, n, 256)
            gv2 = Gt.v(Gt.ap[:, n, :], n * 256, 256)
            P.op("ACT", lambda e, gv2=gv2, b=b: e.activation(out=gv2.ap, in_=k.psum[b][:, 0:256], func=AF.Silu),
                 reads=[PSV(k, b)], writes=[gv2])
        for c in range(2):
            wv = WOH.v(WOH.ap[:, c, :], c * D, D)
            P.dma("POOL", wv.ap, wo[h * 2 + c, :, :], writes=[wv], max_dma_last_dim=4096)
        exchange_finish(k, xctx, SB32)
        P.op("ACT", lambda e: e.activation(out=SBb.ap, in_=SB32.ap, func=AF.Copy), reads=[SB32.full()], writes=[SBb.full()])
        NB3 = 4
        sc_bank, o_bank = {}, {}

        def g_A(n):
            b = ps_bank(k)
            kn = KNT.v(KNT.ap[:, n * 128:(n + 1) * 128], n * 128, 128)
            qv = QT.v(QT.ap[:, n * 128:(n + 1) * 128], n * 128, 128)
            st = STs[n % NB3]
            P.op("PE", lambda e: e.matmul(k.psum[b][:, 0:128], lhsT=kn.ap, rhs=qv.ap, start=True, stop=True),
                 reads=[kn, qv], writes=[PSV(k, b)])
            P.op("DVE", lambda e: e.tensor_tensor(out=st.ap, in0=k.psum[b][:, 0:128], in1=k.MASKU.ap, op=ALU.mult),
                 reads=[PSV(k, b), k.MASKU.full()], writes=[st.full()])

        def g_B(n):
            b2 = ps_bank(k)
            o_bank[n] = b2
            st = STs[n % NB3]
            qv = QT.v(QT.ap[:, n * 128:(n + 1) * 128], n * 128, 128)
            q2 = QT2.v(QT2.ap[:, n * 128:(n + 1) * 128], n * 128, 128)
            vv = Vt.v(Vt.ap[:, n, :], n * 256, 256)
            sl = SLOC.v(SLOC.ap[:, n, :], n * 256, 256)
            P.op("PE", lambda e: e.matmul(k.psum[b2][:, 0:256], lhsT=st.ap, rhs=vv.ap, start=True, stop=False),
                 reads=[st.full(), vv], writes=[PSV(k, b2)])
            P.op("PE", lambda e: e.matmul(k.psum[b2][:, 0:256], lhsT=qv.ap, rhs=sl.ap, start=False, stop=False),
                 reads=[qv, sl], writes=[PSV(k, b2)])
            P.op("PE", lambda e: e.matmul(k.psum[b2][:, 0:256], lhsT=q2.ap, rhs=SBb.ap, start=False, stop=True),
                 reads=[q2, SBb.full()], writes=[PSV(k, b2)])

        def g_C(n):
            b2 = o_bank[n]
            tt, sr = TTs[n % NB3], SRs[n % NB3]
            srf = sr.full()
            P.op("ACT", lambda e: e.activation(out=tt.ap, in_=k.psum[b2][:, 0:256], func=AF.Square, accum_out=sr.ap[:, 0:1]),
                 reads=[PSV(k, b2)], writes=[tt.full(), srf])
            P.op("ACT", lambda e: e.activation(out=sr.ap[:, 1:2], in_=sr.ap[:, 0:1], func=AF.Ln, scale=1.0 / 256,
                                               bias=k.EPSC.ap), reads=[srf, k.EPSC.full()], writes=[srf])
            P.op("ACT", lambda e: e.activation(out=sr.ap[:, 1:2], in_=sr.ap[:, 1:2], func=AF.Exp, scale=-0.5),
                 reads=[srf], writes=[srf])

        def g_D(n):
            b2 = o_bank[n]
            tt, mtok, sr = TTs[n % NB3], MTOKs[n % NB3], SRs[n % NB3]
            P.op("DVE", lambda e: e.scalar_tensor_tensor(
                out=tt.ap, in0=k.psum[b2][:, 0:256], scalar=sr.ap[:, 1:2], in1=HNB.ap, op0=ALU.mult, op1=ALU.mult),
                reads=[PSV(k, b2), sr.full(), HNB.full()], writes=[tt.full()])
            gv2 = Gt.v(Gt.ap[:, n, :], n * 256, 256)
            P.op("DVE", lambda e: e.tensor_tensor(out=mtok.ap, in0=tt.ap, in1=gv2.ap, op=ALU.mult),
                 reads=[tt.full(), gv2], writes=[mtok.full()])

        def g_E(n):
            mtok = MTOKs[n % NB3]
            b3 = ps_bank(k)
            pb = k.psum[b3][:, :].bitcast(BF16)
            for c in range(2):
                P.op("PE", lambda e, c=c: e.transpose(
                    out=pb[:, c * 128:(c + 1) * 128], in_=mtok.ap[:, c * 128:(c + 1) * 128], identity=k.IDB.ap),
                    reads=[mtok.full(), k.IDB.full()], writes=[PSV(k, b3)])
            mv = MIXT.vs(MIXT.ap[:, :, n * 128:(n + 1) * 128], [(c * T + n * 128, 128) for c in range(2)])
            P.op("ACT", lambda e: e.activation(
                out=mv.ap, in_=pb[:, 0:256].rearrange("p (a b) -> p a b", a=2), func=AF.Copy),
                reads=[PSV(k, b3)], writes=[mv])

        for s_ in range(NT + 4):
            if s_ < NT:
                g_A(s_)
            if 0 <= s_ - 1 < NT:
                g_B(s_ - 1)
            if 0 <= s_ - 2 < NT:
                g_C(s_ - 2)
            if 0 <= s_ - 3 < NT:
                g_D(s_ - 3)
            if 0 <= s_ - 4 < NT:
                g_E(s_ - 4)
        emit_outproj(k, MIXT, WOH, 2)


def emit_ssd(k, i):
    P, A, nc = k.P, k.A, k.nc
    j = i // 2
    sz, sxbc = k.w(f"sz{j}"), k.w(f"sxbc{j}")
    wo = k.w(f"wo{i}")
    sdk = k.w(f"sdk{j}")
    SCV = Tile(A, [164], F32)
    QTM = Tile(A, [NT, 128], F32)
    DB = Tile(A, [2, 32, 16], F32)
    SSQ = Tile(A, [NT, 8], F32)
    SM = Tile(A, [64], F32)
    CST, CEN, DEC_, DCU_ = SM.ap[:, 0:16], SM.ap[:, 16:32], SM.ap[:, 32:48], SM.ap[:, 48:64]
    AV = Tile(A, [1], F32)
    smf = SM.full()
    P.dma("SP", SCV.ap, k.w(f"scv{j}")[:, :], writes=[SCV.full()])
    scvf = SCV.full()
    cslD = nc.dram_tensor(f"csl{i}", [32, T], F32)
    dcD = nc.dram_tensor(f"dcd{i}", [2, 32, 16], F32)
    ytD = nc.dram_tensor(f"ytd{i}", [NT, 128, 2048], BF16)
    csl_v = V(None, [("D", f"csl{i}")])
    dcd_v = V(None, [("D", f"dcd{i}")])
    m_layer = A.mark()
    WDT = Tile(A, [KC, 128], BF16)
    DTt = Tile(A, [T], F32)
    CSL = Tile(A, [T], F32)
    Q4 = Tile(A, [T], F32)
    load_w(k, WDT, k.w(f"sdt{j}")[:, :])
    for tb in range(4):
        b = inproj_fm(k, WDT, tb * 512, 512)
        dv = DTt.v(DTt.ap[:, tb * 512:(tb + 1) * 512], tb * 512, 512)
        P.op("ACT", lambda e, dv=dv, b=b: e.activation(out=dv.ap, in_=k.psum[b][:, :], func=AF.Exp,
                                                         bias=SCV.ap[:, 160:161]), reads=[PSV(k, b), scvf], writes=[dv])
    P.op("ACT", lambda e: e.activation(out=DTt.ap, in_=DTt.ap, func=AF.Ln, bias=1.0), reads=[DTt.full()], writes=[DTt.full()])
    P.op("ACT", lambda e: e.activation(out=AV.ap, in_=SCV.ap[:, 161:162], func=AF.Exp), reads=[scvf], writes=[AV.full()])
    P.op("DVE", lambda e: e.tensor_scalar(out=AV.ap, in0=AV.ap, scalar1=-1.0, scalar2=None, op0=ALU.mult),
         reads=[AV.full()], writes=[AV.full()])
    P.op("DVE", lambda e: e.tensor_scalar(out=Q4.ap, in0=DTt.ap, scalar1=AV.ap, scalar2=None, op0=ALU.mult),
         reads=[DTt.full(), AV.full()], writes=[Q4.full()])
    P.op("DVE", lambda e: e.tensor_tensor_scan(
        out=CSL.ap, data0=k.ONE1.ap.to_broadcast([128, T]), data1=Q4.ap, initial=0.0,
        op0=ALU.mult, op1=ALU.add), reads=[Q4.full(), k.ONE1.full()], writes=[CSL.full()])
    cs3 = CSL.ap.rearrange("p (n c) -> p n c", c=128)
    P.op("DVE", lambda e: e.memset(SM.ap[:, 0:1], 0.0), writes=[smf])
    P.op("DVE", lambda e: e.tensor_copy(out=SM.ap[:, 1:16], in_=cs3[:, 0:15, 127]), reads=[CSL.full()], writes=[smf])
    P.op("DVE", lambda e: e.tensor_copy(out=CEN, in_=cs3[:, :, 127]), reads=[CSL.full()], writes=[smf])
    P.op("DVE", lambda e: e.tensor_tensor(out=DEC_, in0=CEN, in1=CST, op=ALU.subtract), reads=[smf], writes=[smf])
    P.op("ACT", lambda e: e.activation(out=DEC_, in_=DEC_, func=AF.Exp), reads=[smf], writes=[smf])
    P.op("ACT", lambda e: e.activation(out=DCU_, in_=CST, func=AF.Exp), reads=[smf], writes=[smf])
    smv = V(SM.ap[0:32, 32:64].rearrange("p (a n) -> p a n", a=2), SM.full().cells)
    P.dma("SP", dcD.ap().rearrange("a h n -> h a n"), smv.ap, reads=[smv], writes=[dcd_v])
    P.dma("SP", DB.ap, dcD.ap().partition_broadcast(128), reads=[dcd_v], writes=[DB.full()])
    P.op("DVE", lambda e: e.tensor_tensor(out=CEN, in0=CEN, in1=CST, op=ALU.subtract), reads=[smf], writes=[smf])
    P.op("DVE", lambda e: e.tensor_tensor(
        out=cs3, in0=cs3, in1=CST.unsqueeze(2).to_broadcast([128, NT, 128]), op=ALU.subtract),
        reads=[CSL.full(), smf], writes=[CSL.full()])
    cslsb = V(CSL.ap[0:32, :], CSL.full().cells)
    P.dma("SP", cslD[:, :], cslsb.ap, reads=[cslsb], writes=[csl_v])
    P.op("ACT", lambda e: e.activation(out=Q4.ap, in_=DTt.ap, func=AF.Ln), reads=[DTt.full()], writes=[Q4.full()])
    P.op("DVE", lambda e: e.tensor_tensor(out=Q4.ap, in0=Q4.ap, in1=CSL.ap, op=ALU.subtract),
         reads=[Q4.full(), CSL.full()], writes=[Q4.full()])
    P.op("ACT", lambda e: e.activation(out=Q4.ap[0:32, :], in_=CSL.ap[0:32, :], func=AF.Exp),
         reads=[CSL.full()], writes=[Q4.full()])
    q3 = Q4.ap.rearrange("p (n c) -> p n c", c=128)
    P.op("DVE", lambda e: e.tensor_tensor(
        out=q3[32:64], in0=CEN[32:64].unsqueeze(2).to_broadcast([32, NT, 128]), in1=cs3[32:64], op=ALU.subtract),
        reads=[CSL.full(), smf], writes=[Q4.full()])
    P.op("ACT", lambda e: e.activation(out=Q4.ap[32:64, :], in_=Q4.ap[32:64, :], func=AF.Exp),
         reads=[Q4.full()], writes=[Q4.full()])
    P.op("DVE", lambda e: e.tensor_tensor(out=Q4.ap[32:64, :], in0=Q4.ap[32:64, :], in1=DTt.ap[32:64, :], op=ALU.mult),
         reads=[Q4.full(), DTt.full()], writes=[Q4.full()])
    P.op("ACT", lambda e: e.activation(out=Q4.ap[64:96, :], in_=CSL.ap[64:96, :], func=AF.Copy),
         reads=[CSL.full()], writes=[Q4.full()])
    for n4 in range(4):
        b = ps_bank(k)
        for t in range(4):
            n = n4 * 4 + t
            qv = Q4.v(Q4.ap[:, n * 128:(n + 1) * 128], n * 128, 128)
            P.op("PE", lambda e, qv=qv, b=b, t=t: e.transpose(
                out=k.psum[b][:, t * 128:(t + 1) * 128], in_=qv.ap, identity=k.IDF.ap),
                reads=[qv, k.IDF.full()], writes=[PSV(k, b)])
        qm = QTM.v(QTM.ap[:, n4 * 4:(n4 + 1) * 4, :], n4 * 4 * 128, 4 * 128)
        P.op("ACT", lambda e, qm=qm, b=b: e.activation(
            out=qm.ap, in_=k.psum[b][:, :].rearrange("p (a b) -> p a b", a=4), func=AF.Copy),
            reads=[PSV(k, b)], writes=[qm])
    A.release(m_layer)
    wi = 0
    WR = [Tile(A, [KC, 128], BF16) for _ in range(2)]
    WZ = Tile(A, [KC, 256], BF16)
    PRE = Tile(A, [T + 4], BF16)
    DG = [Tile(A, [4, 128], BF16) for _ in range(2)]
    DGD = Tile(A, [4, 128], BF16)
    XsT = Tile(A, [2, T], BF16)
    BT = Tile(A, [T], BF16)
    CT = Tile(A, [T], BF16)
    XS = Tile(A, [NT, 256], BF16)
    XW = Tile(A, [NT, 256], BF16)
    Btok = Tile(A, [NT, 128], BF16)
    SLOC = Tile(A, [NT, 256], BF16, off=XsT.off)
    YT = Tile(A, [2, T], BF16)
    S = Tile(A, [256], F32)
    SB32 = Tile(A, [256], F32)
    NB3 = 4
    SBns = [Tile(A, [256], BF16) for _ in range(NB3)]
    BCts = [Tile(A, [4, 128], F32) for _ in range(3)]
    CBMs = [Tile(A, [128], F32) for _ in range(NB3)]
    SEG = [Tile(A, [128], F32) for _ in range(8)]
    MTs = [[Tile(A, [128], BF16) for _ in range(4)] for _ in range(NB3)]
    T1s = [Tile(A, [256], F32, off=XW.off + r_ * 1024) for r_ in range(NB3)]
    THs = [Tile(A, [256], F32, off=XW.off + 4096 + r_ * 1024) for r_ in range(NB3)]
    YFs = [Tile(A, [256], F32, off=PRE.off + r_ * 1024) for r_ in range(NB3)]
    G2s = [Tile(A, [256], BF16) for _ in range(NB3)]
    ZCs = [Tile(A, [256], BF16) for _ in range(NB3)]
    YGBs = [Tile(A, [256], BF16) for _ in range(NB3)]
    DSK = Tile(A, [256], F32)
    dgi = 0
    pre_setup = []
    for g in range(8):
        P.dma("SP", DSK.ap, sdk[:, g * 256:(g + 1) * 256], writes=[DSK.full()])
        load_w(k, WZ, sz[g, :, :])
        for jh in range(4):
            dv_ = DGD.v(DGD.ap[:, jh, :], jh * 128, 128)
            P.op("POOL", lambda e, dv_=dv_, jh=jh: e.tensor_scalar(
                out=dv_.ap, in0=k.IDB.ap, scalar1=DSK.ap[:, jh * 64:jh * 64 + 1], scalar2=None, op0=ALU.mult),
                reads=[k.IDB.full(), DSK.full()], writes=[dv_])

        def conv_setup(cb):
            nonlocal wi, dgi
            w = WR[wi % 2]
            wi += 1
            load_w(k, w, sxbc[cb, :, :])
            dg = DG[dgi % 2]
            dgi += 1
            for tap in range(4):
                dgv = dg.v(dg.ap[:, tap, :], tap * 128, 128)
                P.op("POOL", lambda e, dgv=dgv, tap=tap: e.tensor_scalar(
                    out=dgv.ap, in0=k.IDB.ap, scalar1=SCV.ap[:, cb * 4 + tap:cb * 4 + tap + 1], scalar2=None,
                    op0=ALU.mult), reads=[k.IDB.full(), scvf], writes=[dgv])
            return w, dg

        def conv_proj(w, tb, cb=None):
            if tb == 0:
                hp = PRE.v(PRE.ap[:, 0:3], 0, 3)
                bh = ps_bank(k)
                for kc in range(KC):
                    P.op("PE", lambda e, kc=kc: e.matmul(
                        k.psum[bh][:, 0:3], lhsT=w.ap[:, kc, :], rhs=k.XHB.ap[:, kc, :],
                        start=(kc == 0), stop=(kc == KC - 1)), reads=[w.full(), k.XHB.full()], writes=[PSV(k, bh)])
                P.op("ACT", lambda e: e.activation(out=hp.ap, in_=k.psum[bh][:, 0:3], func=AF.Copy),
                     reads=[PSV(k, bh)], writes=[hp])
            b = inproj_fm(k, w, tb * 512, 512)
            pv = PRE.v(PRE.ap[:, 3 + tb * 512:3 + (tb + 1) * 512], 3 + tb * 512, 512)
            P.op("ACT", lambda e: e.activation(out=pv.ap, in_=k.psum[b][:, :], func=AF.Copy),
                 reads=[PSV(k, b)], writes=[pv])

        def conv_out(cb, dg, dst, tb):
            b = ps_bank(k)
            for tap in range(4):
                pv = PRE.v(PRE.ap[:, tap + tb * 512:tap + (tb + 1) * 512], tap + tb * 512, 512)
                P.op("PE", lambda e, pv=pv, tap=tap: e.matmul(
                    k.psum[b][:, :], lhsT=dg.ap[:, tap, :], rhs=pv.ap, start=(tap == 0), stop=(tap == 3)),
                    reads=[dg.full(), pv], writes=[PSV(k, b)])
            dv = V(dst.ap[:, tb * 512:(tb + 1) * 512], dst.cells)
            P.op("ACT", lambda e: e.activation(
                out=dv.ap, in_=k.psum[b][:, :], func=AF.Silu, bias=SCV.ap[:, 128 + cb:129 + cb]),
                reads=[PSV(k, b), scvf], writes=[dv])

        dsts = [(2 * g, XsT.v(XsT.ap[:, 0, :], 0, T)), (2 * g + 1, XsT.v(XsT.ap[:, 1, :], T, T)),
                (16 + g, BT.full())]
        for bi, (cb, dst) in enumerate(dsts):
            if bi < 2 and pre_setup:
                w, dg = pre_setup.pop(0)
            else:
                w, dg = conv_setup(cb)
            for tb in range(4):
                conv_proj(w, tb, cb)
            for tb in range(4):
                conv_out(cb, dg, dst, tb)
        cbC = 24 + g
        wC, dgC = None, None

        def tr_batch(q, g=g):
            b = ps_bank(k)
            pb = k.psum[b][:, :].bitcast(BF16)
            for t in range(4):
                n = q * 4 + t
                for c in range(2):
                    xv = XsT.v(XsT.ap[:, c, n * 128:(n + 1) * 128], c * T + n * 128, 128)
                    P.op("PE", lambda e, xv=xv, t=t, c=c: e.transpose(
                        out=pb[:, (t * 2 + c) * 128:(t * 2 + c + 1) * 128], in_=xv.ap, identity=k.IDB.ap),
                        reads=[xv, k.IDB.full()], writes=[PSV(k, b)])
            xs4 = XS.v(XS.ap[:, q * 4:(q + 1) * 4, :], q * 4 * 256, 4 * 256)
            P.op("ACT", lambda e: e.activation(
                out=xs4.ap, in_=pb.rearrange("p (a b) -> p a b", a=4), func=AF.Copy),
                reads=[PSV(k, b)], writes=[xs4])
            b2 = ps_bank(k)
            pb2 = k.psum[b2][:, :].bitcast(BF16)
            for t in range(4):
                n = q * 4 + t
                bv = BT.v(BT.ap[:, n * 128:(n + 1) * 128], n * 128, 128)
                P.op("PE", lambda e, bv=bv, t=t: e.transpose(
                    out=pb2[:, t * 128:(t + 1) * 128], in_=bv.ap, identity=k.IDB.ap),
                    reads=[bv, k.IDB.full()], writes=[PSV(k, b2)])
            b4 = Btok.v(Btok.ap[:, q * 4:(q + 1) * 4, :], q * 4 * 128, 4 * 128)
            P.op("ACT", lambda e: e.activation(
                out=b4.ap, in_=pb2[:, 0:512].rearrange("p (a b) -> p a b", a=4), func=AF.Copy),
                reads=[PSV(k, b2)], writes=[b4])
            xw4 = XW.v(XW.ap[:, q * 4:(q + 1) * 4, :], q * 4 * 256, 4 * 256)
            P.op("DVE", lambda e: e.tensor_tensor(
                out=xw4.ap.rearrange("p n (h q) -> p n h q", h=4), in0=xs4.ap.rearrange("p n (h q) -> p n h q", h=4),
                in1=QTM.ap[:, q * 4:(q + 1) * 4, 32 + 4 * g:36 + 4 * g].unsqueeze(3).to_broadcast([128, 4, 4, 64]),
                op=ALU.mult), reads=[xs4, QTM.full()], writes=[xw4])

        s3 = S.ap.rearrange("p (h q) -> p h q", h=4)

        def rec_step(n, g=g):
            b = ps_bank(k)
            bv = Btok.v(Btok.ap[:, n, :], n * 128, 128)
            xw = XW.v(XW.ap[:, n, :], n * 256, 256)
            P.op("PE", lambda e: e.matmul(k.psum[b][:, 0:256], lhsT=bv.ap, rhs=xw.ap, start=True, stop=True),
                 reads=[bv, xw], writes=[PSV(k, b)])
            P.op("DVE", lambda e: e.tensor_tensor(
                out=s3, in0=s3, in1=DB.ap[:, 0, 4 * g:4 * g + 4, n].unsqueeze(2).to_broadcast([128, 4, 64]),
                op=ALU.mult), reads=[S.full(), DB.full()], writes=[S.full()])
            P.op("DVE", lambda e: e.tensor_tensor(out=S.ap, in0=k.psum[b][:, 0:256], in1=S.ap, op=ALU.add),
                 reads=[S.full(), PSV(k, b)], writes=[S.full()])
            if n < NT - 1:
                sl = SLOC.v(SLOC.ap[:, n + 1, :], (n + 1) * 256, 256)
                P.op("ACT", lambda e: e.activation(out=sl.ap, in_=S.ap, func=AF.Copy),
                     reads=[S.full()], writes=[sl])

        P.op("DVE", lambda e: e.memset(S.ap, 0.0), writes=[S.full()])
        for q in range(4):
            tr_batch(q)
        P.op("POOL", lambda e: e.memset(SLOC.ap[:, 0, :], 0.0), writes=[SLOC.v(SLOC.ap[:, 0, :], 0, 256)])
        wC, dgC = conv_setup(cbC)
        for q in range(4):
            conv_proj(wC, q, cbC)
            for n in range(q * 4, q * 4 + 4):
                rec_step(n)
        xctx = exchange_start(k, S.full(), 128, 256, f"sx{i}_{g}")
        for tb in range(4):
            conv_out(cbC, dgC, CT.full(), tb)
        if g < 7:
            pre_setup.extend([conv_setup(2 * (g + 1)), conv_setup(2 * (g + 1) + 1)])
        exchange_finish(k, xctx, SB32)
        sb3 = SB32.ap.rearrange("p (h q) -> p h q", h=4)
        def s_dma(n, g=g):
            bct = BCts[n % 3]
            P.dma("SP", bct.ap, cslD[4 * g:4 * g + 4, n * 128:(n + 1) * 128].partition_broadcast(128),
                  reads=[csl_v], writes=[bct.full()])

        def s_A(n, g=g):
            r = n % NB3
            bct, cbm, th, zc = BCts[n % 3], CBMs[r], THs[r], ZCs[r]
            bv = BT.v(BT.ap[:, n * 128:(n + 1) * 128], n * 128, 128)
            cv = CT.v(CT.ap[:, n * 128:(n + 1) * 128], n * 128, 128)
            b = ps_bank(k)
            P.op("PE", lambda e: e.matmul(k.psum[b][:, 0:128], lhsT=bv.ap, rhs=cv.ap, start=True, stop=True),
                 reads=[bv, cv], writes=[PSV(k, b)])
            bz = inproj_tm(k, WZ, n, 256)
            P.op("DVE", lambda e: e.tensor_tensor(out=cbm.ap, in0=k.psum[b][:, 0:128], in1=k.MASKU.ap, op=ALU.mult),
                 reads=[PSV(k, b), k.MASKU.full()], writes=[cbm.full()])
            for jh in range(4):
                h = 4 * g + jh
                sg = SEG[(n % 2) * 4 + jh]
                P.op("DVE", lambda e, sg=sg, jh=jh, h=h: e.tensor_scalar(
                    out=sg.ap, in0=bct.ap[:, jh, :], scalar1=QTM.ap[:, n, 64 + h:65 + h], scalar2=None, op0=ALU.min),
                    reads=[bct.full(), QTM.full()], writes=[sg.full()])
            for jh in range(4):
                h = 4 * g + jh
                sg = SEG[(n % 2) * 4 + jh]
                P.op("ACT", lambda e, sg=sg, h=h: e.activation(
                    out=sg.ap, in_=sg.ap, func=AF.Exp, bias=QTM.ap[:, n, 96 + h:97 + h]),
                    reads=[sg.full(), QTM.full()], writes=[sg.full()])
            P.op("ACT", lambda e: e.activation(out=th.ap, in_=k.psum[bz][:, 0:256], func=AF.Tanh, scale=0.5),
                 reads=[PSV(k, bz)], writes=[th.full()])
            P.op("ACT", lambda e: e.activation(out=zc.ap, in_=k.psum[bz][:, 0:256], func=AF.Copy),
                 reads=[PSV(k, bz)], writes=[zc.full()])

        def s_B(n, g=g):
            r = n % NB3
            cbm, th, zc, g2 = CBMs[r], THs[r], ZCs[r], G2s[r]
            for jh in range(4):
                sg = SEG[(n % 2) * 4 + jh]
                mt = MTs[r][jh]
                P.op("DVE", lambda e, sg=sg, mt=mt: e.tensor_tensor(out=mt.ap, in0=sg.ap, in1=cbm.ap, op=ALU.mult),
                     reads=[sg.full(), cbm.full()], writes=[mt.full()])
            P.op("DVE", lambda e: e.scalar_tensor_tensor(
                out=g2.ap, in0=th.ap, scalar=1.0, in1=zc.ap, op0=ALU.add, op1=ALU.mult),
                reads=[th.full(), zc.full()], writes=[g2.full()])

        cbank = {}

        def s_C1(n, g=g):
            r = n % NB3
            sbn = SBns[r]
            sl = SLOC.v(SLOC.ap[:, n, :], n * 256, 256)
            cv = CT.v(CT.ap[:, n * 128:(n + 1) * 128], n * 128, 128)
            P.op("DVE", lambda e: e.tensor_tensor(
                out=sbn.ap.rearrange("p (h q) -> p h q", h=4), in0=sb3,
                in1=DB.ap[:, 1, 4 * g:4 * g + 4, n].unsqueeze(2).to_broadcast([128, 4, 64]), op=ALU.mult),
                reads=[SB32.full(), DB.full()], writes=[sbn.full()])
            by = ps_bank(k)
            cbank[n] = by
            for jh in range(4):
                mt = MTs[r][jh]
                xs = XS.v(XS.ap[:, n, jh * 64:(jh + 1) * 64], n * 256 + jh * 64, 64)
                P.op("PE", lambda e, mt=mt, xs=xs, jh=jh: e.matmul(
                    k.psum[by][:, jh * 64:(jh + 1) * 64], lhsT=mt.ap, rhs=xs.ap, start=True, stop=False),
                    reads=[mt.full(), xs], writes=[PSV(k, by)])
                P.op("PE", lambda e, xs=xs, jh=jh: e.matmul(
                    k.psum[by][:, jh * 64:(jh + 1) * 64], lhsT=DGD.ap[:, jh, :], rhs=xs.ap, start=False, stop=True),
                    reads=[DGD.full(), xs], writes=[PSV(k, by)])
            P.op("PE", lambda e: e.matmul(k.psum[by][:, 256:512], lhsT=cv.ap, rhs=sl.ap, start=True, stop=False),
                 reads=[cv, sl], writes=[PSV(k, by)])
            P.op("PE", lambda e: e.matmul(k.psum[by][:, 256:512], lhsT=cv.ap, rhs=sbn.ap, start=False, stop=True),
                 reads=[cv, sbn.full()], writes=[PSV(k, by)])

        def s_C2(n, g=g):
            r = n % NB3
            t1, yf, g2, ygb = T1s[r], YFs[r], G2s[r], YGBs[r]
            by = cbank[n]
            P.op("DVE", lambda e: e.tensor_tensor(
                out=t1.ap.rearrange("p (h q) -> p h q", h=4),
                in0=k.psum[by][:, 256:512].rearrange("p (h q) -> p h q", h=4),
                in1=QTM.ap[:, n, 4 * g:4 * g + 4].unsqueeze(2).to_broadcast([128, 4, 64]), op=ALU.mult),
                reads=[PSV(k, by), QTM.full()], writes=[t1.full()])
            P.op("DVE", lambda e: e.tensor_tensor(out=yf.ap, in0=k.psum[by][:, 0:256], in1=t1.ap, op=ALU.add),
                 reads=[PSV(k, by), t1.full()], writes=[yf.full()])
            P.op("DVE", lambda e: e.tensor_tensor(out=ygb.ap, in0=yf.ap, in1=g2.ap, op=ALU.mult),
                 reads=[yf.full(), g2.full()], writes=[ygb.full()])
            sq = SSQ.v(SSQ.ap[:, n, g:g + 1], n * 8 + g, 1)
            P.op("ACT", lambda e: e.activation(out=t1.ap, in_=ygb.ap, func=AF.Square, accum_out=sq.ap),
                 reads=[ygb.full()], writes=[t1.full(), sq])

        def s_D(n, g=g):
            ygb = YGBs[n % NB3]
            b3 = ps_bank(k)
            pb = k.psum[b3][:, :].bitcast(BF16)
            for c in range(2):
                P.op("PE", lambda e, c=c: e.transpose(
                    out=pb[:, c * 128:(c + 1) * 128], in_=ygb.ap[:, c * 128:(c + 1) * 128], identity=k.IDB.ap),
                    reads=[ygb.full(), k.IDB.full()], writes=[PSV(k, b3)])
            yv = YT.vs(YT.ap[:, :, n * 128:(n + 1) * 128], [(c * T + n * 128, 128) for c in range(2)])
            P.op("ACT", lambda e: e.activation(
                out=yv.ap, in_=pb[:, 0:256].rearrange("p (a b) -> p a b", a=2), func=AF.Copy),
                reads=[PSV(k, b3)], writes=[yv])

        s_dma(0)
        for s_ in range(NT + 4):
            if s_ + 1 < NT:
                s_dma(s_ + 1)
            if s_ < NT:
                s_A(s_)
            if 0 <= s_ - 1 < NT:
                s_B(s_ - 1)
            if 0 <= s_ - 2 < NT:
                s_C1(s_ - 2)
            if 0 <= s_ - 3 < NT:
                s_C2(s_ - 3)
            if 0 <= s_ - 4 < NT:
                s_D(s_ - 4)
        ytv = V(None, [("D", f"ytd{i}", g)])
        for c in range(2):
            P.dma("SP", ytD.ap()[:, :, (2 * g + c) * 128:(2 * g + c + 1) * 128].rearrange("n p t -> p n t"),
                  YT.ap[:, c, :].rearrange("p (n t) -> p n t", n=NT), reads=[YT.full()], writes=[ytv])
    A.release(m_layer)
    WOM = Tile(A, [16, D], BF16)
    YTt = [Tile(A, [16, 128], BF16) for _ in range(2)]
    WST = [Tile(A, [D], F32) for _ in range(2)]
    SNF = Tile(A, [16], F32)
    P.dma("SP", SNF.ap, k.w(f"snf{j}")[:, :], writes=[SNF.full()])
    for c in range(16):
        wv = WOM.v(WOM.ap[:, c, :], c * D, D)
        ws = WST[c % 2]
        P.dma("SP", ws.ap, wo[c, :, :], writes=[ws.full()])
        P.op("DVE", lambda e, wv=wv, ws=ws, c=c: e.tensor_scalar(
            out=wv.ap, in0=ws.ap, scalar1=SNF.ap[:, c:c + 1], scalar2=0.5, op0=ALU.mult, op1=ALU.mult),
            reads=[ws.full(), SNF.full()], writes=[wv])
    rsf = k.RS.full()
    P.op("DVE", lambda e: e.tensor_reduce(out=k.RS.ap, in_=SSQ.ap, axis=mybir.AxisListType.X, op=ALU.add),
         reads=[SSQ.full()], writes=[rsf])
    P.op("DVE", lambda e: e.tensor_scalar(out=k.RS.ap, in0=k.RS.ap, scalar1=0.25 / 2048, scalar2=EPS,
                                           op0=ALU.mult, op1=ALU.add), reads=[rsf], writes=[rsf])
    P.op("DVE", lambda e: e.reciprocal(out=k.RS.ap, in_=k.RS.ap), reads=[rsf], writes=[rsf])
    P.op("ACT", lambda e: e.activation(out=k.RS.ap, in_=k.RS.ap, func=AF.Sqrt), reads=[rsf], writes=[rsf])
    ytall = [V(None, [("D", f"ytd{i}", g)]) for g in range(8)]
    for n in range(NT):
        yt = YTt[n % 2]
        P.dma("SP", yt.ap.rearrange("p a b -> p (a b)"), ytD[n, :, :], reads=ytall, writes=[yt.full()])
        for nb in range(2):
            b = ps_bank(k)
            for c in range(16):
                wv = WOM.v(WOM.ap[:, c, nb * 512:(nb + 1) * 512], c * D + nb * 512, 512)
                P.op("PE", lambda e, yt=yt, wv=wv, b=b, c=c: e.matmul(
                    k.psum[b][:, :], lhsT=yt.ap[:, c, :], rhs=wv.ap, start=(c == 0), stop=(c == 15)),
                    reads=[yt.full(), wv], writes=[PSV(k, b)])
            hv = hview(k, n, nb * 512, 512)
            P.op("DVE", lambda e, hv=hv, b=b, n=n: e.scalar_tensor_tensor(
                out=hv.ap, in0=k.psum[b][:, :], scalar=k.RS.ap[:, n:n + 1], in1=hv.ap, op0=ALU.mult, op1=ALU.add),
                reads=[PSV(k, b), hv, rsf], writes=[hv])


def emit_output(k, cfg, bcp_d, out_d):
    P, A = k.P, k.A
    outs = []
    if not cfg.get("final_norm", True):
        for j in range(NT):
            hv = hview(k, j)
            outs.append(P.dma("SP", out_d[j * 128:(j + 1) * 128, :], hv.ap, reads=[hv],
                              writes=[V(None, [("D", "out", j)])]))
    else:
        FN = Tile(A, [D], F32)
        OB = [Tile(A, [D], F32) for _ in range(2)]
        P.dma("SP", FN.ap, bcp_d[:, 0:1024], writes=[FN.full()])
        SS, RS = k.SS, k.RS
        ssf, rsf = SS.full(), RS.full()
        junk = k.JUNK.full()
        for j in range(NT):
            hv = hview(k, j)
            P.op("ACT", lambda e, hv=hv, j=j: e.activation(
                out=k.JUNK.ap, in_=hv.ap, func=AF.Square, accum_out=SS.ap[:, j:j + 1]),
                reads=[hv], writes=[junk, ssf])
        P.op("DVE", lambda e: e.tensor_scalar(out=RS.ap, in0=SS.ap, scalar1=1.0 / D, scalar2=EPS,
                                               op0=ALU.mult, op1=ALU.add), reads=[ssf], writes=[rsf])
        P.op("DVE", lambda e: e.reciprocal(out=RS.ap, in_=RS.ap), reads=[rsf], writes=[rsf])
        P.op("ACT", lambda e: e.activation(out=RS.ap, in_=RS.ap, func=AF.Sqrt), reads=[rsf], writes=[rsf])
        for j in range(NT):
            hv = hview(k, j)
            ob = OB[j % 2]
            P.op("DVE", lambda e, hv=hv, ob=ob, j=j: e.scalar_tensor_tensor(
                out=ob.ap, in0=hv.ap, scalar=RS.ap[:, j:j + 1], in1=FN.ap, op0=ALU.mult, op1=ALU.mult),
                reads=[hv, rsf, FN.full()], writes=[ob.full()])
            outs.append(P.dma("SP", out_d[j * 128:(j + 1) * 128, :], ob.ap, reads=[ob.full()],
                              writes=[V(None, [("D", "out", j)])]))
    fin = Op("SP", lambda e: e.nop())
    fin.seq = len(P.ops["SP"])
    for o in outs:
        P._need(fin, o)
    P.ops["SP"].append(fin)


FULL_CFG = dict(sublayers=[("mix", 0), ("ffn", 0), ("mix", 1), ("ffn", 1),
                           ("mix", 2), ("ffn", 2), ("mix", 3), ("ffn", 3)], final_norm=True)


def run(inputs, cfg):
    nc = build(cfg)
    maps = prep_inputs(inputs, set(nc.used_inputs))
    maps = [{kk: m[kk] for kk in nc.used_inputs} for m in maps]
    res = run_bass_kernel_spmd(nc, maps, core_ids=list(range(8)))
    out = np.zeros((4, 4096, D), np.float32)
    for c in range(8):
        b, half = c // 2, c % 2
        out[b, half * T:(half + 1) * T] = res.results[c]["out"]
    return out


def kernel(**inputs):
    return run(inputs, FULL_CFG)
```
